# Optimizing a Trainium2 kernel written in Bass

```python
import math
import jax, jax.numpy as jnp
from jax import lax
import numpy as np

D_MODEL = 1024
BATCH = 16
SEQ = 2048
DEPTH = 1

CTX_LEN = 256
GRID_W = 64
EPS = 1e-6
ROPE_THETA = 10000.0

DA_HEADS = 4
DA_HD = 64
DA_WIDTH = DA_HEADS * 2 * DA_HD
DA_Q_BLOCK = 128

ML_HEADS = 4
ML_HD = 128
ML_WIDTH = ML_HEADS * ML_HD
ML_CONV = 3
ML_CHUNK = 128
ML_GATES = 4 * ML_HEADS

IN_WIDTHS = (DA_WIDTH, DA_WIDTH, DA_WIDTH, DA_WIDTH,
             ML_WIDTH, ML_WIDTH, ML_WIDTH, ML_WIDTH,
             ML_GATES,
             D_MODEL, D_MODEL)
IN_WIDTH = 4 * DA_WIDTH + 4 * ML_WIDTH + ML_GATES + 2 * D_MODEL

kernel_name = "hybrid_diffattn_mlstm_dit_block"


def rms_norm(x, g):
    xf = x.astype(jnp.float32)
    y = xf * lax.rsqrt(jnp.mean(xf * xf, axis=-1, keepdims=True) + EPS)
    return (y * g.astype(jnp.float32)).astype(x.dtype)


def split_cols(p):
    idx = []
    acc = 0
    for w in IN_WIDTHS[:-1]:
        acc += w
        idx.append(acc)
    return jnp.split(p, idx, axis=-1)


def split_heads(a, h):
    b, t, _ = a.shape
    return a.reshape(b, t, h, -1).transpose(0, 2, 1, 3)


def merge_heads(a):
    b, h, t, d = a.shape
    return a.transpose(0, 2, 1, 3).reshape(b, t, h * d)


def axial_rope_angles(n_tokens):
    rows = n_tokens // GRID_W
    row_id = jnp.repeat(jnp.arange(rows, dtype=jnp.float32), GRID_W)
    col_id = jnp.tile(jnp.arange(GRID_W, dtype=jnp.float32), rows)
    n_freq = DA_HD // 4
    inv_freq = ROPE_THETA ** (-jnp.arange(n_freq, dtype=jnp.float32) / n_freq)
    return row_id[:, None] * inv_freq, col_id[:, None] * inv_freq


def rope_1d(x, ang):
    nf = ang.shape[-1]
    x1, x2 = x[..., :nf], x[..., nf:]
    cos, sin = jnp.cos(ang), jnp.sin(ang)
    return jnp.concatenate([x1 * cos - x2 * sin, x1 * sin + x2 * cos], axis=-1)


def rope_2d(x, ang_r, ang_c):
    xf = x.astype(jnp.float32)
    half = x.shape[-1] // 2
    y = jnp.concatenate([rope_1d(xf[..., :half], ang_r), rope_1d(xf[..., half:], ang_c)], axis=-1)
    return y.astype(x.dtype)


def diff_maps(a, g):
    b, t, _ = a.shape
    a = a.reshape(b, t, DA_HEADS, 2, DA_HD).transpose(3, 0, 2, 1, 4)
    a = rms_norm(a, g)
    return a[0], a[1]


def diff_attend(q1, q2, k1, k2, v, lam):
    scale = DA_HD ** -0.5
    s1 = jnp.einsum('bhqd,bhkd->bhqk', q1, k1, preferred_element_type=jnp.float32) * scale
    s2 = jnp.einsum('bhqd,bhkd->bhqk', q2, k2, preferred_element_type=jnp.float32) * scale
    p = jax.nn.softmax(s1, axis=-1) - lam * jax.nn.softmax(s2, axis=-1)
    return jnp.einsum('bhqk,bhkd->bhqd', p.astype(v.dtype), v)


def diff_attention_branch(qa_c, ka_c, va_c, za_c, qa_l, ka_l, va_l, za_l,
                          q_g, k_g, lam, lam_init, head_g, ang_r, ang_c, need_ctx):
    q1c, q2c = diff_maps(qa_c, q_g)
    k1c, k2c = diff_maps(ka_c, k_g)
    vc = split_heads(va_c, DA_HEADS)
    q1l, q2l = diff_maps(qa_l, q_g)
    k1l, k2l = diff_maps(ka_l, k_g)
    q1l, q2l = rope_2d(q1l, ang_r, ang_c), rope_2d(q2l, ang_r, ang_c)
    k1l, k2l = rope_2d(k1l, ang_r, ang_c), rope_2d(k2l, ang_r, ang_c)
    vl = split_heads(va_l, DA_HEADS)
    k1 = jnp.concatenate([k1c, k1l], axis=2)
    k2 = jnp.concatenate([k2c, k2l], axis=2)
    v = jnp.concatenate([vc, vl], axis=2)
    b, h, n, _ = q1l.shape
    nb = n // DA_Q_BLOCK

    def to_blocks(q):
        return q.reshape(b, h, nb, DA_Q_BLOCK, DA_HD).transpose(2, 0, 1, 3, 4)

    out = lax.map(lambda qs: diff_attend(qs[0], qs[1], k1, k2, v, lam),
                  (to_blocks(q1l), to_blocks(q2l)))
    out = out.transpose(1, 2, 0, 3, 4).reshape(b, h, n, 2 * DA_HD)

    def finish(o, z):
        return merge_heads(rms_norm(o, head_g) * (1.0 - lam_init)) * jax.nn.silu(z)

    y_l = finish(out, za_l)
    y_c = finish(diff_attend(q1c, q2c, k1c, k2c, vc, lam), za_c) if need_ctx else None
    return y_l, y_c


def short_conv(x, w, bias):
    k_w = w.shape[0]
    pad = k_w // 2
    t = x.shape[1]
    xp = jnp.pad(x, ((0, 0), (pad, pad), (0, 0)))
    y = bias + w[0] * xp[:, 0:t]
    for k in range(1, k_w):
        y = y + w[k] * xp[:, k:k + t]
    return y


def mlstm_prepare(xm, vm, gates, conv_w, conv_b, wq, wk, if_bias):
    b, t, _ = xm.shape
    xc = jax.nn.silu(short_conv(xm, conv_w, conv_b))
    xh = split_heads(xc, ML_HEADS)
    q = jnp.einsum('bhtd,hde->bhte', xh, wq).astype(jnp.float32)
    k = (jnp.einsum('bhtd,hde->bhte', xh, wk) * (ML_HD ** -0.5)).astype(jnp.float32)
    v = split_heads(vm, ML_HEADS).astype(jnp.float32)
    g = (gates + if_bias).astype(jnp.float32).reshape(b, t, 2, 2, ML_HEADS)
    g = g.transpose(2, 3, 0, 4, 1)
    ig = g[:, 0]
    lf = jax.nn.log_sigmoid(g[:, 1])
    return xc, q, k, v, ig, lf


def mlstm_chunkwise(q, k, v, ig, lf, state):
    b, h, t, d = q.shape
    nc = t // ML_CHUNK
    tril = jnp.tril(jnp.ones((ML_CHUNK, ML_CHUNK), dtype=bool))

    def chunks(a):
        a = a.reshape((b, h, nc, ML_CHUNK) + a.shape[3:])
        return jnp.moveaxis(a, 2, 0)

    def step(carry, inp):
        c_st, n_st, m_st = carry
        qc, kc, vc, ic, fc = inp
        cum_f = jnp.cumsum(fc, axis=-1)
        log_d = cum_f[..., :, None] - cum_f[..., None, :] + ic[..., None, :]
        log_d = jnp.where(tril, log_d, -jnp.inf)
        m_inter = cum_f + m_st[..., None]
        m_t = jnp.maximum(m_inter, jnp.max(log_d, axis=-1))
        s = jnp.einsum('bhtd,bhsd->bhts', qc, kc) * jnp.exp(log_d - m_t[..., None])
        dec = jnp.exp(m_inter - m_t)
        num = jnp.einsum('bhts,bhsd->bhtd', s, vc) + dec[..., None] * jnp.einsum('bhtk,bhkv->bhtv', qc, c_st)
        den = jnp.sum(s, axis=-1) + dec * jnp.einsum('bhtk,bhk->bht', qc, n_st)
        h_out = num / jnp.maximum(jnp.abs(den), jnp.exp(-m_t))[..., None]
        f_tot = cum_f[..., -1]
        g_s = f_tot[..., None] - cum_f + ic
        m_new = jnp.maximum(f_tot + m_st, jnp.max(g_s, axis=-1))
        w_s = jnp.exp(g_s - m_new[..., None])
        carry_dec = jnp.exp(f_tot + m_st - m_new)
        c_new = carry_dec[..., None, None] * c_st + jnp.einsum('bhs,bhsk,bhsv->bhkv', w_s, kc, vc)
        n_new = carry_dec[..., None] * n_st + jnp.einsum('bhs,bhsk->bhk', w_s, kc)
        return (c_new, n_new, m_new), h_out

    state, hs = lax.scan(step, state, (chunks(q), chunks(k), chunks(v), chunks(ig), chunks(lf)))
    hs = jnp.moveaxis(hs, 0, 2).reshape(b, h, t, d)
    return hs, state


def mlstm_bidirectional(qc, kc, vc, igc, lfc, ql, kl, vl, igl, lfl):
    b, h, _, d = ql.shape
    zero = (jnp.zeros((b, h, d, d), jnp.float32), jnp.zeros((b, h, d), jnp.float32),
            jnp.zeros((b, h), jnp.float32))

    def flip(a):
        return jnp.flip(a, axis=2)

    hc_f, st_f = mlstm_chunkwise(qc, kc, vc, igc[0], lfc[0], zero)
    hl_f, _ = mlstm_chunkwise(ql, kl, vl, igl[0], lfl[0], st_f)
    hc_b, st_b = mlstm_chunkwise(flip(qc), flip(kc), flip(vc), flip(igc[1]), flip(lfc[1]), zero)
    hl_b, _ = mlstm_chunkwise(flip(ql), flip(kl), flip(vl), flip(igl[1]), flip(lfl[1]), st_b)
    return hc_f + flip(hc_b), hl_f + flip(hl_b)


def mlstm_branch(xm_c, vm_c, g_c, zb_c, ob_c, xm_l, vm_l, g_l, zb_l, ob_l,
                 conv_w, conv_b, wq, wk, if_bias, head_g, skip, need_ctx):
    xcv_c, qc, kc, vc, igc, lfc = mlstm_prepare(xm_c, vm_c, g_c, conv_w, conv_b, wq, wk, if_bias)
    xcv_l, ql, kl, vl, igl, lfl = mlstm_prepare(xm_l, vm_l, g_l, conv_w, conv_b, wq, wk, if_bias)
    hc, hl = mlstm_bidirectional(qc, kc, vc, igc, lfc, ql, kl, vl, igl, lfl)
    g_heads = head_g.reshape(ML_HEADS, 1, ML_HD)

    def finish(hh, xcv, z, o):
        hh = merge_heads(rms_norm(hh, g_heads)).astype(z.dtype)
        return (jax.nn.sigmoid(o) * hh + skip * xcv) * jax.nn.silu(z)

    y_l = finish(hl, xcv_l, zb_l, ob_l)
    y_c = finish(hc, xcv_c, zb_c, ob_c) if need_ctx else None
    return y_l, y_c


def setup_inputs(seed: int = 0) -> dict:
    key = jax.random.key(seed)
    ks = jax.random.split(key, 32)
    f32 = jnp.float32
    D, L = D_MODEL, DEPTH

    def nrm(k, shape, scale):
        return jax.random.normal(k, shape, f32) * scale

    i_bias = nrm(ks[8], (L, 2, 1, ML_HEADS), 0.1)
    f_bias = jnp.linspace(3.0, 6.0, ML_HEADS, dtype=f32) + nrm(ks[9], (L, 2, 1, ML_HEADS), 0.1)
    b_if = jnp.concatenate([i_bias, f_bias], axis=2).reshape(L, ML_GATES)
    return {
        "x": nrm(ks[0], (BATCH, SEQ, D), 1.0),
        "c": nrm(ks[1], (BATCH, D), 1.0),
        "ctx": nrm(ks[2], (BATCH, CTX_LEN, D), 1.0),
        "c_ctx": nrm(ks[3], (D,), 1.0),
        "norm_w": 1.0 + nrm(ks[4], (L, D), 0.1),
        "w_mod": nrm(ks[5], (L, D, 3 * D), 0.5 * D ** -0.5),
        "b_mod": nrm(ks[6], (L, 3 * D), 0.02),
        "w_in": nrm(ks[7], (L, D, IN_WIDTH), D ** -0.5),
        "b_if": b_if,
        "da_q_norm": 1.0 + nrm(ks[10], (L, DA_HD), 0.1),
        "da_k_norm": 1.0 + nrm(ks[11], (L, DA_HD), 0.1),
        "da_lambda_q1": nrm(ks[12], (L, DA_HD), 0.1),
        "da_lambda_k1": nrm(ks[13], (L, DA_HD), 0.1),
        "da_lambda_q2": nrm(ks[14], (L, DA_HD), 0.1),
        "da_lambda_k2": nrm(ks[15], (L, DA_HD), 0.1),
        "da_head_norm": 1.0 + nrm(ks[16], (L, 2 * DA_HD), 0.1),
        "w_out_a": nrm(ks[17], (L, DA_WIDTH, D), DA_WIDTH ** -0.5),
        "ml_conv_w": nrm(ks[18], (L, ML_CONV, ML_WIDTH), ML_CONV ** -0.5),
        "ml_conv_b": nrm(ks[19], (L, ML_WIDTH), 0.02),
        "ml_wq": nrm(ks[20], (L, ML_HEADS, ML_HD, ML_HD), ML_HD ** -0.5),
        "ml_wk": nrm(ks[21], (L, ML_HEADS, ML_HD, ML_HD), ML_HD ** -0.5),
        "ml_head_norm": 1.0 + nrm(ks[22], (L, ML_WIDTH), 0.1),
        "ml_skip": 1.0 + nrm(ks[23], (L, ML_WIDTH), 0.1),
        "w_out_b": nrm(ks[24], (L, ML_WIDTH, D), ML_WIDTH ** -0.5),
        "w_o": nrm(ks[25], (L, D, D), D ** -0.5),
    }


def reference(x, c, ctx, c_ctx, norm_w, w_mod, b_mod, w_in, b_if, da_q_norm, da_k_norm,
              da_lambda_q1, da_lambda_k1, da_lambda_q2, da_lambda_k2, da_head_norm, w_out_a,
              ml_conv_w, ml_conv_b, ml_wq, ml_wk, ml_head_norm, ml_skip, w_out_b, w_o):
    n_lat = x.shape[1]
    ang_r, ang_c = axial_rope_angles(n_lat)
    for l in range(DEPTH):
        need_ctx = l < DEPTH - 1
        lam_init = 0.8 - 0.6 * math.exp(-0.3 * l)
        mod_l = jax.nn.silu(c) @ w_mod[l] + b_mod[l]
        mod_c = jax.nn.silu(c_ctx) @ w_mod[l] + b_mod[l]
        sh_l, sc_l, gt_l = jnp.split(mod_l, 3, axis=-1)
        sh_c, sc_c, gt_c = jnp.split(mod_c, 3, axis=-1)
        h_l = rms_norm(x, norm_w[l]) * (1.0 + sc_l[:, None]) + sh_l[:, None]
        h_c = rms_norm(ctx, norm_w[l]) * (1.0 + sc_c) + sh_c
        qa_l, ka_l, va_l, za_l, xm_l, vm_l, zb_l, ob_l, gif_l, ga_l, gb_l = split_cols(h_l @ w_in[l])
        qa_c, ka_c, va_c, za_c, xm_c, vm_c, zb_c, ob_c, gif_c, ga_c, gb_c = split_cols(h_c @ w_in[l])
        lam = (jnp.exp(jnp.sum(da_lambda_q1[l].astype(jnp.float32) * da_lambda_k1[l].astype(jnp.float32)))
               - jnp.exp(jnp.sum(da_lambda_q2[l].astype(jnp.float32) * da_lambda_k2[l].astype(jnp.float32)))
               + lam_init)
        ya_l, ya_c = diff_attention_branch(qa_c, ka_c, va_c, za_c, qa_l, ka_l, va_l, za_l,
                                           da_q_norm[l], da_k_norm[l], lam, lam_init,
                                           da_head_norm[l], ang_r, ang_c, need_ctx)
        yb_l, yb_c = mlstm_branch(xm_c, vm_c, gif_c, zb_c, ob_c, xm_l, vm_l, gif_l, zb_l, ob_l,
                                  ml_conv_w[l], ml_conv_b[l], ml_wq[l], ml_wk[l], b_if[l],
                                  ml_head_norm[l], ml_skip[l], need_ctx)
        y_l = (jax.nn.sigmoid(ga_l) * (ya_l @ w_out_a[l])
               + jax.nn.sigmoid(gb_l) * (yb_l @ w_out_b[l])) @ w_o[l]
        if need_ctx:
            y_c = (jax.nn.sigmoid(ga_c) * (ya_c @ w_out_a[l])
                   + jax.nn.sigmoid(gb_c) * (yb_c @ w_out_b[l])) @ w_o[l]
            ctx = ctx + gt_c * y_c
        x = x + gt_l[:, None] * y_l
    return x
```

```python
import contextlib
import numpy as np
import concourse.bass as bass
import concourse.mybir as mybir
from concourse.bass_utils import run_bass_kernel_spmd

F32 = mybir.dt.float32
BF16 = mybir.dt.bfloat16
AF = mybir.ActivationFunctionType
ALU = mybir.AluOpType
AX = mybir.AxisListType

COMPUTE = ("pe", "act", "dve", "pool")
ENGS = ("pe", "act", "dve", "pool", "sp")

D = 1024
NB = 2
T = 2048
CT = 256
TT = T + CT
NT = TT // 128
EPS = 1e-6
INW = 6160
C_Q, C_K, C_V, C_ZA, C_XM, C_VM, C_ZB, C_OB, C_G, C_GA, C_GB = 0, 512, 1024, 1536, 2048, 2560, 3072, 3584, 4096, 4112, 5136
LAM_INIT = 0.2


class Op:
    __slots__ = ("eng", "fn", "deps", "signal", "dma", "cnt")

    def __init__(self, eng, fn, dma):
        self.eng = eng
        self.fn = fn
        self.deps = []
        self.signal = False
        self.dma = dma
        self.cnt = 0


class Sched:
    def __init__(self, nc):
        self.nc = nc
        self.ops = {e: [] for e in ENGS}
        self.last_w = {}
        self.readers = {}
        self.dma_streams = {}

    def op(self, eng, fn, reads=(), writes=(), dma=None):
        o = Op(eng, fn, dma)
        deps = []
        is_async = dma is not None
        for r in reads:
            w = self.last_w.get(r)
            if w is not None:
                deps.append(w)
        for r in writes:
            w = self.last_w.get(r)
            if w is not None and (is_async or w.dma is not None or w.eng != eng):
                deps.append(w)
            for rd in self.readers.get(r, ()):
                if is_async or rd.dma is not None or rd.eng != eng:
                    deps.append(rd)
        if dma is not None:
            lst = self.dma_streams.setdefault(dma, [])
            if lst:
                deps.append(lst[-1])
            lst.append(o)
        seen = set()
        for d in deps:
            if id(d) not in seen:
                seen.add(id(d))
                o.deps.append(d)
                d.signal = True
        for r in reads:
            self.readers.setdefault(r, []).append(o)
        for r in writes:
            self.last_w[r] = o
            self.readers[r] = []
        self.ops[eng].append(o)
        return o

    def emit(self, final_waits):
        nc = self.nc
        with contextlib.ExitStack() as es:
            sems = {e: es.enter_context(nc.semaphore("s_" + e)) for e in COMPUTE}
            dsems = {k: es.enter_context(nc.semaphore("d_" + k)) for k in self.dma_streams}
            for e in ENGS:
                c = 0
                for o in self.ops[e]:
                    if o.dma is None and o.signal:
                        c += 1
                    o.cnt = c
            for k, lst in self.dma_streams.items():
                for i, o in enumerate(lst):
                    o.cnt = 16 * (i + 1)
            block = es.enter_context(nc.Block())

            def run(engname, eng):
                waited = {}
                for o in self.ops[engname]:
                    for d in o.deps:
                        if d.dma is not None:
                            sem, key = dsems[d.dma], ("d", d.dma)
                        else:
                            sem, key = sems[d.eng], ("e", d.eng)
                        if waited.get(key, 0) >= d.cnt:
                            continue
                        waited[key] = d.cnt
                        eng.wait_ge(sem, d.cnt)
                    inst = o.fn(eng)
                    if o.dma is not None:
                        inst.then_inc(dsems[o.dma], 16)
                    elif o.signal:
                        inst.then_inc(sems[engname], 1)
                for d in final_waits.get(engname, ()):
                    eng.wait_ge(dsems[d.dma], d.cnt)

            @block.tensor
            def _(eng):
                run("pe", eng)

            @block.scalar
            def _(eng):
                run("act", eng)

            @block.vector
            def _(eng):
                run("dve", eng)

            @block.gpsimd
            def _(eng):
                run("pool", eng)

            @block.sync
            def _(eng):
                run("sp", eng)


def build_nc(debug=False):
    nc = bass.Bass("TRN2", target_bir_lowering=False)

    def dram(name, shape, kind="ExternalInput"):
        return nc.dram_tensor(name, list(shape), F32, kind=kind).ap()

    x_d = dram("x", [NB, T, D])
    ctx_d = dram("ctx", [NB, CT, D])
    cvec_d = dram("cvec", [3, D])
    wmod_d = dram("w_mod", [D, 3 * D])
    win_d = dram("w_in", [D, INW])
    prm_d = dram("params", [64, 128])
    bmod_d = dram("b_mod", [1, 3 * D])
    lamv_d = dram("lamv", [1, 256])
    bif_d = dram("b_if", [1, 16])
    wq_d = dram("ml_wq", [4, 128, 128])
    wk_d = dram("ml_wk", [4, 128, 128])
    woa_d = dram("w_out_a", [512, D])
    wob_d = dram("w_out_b", [512, D])
    wo_d = dram("w_o", [D, D])
    cst_d = dram("consts", [128, 1408])
    rope_d = dram("rope", [128, 2, T])
    out_d = dram("out", [NB, T, D], kind="ExternalOutput")
    dbg = {}
    if debug:
        dbg["hT"] = dram("dbg_hT", [128, 8 * TT], kind="ExternalOutput")
        dbg["yaT"] = dram("dbg_yaT", [128, 4 * T], kind="ExternalOutput")
        dbg["ybT"] = dram("dbg_ybT", [128, 4 * T], kind="ExternalOutput")

    es = contextlib.ExitStack()
    with es:
        def sb(name, shape, dt):
            return es.enter_context(nc.sbuf_tensor("sb_" + name, list(shape), dt))

        PS = es.enter_context(nc.psum_tensor("PS", [128, 4096], F32))

        def bank(i, n=512, off=0):
            return PS[:, i * 512 + off:i * 512 + off + n]

        def PK(*banks):
            return [("ps", b) for b in banks]

        S = Sched(nc)

        def MM(out, lhsT, rhs, start, stop, reads, writes):
            S.op("pe", lambda e: e.matmul(out, lhsT, rhs, start=start, stop=stop), reads, writes)

        def TR(out, in_, ident, reads, writes):
            S.op("pe", lambda e: e.transpose(out, in_, ident), reads, writes)

        def ACT(out, in_, func, reads, writes, scale=1.0, bias=None, accum=None):
            def f(e):
                kw = {}
                if bias is not None:
                    kw["bias"] = bias
                if accum is not None:
                    kw["accum_out"] = accum
                return e.activation(out=out, in_=in_, func=func, scale=scale, **kw)
            S.op("act", f, reads, writes)

        def TS(eng, out, in0, s1, s2, op0, op1, reads, writes):
            if s2 is None:
                S.op(eng, lambda e: e.tensor_scalar(out=out, in0=in0, scalar1=s1, scalar2=None, op0=op0), reads, writes)
            else:
                S.op(eng, lambda e: e.tensor_scalar(out=out, in0=in0, scalar1=s1, scalar2=s2, op0=op0, op1=op1), reads, writes)

        def TTo(eng, out, in0, in1, op, reads, writes):
            S.op(eng, lambda e: e.tensor_tensor(out=out, in0=in0, in1=in1, op=op), reads, writes)

        def STT(eng, out, in0, scalar, in1, op0, op1, reads, writes):
            S.op(eng, lambda e: e.scalar_tensor_tensor(out=out, in0=in0, scalar=scalar, in1=in1, op0=op0, op1=op1), reads, writes)

        def CP(eng, out, in_, reads, writes):
            if eng == "act":
                return ACT(out, in_, AF.Copy, reads, writes)
            S.op(eng, lambda e: e.tensor_copy(out=out, in_=in_), reads, writes)

        def MS(eng, ap, val, writes):
            S.op(eng, lambda e: e.memset(ap, val), (), writes)

        def DMA(eng, out, in_, stream, reads, writes):
            return S.op(eng, lambda e: e.dma_start(out=out, in_=in_), reads, writes, dma=stream)

        def sigmoid_act(out, in_, tmp, reads, writes, tkey):
            ACT(tmp, in_, AF.Exp, reads, [tkey], scale=-1.0)
            ACT(tmp, tmp, AF.Ln, [tkey], [tkey], bias=1.0)
            ACT(out, tmp, AF.Exp, [tkey], writes, scale=-1.0)

        def rstd_act(out, in_, n, reads, writes):
            ACT(out, in_, AF.Ln, reads, writes, scale=1.0 / n, bias=EPS)
            ACT(out, out, AF.Exp, writes, writes, scale=-0.5)

        cst = sb("cst", [128, 1408], F32)
        ident = cst[:, 0:128]
        mask_f = cst[:, 128:256]
        mask_b = cst[:, 256:384]
        tri_f = cst[:, 384:512]
        tri_b = cst[:, 512:640]
        ones = cst[:, 640:768]
        c16 = sb("c16", [128, 384], BF16)
        ident16 = c16[:, 0:128]
        blk16 = c16[:, 128:256]
        perm16 = c16[:, 256:384]
        rope = sb("rope", [128, 2, T], F32)
        pcol = sb("pcol", [128, 64], F32)
        gcol = sb("gcol", [128, 4], F32)
        gtrow = sb("gtrow", [3, D], F32)
        modT = sb("modT", [128, 16, 3], F32)
        svec = sb("svec", [128, 8, 3], F32)
        bifbc = sb("bifbc", [128, 16], F32)
        wqk16 = sb("wqk16", [128, 2, 4, 128], BF16)
        wg16 = sb("wg16", [128, 8, 16], BF16)
        hT = sb("hT", [128, 8, TT], BF16)
        yT = sb("yT", [128, 2, 4, T], BF16)
        smallc = sb("smallc", [128, 64], F32)
        ARENA_N = 53000
        arena = sb("arena", [128, ARENA_N], BF16)

        class Carve:
            def __init__(self):
                self.off = 0
                self.keys = []

            def take(self, key, nelem, dt):
                if dt == F32:
                    if self.off % 2:
                        self.off += 1
                    ap = arena[:, self.off:self.off + 2 * nelem].bitcast(F32)
                    self.off += 2 * nelem
                else:
                    ap = arena[:, self.off:self.off + nelem]
                    self.off += nelem
                    if self.off % 2:
                        self.off += 1
                assert self.off <= ARENA_N, (key, self.off)
                self.keys.append(key)
                return ap

        prev_keys = []

        def fence(new_keys):
            nonlocal prev_keys
            allk = list(dict.fromkeys(prev_keys + new_keys))
            MS("pool", smallc[:, 63:64], 0.0, allk + ["fence_dummy"])
            prev_keys = list(new_keys)

        DMA("sp", cst[:], cst_d, "c0", [], ["cst"])
        DMA("sp", rope[:], rope_d, "c1", [], ["rope"])
        CP("dve", c16[:, 0:128], cst[:, 0:128], ["cst"], ["c16"])
        CP("dve", c16[:, 128:384], cst[:, 768:1024], ["cst"], ["c16"])
        DMA("sp", bifbc[:], bif_d.partition_broadcast(128), "c2", [], ["bifbc"])
        DMA("pool", wqk16[:, 0], wq_d.rearrange("h d e -> d h e"), "c3", [], ["wqk16"])
        DMA("pool", wqk16[:, 1], wk_d.rearrange("h d e -> d h e"), "c4", [], ["wqk16"])
        DMA("pool", wg16[:], win_d[:, C_G:C_G + 16].rearrange("(kc p) c -> p kc c", p=128), "c5", [], ["wg16"])

        P0 = Carve()
        prm = P0.take("prm", 128, F32)[0:64, :]
        cs = P0.take("cs", D, F32)[0:3, :]
        cs2 = P0.take("cs2", D, F32)[0:3, :]
        scT = P0.take("scT", 24, F32).rearrange("p (k b) -> p k b", b=3)
        brow = P0.take("brow", 3 * D, F32)[0:3, :]
        modrow = P0.take("modrow", 3 * D, F32)[0:3, :]
        lv = P0.take("lv", 256, F32)[0:1, :]
        lt = P0.take("lt", 128, F32)[0:1, :]
        ls = P0.take("ls", 8, F32)[0:1, :]
        wms = [P0.take("wm%d" % i, 8 * 512, F32).rearrange("p (k c) -> p k c", c=512) for i in range(2)]
        fence(P0.keys)
        DMA("sp", prm, prm_d, "c6", [], ["prm"])
        DMA("sp", cs, cvec_d, "c7", [], ["cs"])
        DMA("sp", brow, bmod_d.partition_broadcast(3), "c8", [], ["brow"])
        DMA("sp", lv, lamv_d, "c9", [], ["lv"])
        TR(bank(0, 64), prm, ident[0:64, 0:64], ["prm", "cst"], PK(0))
        CP("dve", pcol[:], bank(0, 64), PK(0), ["pcol"])
        TS("dve", gcol[:, 0:1], pcol[:, 32:33], 1.0 - LAM_INIT, None, ALU.mult, None, ["pcol"], ["gcol"])
        TS("dve", gcol[:, 1:2], pcol[:, 33:34], 0.125, None, ALU.mult, None, ["pcol"], ["gcol"])
        CP("dve", gcol[:, 2:3], pcol[:, 34:35], ["pcol"], ["gcol"])
        lv4 = lv.rearrange("p (a b c) -> p a b c", a=2, b=2)
        TTo("dve", lt.rearrange("p (a c) -> p a c", a=2), lv4[:, :, 0, :], lv4[:, :, 1, :], ALU.mult, ["lv"], ["lt"])
        S.op("dve", lambda e: e.reduce_sum(out=ls[:, 0:2], in_=lt.rearrange("p (a c) -> p a c", a=2), axis=AX.X), ["lt"], ["ls"])
        ACT(ls[:, 2:4], ls[:, 0:2], AF.Exp, ["ls"], ["ls2"])
        TTo("dve", ls[:, 4:5], ls[:, 3:4], ls[:, 2:3], ALU.subtract, ["ls2"], ["ls3"])
        TS("dve", ls[:, 5:6], ls[:, 4:5], -LAM_INIT, None, ALU.add, None, ["ls3"], ["ls4"])
        MM(bank(0, 1, 64), ones[0:1, 0:128], ls[:, 5:6], True, True, ["ls4", "cst"], PK(0))
        CP("dve", gcol[:, 3:4], bank(0, 1, 64), PK(0), ["gcol"])
        sigmoid_act(cs2, cs, cs2, ["cs"], ["cs2"], "cs2")
        TTo("dve", cs2, cs2, cs, ALU.mult, ["cs2", "cs"], ["cs2"])
        for kc in range(8):
            TR(bank(1, 3, 4 * kc), cs2[:, kc * 128:(kc + 1) * 128], ident[0:3, 0:3], ["cs2", "cst"], PK(1))
        CP("dve", scT, bank(1, 32).rearrange("p (k b) -> p k b", b=4)[:, :, 0:3], PK(1), ["scT"])
        for cg in range(6):
            wm = wms[cg % 2]
            wk_ = "wm%d" % (cg % 2)
            DMA("sp", wm, wmod_d[:, cg * 512:(cg + 1) * 512].rearrange("(kc p) c -> p kc c", p=128), wk_, [], [wk_])
            pb = 2 + (cg % 2)
            for kc in range(8):
                MM(bank(pb)[0:3, :], scT[:, kc, :], wm[:, kc, :], kc == 0, kc == 7, ["scT", wk_], PK(pb))
            TTo("dve", modrow[:, cg * 512:(cg + 1) * 512], bank(pb)[0:3, :], brow[:, cg * 512:(cg + 1) * 512], ALU.add,
                PK(pb) + ["brow"], ["modrow"])
        for j in range(16):
            TR(bank(1, 3, 4 * j), modrow[:, j * 128:(j + 1) * 128], ident[0:3, 0:3], ["modrow", "cst"], PK(1))
        CP("dve", modT[:], bank(1, 64).rearrange("p (k b) -> p k b", b=4)[:, :, 0:3], PK(1), ["modT"])
        TS("dve", svec[:], modT[:, 8:16, :], 1.0, None, ALU.add, None, ["modT"], ["svec"])
        TTo("dve", svec[:], svec[:], pcol[:, 0:8].unsqueeze(2).to_broadcast([128, 8, 3]), ALU.mult, ["svec", "pcol"], ["svec"])
        CP("dve", gtrow[:], modrow[:, 2048:3072], ["modrow"], ["gtrow"])

        out_ops = []

        for b in range(NB):
            PA = Carve()
            xts = [PA.take("xt%d" % i, D, F32) for i in range(2)]
            sqj = PA.take("sqj", D, BF16)
            xns = [PA.take("xn%d" % i, D, BF16) for i in range(2)]
            fence(PA.keys)
            groups = [(0, 2)] + [(2 + 4 * g, 4) for g in range(4)]
            psb_all = PS[:, 0:2048].bitcast(BF16).rearrange("p (k t) -> p k t", t=512)
            ti = 0
            for (t0, ntile) in groups:
                mi = 2 if t0 == 0 else b
                for j in range(ntile):
                    tt = t0 + j
                    sl = ti % 2
                    ti += 1
                    src = ctx_d[b, tt * 128:(tt + 1) * 128, :] if tt < 2 else x_d[b, (tt - 2) * 128:(tt - 1) * 128, :]
                    DMA("sp", xts[sl], src, "xt%d" % sl, [], ["xt%d" % sl])
                    ACT(sqj, xts[sl], AF.Square, ["xt%d" % sl], ["sqj", "ssA"], accum=smallc[:, 0:1])
                    rstd_act(smallc[:, 1:2], smallc[:, 0:1], D, ["ssA"], ["rsA"])
                    TS("pool" if sl else "dve", xns[sl], xts[sl], smallc[:, 1:2], None, ALU.mult, None, ["xt%d" % sl, "rsA"], ["xn%d" % sl])
                    for kc in range(8):
                        TR(psb_all[:, kc, j * 128:(j + 1) * 128], xns[sl][:, kc * 128:(kc + 1) * 128], ident16,
                           ["xn%d" % sl, "c16"], PK(kc // 2))
                n = ntile * 128
                for kc in range(8):
                    o_ = hT[:, kc, t0 * 128:t0 * 128 + n]
                    i_ = psb_all[:, kc, 0:n]
                    if kc % 2:
                        ACT(o_, i_, AF.Identity, PK(kc // 2) + ["svec", "modT"], ["hT"], scale=svec[:, kc, mi:mi + 1], bias=modT[:, kc, mi:mi + 1])
                    else:
                        TS("dve", o_, i_, svec[:, kc, mi:mi + 1], modT[:, kc, mi:mi + 1], ALU.mult, ALU.add, PK(kc // 2) + ["svec", "modT"], ["hT"])
            if debug and b == 0:
                out_ops.append(DMA("pool", dbg["hT"], hT[:].rearrange("p k t -> p (k t)"), "dbgA", ["hT"], []))

            PB = Carve()
            wv16 = PB.take("wv16", 8 * 512, BF16).rearrange("p (k c) -> p k c", c=512)
            wh = [[PB.take("wh%d_%d" % (i, j), 8 * 128, BF16).rearrange("p (k c) -> p k c", c=128) for j in range(3)] for i in range(2)]
            vaug = PB.take("vaug", NT * 4 * 130, BF16).rearrange("p (t h c) -> p t h c", h=4, c=130)
            kT = PB.take("kT", TT, BF16)
            qT = PB.take("qT", 512, BF16)
            E = [PB.take("E%d" % m, NT * 512, BF16).rearrange("p (t q) -> p t q", q=512) for m in range(2)]
            x2 = PB.take("x2", 512, BF16)
            rsb = PB.take("rsb", 512, F32)
            xn = PB.take("xnq", 512, F32)
            xn16 = PB.take("xn16", 512, BF16)
            t1 = PB.take("t1", 512, F32)
            t2 = PB.take("t2", 512, F32)
            zs = PB.take("zs", 512, F32)
            ztmp = PB.take("ztmp", 512, F32)
            osb = PB.take("osb", 128, F32)
            onb = PB.take("onb", 128, F32)
            fence(PB.keys)
            DMA("pool", wv16, win_d[:, C_V:C_V + 512].rearrange("(kc p) c -> p kc c", p=128), "wv", [], ["wv16"])
            MS("dve", vaug[:, :, :, 128:130], 1.0, ["vaug"])
            for tt in range(NT):
                pb = tt % 2
                for kc in range(8):
                    MM(bank(pb), hT[:, kc, tt * 128:(tt + 1) * 128], wv16[:, kc, :], kc == 0, kc == 7, ["hT", "wv16"], PK(pb))
                CP("act" if False else "dve", vaug[:, tt, :, 0:128], bank(pb).rearrange("p (h c) -> p h c", c=128), PK(pb), ["vaug"])

            def qk_norm_rope(P, n, gc, rope_off, out16, wkey):
                ACT(x2[:, 0:n], bank(P, n), AF.Square, PK(P), ["x2"])
                MM(bank(6, n), blk16, x2[:, 0:n], True, True, ["x2", "c16"], PK(6))
                ACT(rsb[:, 0:n], bank(6, n), AF.Ln, PK(6), ["rsb"], bias=EPS)
                ACT(rsb[:, 0:n], rsb[:, 0:n], AF.Exp, ["rsb"], ["rsb"], scale=-0.5)
                if rope_off is None:
                    STT("dve", out16, bank(P, n), gc, rsb[:, 0:n], ALU.mult, ALU.mult, PK(P) + ["rsb", "gcol"], [wkey])
                    return
                STT("dve", xn[:, 0:n], bank(P, n), gc, rsb[:, 0:n], ALU.mult, ALU.mult, PK(P) + ["rsb", "gcol"], ["xnq"])
                CP("pool", xn16[:, 0:n], xn[:, 0:n], ["xnq"], ["xn16"])
                MM(bank(7, n), perm16, xn16[:, 0:n], True, True, ["xn16", "c16"], PK(7))
                TTo("pool", t1[:, 0:n], xn[:, 0:n], rope[:, 0, rope_off:rope_off + n], ALU.mult, ["xnq", "rope"], ["t1"])
                TTo("dve", t2[:, 0:n], bank(7, n), rope[:, 1, rope_off:rope_off + n], ALU.mult, PK(7) + ["rope"], ["t2"])
                TTo("pool", out16, t1[:, 0:n], t2[:, 0:n], ALU.add, ["t1", "t2"], [wkey])

            for h in range(4):
                ws = wh[h % 2]
                wkeys = ["wh%d_%d" % (h % 2, j) for j in range(3)]
                for j, c0 in enumerate((C_Q, C_K, C_ZA)):
                    DMA("pool", ws[j], win_d[:, c0 + h * 128:c0 + (h + 1) * 128].rearrange("(kc p) c -> p kc c", p=128),
                        wkeys[j], [], [wkeys[j]])
                for (tok0, n, roff) in [(0, 256, None)] + [(256 + 512 * g, 512, 512 * g) for g in range(4)]:
                    for kc in range(8):
                        MM(bank(0, n), ws[1][:, kc, :], hT[:, kc, tok0:tok0 + n], kc == 0, kc == 7, ["hT", wkeys[1]], PK(0))
                    qk_norm_rope(0, n, gcol[:, 2:3], roff, kT[:, tok0:tok0 + n], "kT")
                for qg in range(4):
                    q0 = 256 + 512 * qg
                    for kc in range(8):
                        MM(bank(0), ws[0][:, kc, :], hT[:, kc, q0:q0 + 512], kc == 0, kc == 7, ["hT", wkeys[0]], PK(0))
                    qk_norm_rope(0, 512, gcol[:, 1:2], 512 * qg, qT[:, :], "qT")
                    for kc in range(8):
                        MM(bank(1), ws[2][:, kc, :], hT[:, kc, q0:q0 + 512], kc == 0, kc == 7, ["hT", wkeys[2]], PK(1))
                    sigmoid_act(ztmp, bank(1), ztmp, PK(1), ["ztmp"], "ztmp")
                    TTo("dve", zs, bank(1), ztmp, ALU.mult, PK(1) + ["ztmp"], ["zs"])
                    for m in range(2):
                        for kp in range(NT // 2):
                            pb = 2 + 2 * (kp % 2)
                            for u in range(2):
                                kt = 2 * kp + u
                                MM(bank(pb + u), kT[m * 64:(m + 1) * 64, kt * 128:(kt + 1) * 128], qT[m * 64:(m + 1) * 64, :], True, True,
                                   ["kT", "qT"], PK(pb + u))
                            ACT(E[m][:, 2 * kp:2 * kp + 2, :], PS[:, pb * 512:(pb + 2) * 512].rearrange("p (t q) -> p t q", q=512), AF.Exp,
                                PK(pb, pb + 1), ["E%d" % m])
                    for qt in range(4):
                        for m in range(2):
                            pb = 6 + m
                            for kt in range(NT):
                                MM(bank(pb, 130), E[m][:, kt, qt * 128:(qt + 1) * 128], vaug[:, kt, h, :], kt == 0, kt == NT - 1,
                                   ["E%d" % m, "vaug"], PK(pb))
                        S.op("dve", lambda e: e.reciprocal(out=smallc[:, 4:5], in_=bank(6, 1, 128)), PK(6), ["rB1"])
                        S.op("dve", lambda e: e.reciprocal(out=smallc[:, 5:6], in_=bank(7, 1, 128)), PK(7), ["rB2"])
                        TTo("dve", smallc[:, 5:6], smallc[:, 5:6], gcol[:, 3:4], ALU.mult, ["rB2", "gcol"], ["rB2"])
                        ACT(osb, bank(6, 128), AF.Copy, PK(6) + ["rB1"], ["osb"], scale=smallc[:, 4:5])
                        STT("dve", osb, bank(7, 128), smallc[:, 5:6], osb, ALU.mult, ALU.add, PK(7) + ["rB2", "osb"], ["osb"])
                        ACT(onb, osb, AF.Square, ["osb"], ["onb", "ssB"], accum=smallc[:, 6:7])
                        rstd_act(smallc[:, 7:8], smallc[:, 6:7], 128, ["ssB"], ["rsB"])
                        TS("dve", onb, osb, smallc[:, 7:8], None, ALU.mult, None, ["osb", "rsB"], ["onb"])
                        TR(bank(1, 128), onb, ident, ["onb", "cst"], PK(1))
                        tq = 512 * qg + 128 * qt
                        STT("dve", yT[:, 0, h, tq:tq + 128], bank(1, 128), gcol[:, 0:1], zs[:, qt * 128:(qt + 1) * 128], ALU.mult, ALU.mult,
                            PK(1) + ["gcol", "zs"], ["yT"])
            if debug and b == 0:
                out_ops.append(DMA("pool", dbg["yaT"], yT[:, 0].rearrange("p h t -> p (h t)"), "dbgB", ["yT"], []))

            PC = Carve()
            whc = [[PC.take("whc%d_%d" % (i, j), 8 * 128, BF16).rearrange("p (k c) -> p k c", c=128) for j in range(4)] for i in range(2)]
            gsb = PC.take("gsb", NT * 16, F32).rearrange("p (t c) -> p t c", c=16)
            nlf = PC.take("nlf", 2 * NT * 4, F32).rearrange("p (d t h) -> p d t h", d=2, h=4)
            garg = PC.take("garg", 2 * NT * 4, F32).rearrange("p (d t h) -> p d t h", d=2, h=4)
            wcol = PC.take("wcol", 2 * NT * 4, F32).rearrange("p (d t h) -> p d t h", d=2, h=4)
            wkcol = PC.take("wkcol", 2 * NT * 4, F32).rearrange("p (d t h) -> p d t h", d=2, h=4)
            fcol = PC.take("fcol", 2 * NT * 4, F32).rearrange("p (d t h) -> p d t h", d=2, h=4)
            cbc = PC.take("cbc", 2 * NT * 4, F32).rearrange("p (d t h) -> p d t h", d=2, h=4)
            xp_c = PC.take("xp_c", CT + 2, F32)
            xp_l = PC.take("xp_l", T + 2, F32)
            xc = PC.take("xc", TT, F32)
            xctmp = PC.take("xctmp", TT, F32)
            xc16 = PC.take("xc16c", TT, BF16)
            qmT = PC.take("qmT", TT, BF16)
            kmT = PC.take("kmT", TT, BF16)
            kw = PC.take("kw", 2 * NT * 128, BF16).rearrange("p (d t c) -> p d t c", d=2, c=128)
            vm = PC.take("vm", NT * 130, BF16).rearrange("p (t c) -> p t c", c=130)
            hf = PC.take("hf", 16 * 128, F32).rearrange("p (t c) -> p t c", c=128)
            P32 = PC.take("P32", 130, F32)
            C16 = PC.take("C16", 130, BF16)
            sT16 = PC.take("sT16", 128, BF16)
            hs = PC.take("hs", 128, F32)
            hn = PC.take("hn", 128, F32)
            sigo = PC.take("sigo", 512, F32)
            silz = PC.take("silz", 512, F32)
            gtmp = PC.take("gtmp", 512, F32)
            yt1 = PC.take("yt1", 128, F32)
            yt2 = PC.take("yt2", 128, F32)
            fence(PC.keys)
            for tt in range(NT):
                for kc in range(8):
                    MM(bank(0, 16, 16 * tt), hT[:, kc, tt * 128:(tt + 1) * 128], wg16[:, kc, :], kc == 0, kc == 7, ["hT", "wg16"], PK(0))
            TTo("dve", gsb, bank(0, NT * 16).rearrange("p (t c) -> p t c", c=16), bifbc[:].unsqueeze(1).to_broadcast([128, NT, 16]), ALU.add,
                PK(0) + ["bifbc"], ["gsb"])
            g5 = gsb.rearrange("p t (d g h) -> p d g t h", d=2, g=2)
            for d in range(2):
                ACT(nlf[:, d], g5[:, d, 1], AF.Exp, ["gsb"], ["nlf"], scale=-1.0)
            nlf2 = nlf.rearrange("p d t h -> p (d t h)")
            ACT(nlf2, nlf2, AF.Ln, ["nlf"], ["nlf"], bias=1.0)
            NG = NT * 4
            for d in range(2):
                MM(bank(1, NG, d * NG), tri_f if d == 0 else tri_b, nlf[:, d].rearrange("p t h -> p (t h)"), True, True, ["nlf", "cst"], PK(1))
            MM(bank(2, 2 * NG), ones, nlf2, True, True, ["nlf", "cst"], PK(2))
            for d in range(2):
                TTo("dve", garg[:, d], g5[:, d, 0], bank(1, NG, d * NG).rearrange("p (t h) -> p t h", h=4), ALU.subtract, PK(1) + ["gsb"], ["garg"])
            ACT(wcol.rearrange("p d t h -> p (d t h)"), garg.rearrange("p d t h -> p (d t h)"), AF.Exp, ["garg"], ["wcol"])
            ACT(fcol.rearrange("p d t h -> p (d t h)"), bank(1, 2 * NG), AF.Exp, PK(1), ["fcol"], scale=-1.0)
            ACT(cbc.rearrange("p d t h -> p (d t h)"), bank(2, 2 * NG), AF.Exp, PK(2), ["cbc"], scale=-1.0)
            TS("dve", wkcol.rearrange("p d t h -> p (d t h)"), wcol.rearrange("p d t h -> p (d t h)"), 128.0 ** -0.5, None, ALU.mult, None,
               ["wcol"], ["wkcol"])
            MS("dve", vm[:, :, 128:130], 1.0, ["vm"])
            MS("dve", xp_c[:, 0:1], 0.0, ["xp_c"])
            MS("dve", xp_c[:, CT + 1:CT + 2], 0.0, ["xp_c"])
            MS("dve", xp_l[:, 0:1], 0.0, ["xp_l"])
            MS("dve", xp_l[:, T + 1:T + 2], 0.0, ["xp_l"])

            tokgroups = [(0, 256)] + [(256 + 512 * g, 512) for g in range(4)]
            for h in range(4):
                ws = whc[h % 2]
                wkeys = ["whc%d_%d" % (h % 2, j) for j in range(4)]
                for j, c0 in enumerate((C_XM, C_VM, C_ZB, C_OB)):
                    DMA("pool", ws[j], win_d[:, c0 + h * 128:c0 + (h + 1) * 128].rearrange("(kc p) c -> p kc c", p=128),
                        wkeys[j], [], [wkeys[j]])
                for gi, (tok0, n) in enumerate(tokgroups):
                    pb = gi % 2
                    for kc in range(8):
                        MM(bank(pb, n), ws[0][:, kc, :], hT[:, kc, tok0:tok0 + n], kc == 0, kc == 7, ["hT", wkeys[0]], PK(pb))
                    if tok0 == 0:
                        CP("act", xp_c[:, 1:1 + CT], bank(pb, n), PK(pb), ["xp_c"])
                    else:
                        CP("act", xp_l[:, 1 + tok0 - CT:1 + tok0 - CT + n], bank(pb, n), PK(pb), ["xp_l"])
                cw = lambda k: pcol[:, 8 + 4 * k + h:9 + 4 * k + h]
                for (xp, key, o0, n) in ((xp_c, "xp_c", 0, CT), (xp_l, "xp_l", CT, T)):
                    TS("dve", xctmp[:, o0:o0 + n], xp[:, 1:1 + n], cw(1), pcol[:, 20 + h:21 + h], ALU.mult, ALU.add, [key, "pcol"], ["xctmp"])
                    STT("dve", xctmp[:, o0:o0 + n], xp[:, 0:n], cw(0), xctmp[:, o0:o0 + n], ALU.mult, ALU.add, [key, "pcol", "xctmp"], ["xctmp"])
                    STT("dve", xctmp[:, o0:o0 + n], xp[:, 2:2 + n], cw(2), xctmp[:, o0:o0 + n], ALU.mult, ALU.add, [key, "pcol", "xctmp"], ["xctmp"])
                sigmoid_act(xc, xctmp, xc, ["xctmp"], ["xc"], "xc")
                TTo("pool", xc, xc, xctmp, ALU.mult, ["xc", "xctmp"], ["xc"])
                CP("dve", xc16, xc, ["xc"], ["xc16c"])
                for gi, (tok0, n) in enumerate(tokgroups):
                    MM(bank(2, n), wqk16[:, 0, h, :], xc16[:, tok0:tok0 + n], True, True, ["xc16c", "wqk16"], PK(2))
                    MM(bank(3, n), wqk16[:, 1, h, :], xc16[:, tok0:tok0 + n], True, True, ["xc16c", "wqk16"], PK(3))
                    CP("act", qmT[:, tok0:tok0 + n], bank(2, n), PK(2), ["qmT"])
                    ACT(kmT[:, tok0:tok0 + n], bank(3, n), AF.Copy, PK(3), ["kmT"], scale=128.0 ** -0.5)
                for tt in range(NT):
                    pb = 4 + (tt % 2)
                    MM(bank(pb, 128), xc16[:, tt * 128:(tt + 1) * 128], wqk16[:, 1, h, :], True, True, ["xc16c", "wqk16"], PK(pb))
                    for kc in range(8):
                        MM(bank(pb, 128, 128), hT[:, kc, tt * 128:(tt + 1) * 128], ws[1][:, kc, :], kc == 0, kc == 7, ["hT", wkeys[1]], PK(pb))
                    ACT(kw[:, 0, tt, :], bank(pb, 128), AF.Copy, PK(pb) + ["wkcol"], ["kw"], scale=wkcol[:, 0, tt, h:h + 1])
                    ACT(kw[:, 1, tt, :], bank(pb, 128), AF.Copy, PK(pb) + ["wkcol"], ["kw"], scale=wkcol[:, 1, tt, h:h + 1])
                    CP("dve", vm[:, tt, 0:128], bank(pb, 128, 128), PK(pb), ["vm"])
                for d in range(2):
                    order = list(range(NT)) if d == 0 else [1, 0] + list(range(NT - 1, 1, -1))
                    msk = mask_f if d == 0 else mask_b
                    MS("dve", P32, 0.0, ["P32"])
                    for ci, tt in enumerate(order):
                        ccol = cbc[:, d, tt, h:h + 1]
                        last = ci == NT - 1
                        lat = tt >= 2
                        lt_ = tt - 2
                        if lat:
                            TS("pool", C16, P32, ccol, None, ALU.mult, None, ["P32", "cbc"], ["C16"])
                            MM(bank(6, 128), kmT[:, tt * 128:(tt + 1) * 128], qmT[:, tt * 128:(tt + 1) * 128], True, True, ["kmT", "qmT"], PK(6))
                            STT("dve", sT16, bank(6, 128), wcol[:, d, tt, h:h + 1], msk, ALU.mult, ALU.mult, PK(6) + ["wcol", "cst"], ["sT16"])
                            MM(bank(7, 130), qmT[:, tt * 128:(tt + 1) * 128], C16, True, False, ["qmT", "C16"], PK(7))
                            MM(bank(7, 130), sT16, vm[:, tt, :], False, True, ["sT16", "vm"], PK(7))
                            ACT(smallc[:, 8:9], bank(7, 1, 128), AF.Abs, PK(7), ["dnC"])
                            TTo("dve", smallc[:, 8:9], smallc[:, 8:9], fcol[:, d, tt, h:h + 1], ALU.max, ["dnC", "fcol"], ["dnC"])
                            S.op("dve", lambda e: e.reciprocal(out=smallc[:, 9:10], in_=smallc[:, 8:9]), ["dnC"], ["rdC"])
                            if d == 0:
                                ACT(hf[:, lt_, :], bank(7, 128), AF.Copy, PK(7) + ["rdC"], ["hf"], scale=smallc[:, 9:10])
                            else:
                                STT("dve", hs, bank(7, 128), smallc[:, 9:10], hf[:, lt_, :], ALU.mult, ALU.add, PK(7) + ["rdC", "hf"], ["hs"])
                        if not last:
                            pb = 4 + (ci % 2)
                            MM(bank(pb, 130), kw[:, d, tt, :], vm[:, tt, :], True, True, ["kw", "vm"], PK(pb))
                            STT("dve", P32, P32, ccol, bank(pb, 130), ALU.mult, ALU.add, ["P32", "cbc"] + PK(pb), ["P32"])
                        if lat and d == 1:
                            if lt_ % 4 == 3:
                                g0 = CT + (lt_ - 3) * 128
                                for kc in range(8):
                                    MM(bank(2), ws[3][:, kc, :], hT[:, kc, g0:g0 + 512], kc == 0, kc == 7, ["hT", wkeys[3]], PK(2))
                                sigmoid_act(sigo, bank(2), sigo, PK(2), ["sigo"], "sigo")
                                for kc in range(8):
                                    MM(bank(3), ws[2][:, kc, :], hT[:, kc, g0:g0 + 512], kc == 0, kc == 7, ["hT", wkeys[2]], PK(3))
                                sigmoid_act(gtmp, bank(3), gtmp, PK(3), ["gtmp"], "gtmp")
                                TTo("dve", silz, bank(3), gtmp, ALU.mult, PK(3) + ["gtmp"], ["silz"])
                            gofs = (lt_ % 4) * 128
                            ACT(hn, hs, AF.Square, ["hs"], ["hn", "ssC"], accum=smallc[:, 10:11])
                            rstd_act(smallc[:, 11:12], smallc[:, 10:11], 128, ["ssC"], ["rsC"])
                            TS("dve", hn, hs, smallc[:, 11:12], None, ALU.mult, None, ["hs", "rsC"], ["hn"])
                            TR(bank(1, 128), hn, ident, ["hn", "cst"], PK(1))
                            STT("dve", yt1, bank(1, 128), pcol[:, 24 + h:25 + h], sigo[:, gofs:gofs + 128], ALU.mult, ALU.mult,
                                PK(1) + ["pcol", "sigo"], ["yt1"])
                            STT("dve", yt2, xc[:, tt * 128:(tt + 1) * 128], pcol[:, 28 + h:29 + h], yt1, ALU.mult, ALU.add,
                                ["xc", "pcol", "yt1"], ["yt2"])
                            TTo("pool", yT[:, 1, h, lt_ * 128:(lt_ + 1) * 128], yt2, silz[:, gofs:gofs + 128], ALU.mult, ["yt2", "silz"], ["yT"])
            if debug and b == 0:
                out_ops.append(DMA("pool", dbg["ybT"], yT[:, 1].rearrange("p h t -> p (h t)"), "dbgC", ["yT"], []))

            PD = Carve()
            wga = PD.take("wga", 8 * 1024, BF16).rearrange("p (k c) -> p k c", c=1024)
            wgb = PD.take("wgb", 8 * 1024, BF16).rearrange("p (k c) -> p k c", c=1024)
            woa = PD.take("woa", 4 * 1024, BF16).rearrange("p (k c) -> p k c", c=1024)
            wob = PD.take("wob", 4 * 1024, BF16).rearrange("p (k c) -> p k c", c=1024)
            wo = PD.take("wo", 8 * 1024, BF16).rearrange("p (k c) -> p k c", c=1024)
            mT = PD.take("mT", 8 * 512, BF16).rearrange("p (k t) -> p k t", t=512)
            sga = PD.take("sga", 512, F32)
            sgb = PD.take("sgb", 512, F32)
            ma = PD.take("ma", 512, F32)
            mb_ = PD.take("mb", 512, F32)
            xo = [PD.take("xo%d" % i, D, F32) for i in range(2)]
            yg = PD.take("yg", D, F32)
            gtb = PD.take("gtb", D, F32)
            fence(PD.keys)
            for hf_ in range(2):
                MM(bank(4 + hf_), cst[0:3, 1024 + b * 128:1024 + (b + 1) * 128], gtrow[:, hf_ * 512:(hf_ + 1) * 512],
                   True, True, ["cst", "gtrow"], PK(4 + hf_))
                CP("dve", gtb[:, hf_ * 512:(hf_ + 1) * 512], bank(4 + hf_), PK(4 + hf_), ["gtb"])
            DMA("pool", wga, win_d[:, C_GA:C_GA + 1024].rearrange("(kc p) c -> p kc c", p=128), "wd0", [], ["wga"])
            DMA("pool", wgb, win_d[:, C_GB:C_GB + 1024].rearrange("(kc p) c -> p kc c", p=128), "wd1", [], ["wgb"])
            DMA("pool", woa, woa_d.rearrange("(kc p) c -> p kc c", p=128), "wd2", [], ["woa"])
            DMA("pool", wob, wob_d.rearrange("(kc p) c -> p kc c", p=128), "wd3", [], ["wob"])
            DMA("pool", wo, wo_d.rearrange("(kc p) c -> p kc c", p=128), "wd4", [], ["wo"])
            oi = 0
            for g in range(4):
                t0 = 512 * g
                for j in range(8):
                    cs_ = slice(j * 128, (j + 1) * 128)
                    for kc in range(4):
                        MM(bank(0), woa[:, kc, cs_], yT[:, 0, kc, t0:t0 + 512], kc == 0, kc == 3, ["woa", "yT"], PK(0))
                    for kc in range(4):
                        MM(bank(1), wob[:, kc, cs_], yT[:, 1, kc, t0:t0 + 512], kc == 0, kc == 3, ["wob", "yT"], PK(1))
                    for kc in range(8):
                        MM(bank(2), wga[:, kc, cs_], hT[:, kc, CT + t0:CT + t0 + 512], kc == 0, kc == 7, ["wga", "hT"], PK(2))
                    for kc in range(8):
                        MM(bank(3), wgb[:, kc, cs_], hT[:, kc, CT + t0:CT + t0 + 512], kc == 0, kc == 7, ["wgb", "hT"], PK(3))
                    sigmoid_act(sga, bank(2), sga, PK(2), ["sga"], "sga")
                    sigmoid_act(sgb, bank(3), sgb, PK(3), ["sgb"], "sgb")
                    TTo("dve", ma, bank(0), sga, ALU.mult, PK(0) + ["sga"], ["ma"])
                    TTo("dve", mb_, bank(1), sgb, ALU.mult, PK(1) + ["sgb"], ["mb"])
                    TTo("pool", mT[:, j, :], ma, mb_, ALU.add, ["ma", "mb"], ["mT"])
                for qt in range(4):
                    tok = t0 + qt * 128
                    sl = oi % 2
                    oi += 1
                    xk = "xo%d" % sl
                    DMA("sp", xo[sl], x_d[b, tok:tok + 128, :], "xin%d" % sl, [], [xk])
                    for hf_ in range(2):
                        pb = 4 + hf_
                        for j in range(8):
                            MM(bank(pb), mT[:, j, qt * 128:(qt + 1) * 128], wo[:, j, hf_ * 512:(hf_ + 1) * 512], j == 0, j == 7, ["mT", "wo"], PK(pb))
                        TTo("dve", yg[:, hf_ * 512:(hf_ + 1) * 512], bank(pb), gtb[:, hf_ * 512:(hf_ + 1) * 512], ALU.mult,
                            PK(pb) + ["gtb"], ["yg"])
                    TTo("pool", xo[sl], xo[sl], yg, ALU.add, [xk, "yg"], [xk])
                    out_ops.append(DMA("sp", out_d[b, tok:tok + 128, :], xo[sl], "xout%d" % sl, [xk], []))

        last = {}
        for o in out_ops:
            last[o.dma] = o
        S.emit({"sp": [o for o in last.values()]})
    return nc


def _consts():
    c = np.zeros((128, 1408), np.float32)
    i = np.arange(128)
    c[:, 0:128] = np.eye(128)
    c[:, 128:256] = (i[:, None] <= i[None, :])
    c[:, 256:384] = (i[:, None] >= i[None, :])
    c[:, 384:512] = (i[:, None] > i[None, :])
    c[:, 512:640] = (i[:, None] < i[None, :])
    c[:, 640:768] = 1.0
    c[:, 768:896] = (i[:, None] // 64 == i[None, :] // 64) / 64.0
    p = np.where((i % 32) < 16, i + 16, i - 16)
    perm = np.zeros((128, 128), np.float32)
    perm[p, i] = 1.0
    c[:, 896:1024] = perm
    for b in range(3):
        c[b, 1024 + b * 128:1024 + (b + 1) * 128] = 1.0
    t = np.arange(T)
    row = (t // 64).astype(np.float32)
    col = (t % 64).astype(np.float32)
    inv = (10000.0 ** (-np.arange(16, dtype=np.float32) / 16)).astype(np.float32)
    rope = np.zeros((128, 2, T), np.float32)
    for m in range(128):
        d = m % 64
        pos = row if d < 32 else col
        ang = (pos * inv[d % 16]).astype(np.float32)
        rope[m, 0] = np.cos(ang)
        sn = np.sin(ang)
        rope[m, 1] = -sn if (d % 32) < 16 else sn
    return c, rope


_NC_CACHE = {}


def kernel(x, c, ctx, c_ctx, norm_w, w_mod, b_mod, w_in, b_if, da_q_norm, da_k_norm,
           da_lambda_q1, da_lambda_k1, da_lambda_q2, da_lambda_k2, da_head_norm, w_out_a,
           ml_conv_w, ml_conv_b, ml_wq, ml_wk, ml_head_norm, ml_skip, w_out_b, w_o, _debug=False):
    f = lambda a: np.ascontiguousarray(np.asarray(a, dtype=np.float32))
    x, c, ctx, c_ctx = f(x), f(c), f(ctx), f(c_ctx)
    params = np.zeros((64, 128), np.float32)
    params[0:8] = f(norm_w).reshape(8, 128)
    params[8:20] = f(ml_conv_w).reshape(12, 128)
    params[20:24] = f(ml_conv_b).reshape(4, 128)
    params[24:28] = f(ml_head_norm).reshape(4, 128)
    params[28:32] = f(ml_skip).reshape(4, 128)
    params[32] = f(da_head_norm).reshape(128)
    params[33] = np.concatenate([f(da_q_norm).reshape(64)] * 2)
    params[34] = np.concatenate([f(da_k_norm).reshape(64)] * 2)
    lamv = np.concatenate([f(da_lambda_q1).reshape(64), f(da_lambda_k1).reshape(64),
                           f(da_lambda_q2).reshape(64), f(da_lambda_k2).reshape(64)])[None, :]
    consts, rope = _consts()
    shared = {
        "w_mod": f(w_mod).reshape(D, 3 * D), "w_in": f(w_in).reshape(D, INW), "params": params,
        "b_mod": f(b_mod).reshape(1, 3 * D), "lamv": np.ascontiguousarray(lamv), "b_if": f(b_if).reshape(1, 16),
        "ml_wq": f(ml_wq).reshape(4, 128, 128), "ml_wk": f(ml_wk).reshape(4, 128, 128),
        "w_out_a": f(w_out_a).reshape(512, D), "w_out_b": f(w_out_b).reshape(512, D), "w_o": f(w_o).reshape(D, D),
        "consts": consts, "rope": rope,
    }
    n = 8
    in_maps = []
    for i in range(n):
        m = dict(shared)
        m["x"] = np.ascontiguousarray(x[NB * i:NB * (i + 1)])
        m["ctx"] = np.ascontiguousarray(ctx[NB * i:NB * (i + 1)])
        m["cvec"] = np.ascontiguousarray(np.stack([c[NB * i], c[NB * i + 1], c_ctx]))
        in_maps.append(m)
    key = bool(_debug)
    if key not in _NC_CACHE:
        _NC_CACHE[key] = build_nc(debug=key)
    nc = _NC_CACHE[key]
    res = run_bass_kernel_spmd(nc, in_maps, core_ids=list(range(n)))
    out = np.concatenate([r["out"] for r in res.results], axis=0)
    if _debug:
        return out, res.results
    return out
```

```python
import contextlib
import numpy as np
import concourse.bass as bass
import concourse.mybir as mybir
from concourse.bass_utils import run_bass_kernel_spmd

F32 = mybir.dt.float32
BF16 = mybir.dt.bfloat16
AF = mybir.ActivationFunctionType
ALU = mybir.AluOpType
AX = mybir.AxisListType

COMPUTE = ("pe", "act", "dve", "pool")
ENGS = ("pe", "act", "dve", "pool", "sp")

D = 1024
NB = 2
T = 2048
CT = 256
TT = T + CT
NT = TT // 128
EPS = 1e-6
INW = 6160
C_Q, C_K, C_V, C_ZA, C_XM, C_VM, C_ZB, C_OB, C_G, C_GA, C_GB = 0, 512, 1024, 1536, 2048, 2560, 3072, 3584, 4096, 4112, 5136
LAM_INIT = 0.2


class Op:
    __slots__ = ("eng", "fn", "deps", "signal", "dma", "cnt")

    def __init__(self, eng, fn, dma):
        self.eng = eng
        self.fn = fn
        self.deps = []
        self.signal = False
        self.dma = dma
        self.cnt = 0


class Sched:
    def __init__(self, nc):
        self.nc = nc
        self.ops = {e: [] for e in ENGS}
        self.last_w = {}
        self.readers = {}
        self.dma_streams = {}

    def op(self, eng, fn, reads=(), writes=(), dma=None):
        o = Op(eng, fn, dma)
        deps = []
        is_async = dma is not None
        for r in reads:
            w = self.last_w.get(r)
            if w is not None:
                deps.append(w)
        for r in writes:
            w = self.last_w.get(r)
            if w is not None and (is_async or w.dma is not None or w.eng != eng):
                deps.append(w)
            for rd in self.readers.get(r, ()):
                if is_async or rd.dma is not None or rd.eng != eng:
                    deps.append(rd)
        if dma is not None:
            lst = self.dma_streams.setdefault(dma, [])
            if lst:
                deps.append(lst[-1])
            lst.append(o)
        seen = set()
        for d in deps:
            if id(d) not in seen:
                seen.add(id(d))
                o.deps.append(d)
                d.signal = True
        for r in reads:
            self.readers.setdefault(r, []).append(o)
        for r in writes:
            self.last_w[r] = o
            self.readers[r] = []
        self.ops[eng].append(o)
        return o

    def emit(self, final_waits):
        nc = self.nc
        with contextlib.ExitStack() as es:
            sems = {e: es.enter_context(nc.semaphore("s_" + e)) for e in COMPUTE}
            dsems = {k: es.enter_context(nc.semaphore("d_" + k)) for k in self.dma_streams}
            for e in ENGS:
                c = 0
                for o in self.ops[e]:
                    if o.dma is None and o.signal:
                        c += 1
                    o.cnt = c
            for k, lst in self.dma_streams.items():
                for i, o in enumerate(lst):
                    o.cnt = 16 * (i + 1)
            block = es.enter_context(nc.Block())

            def run(engname, eng):
                waited = {}
                for o in self.ops[engname]:
                    for d in o.deps:
                        if d.dma is not None:
                            sem, key = dsems[d.dma], ("d", d.dma)
                        else:
                            sem, key = sems[d.eng], ("e", d.eng)
                        if waited.get(key, 0) >= d.cnt:
                            continue
                        waited[key] = d.cnt
                        eng.wait_ge(sem, d.cnt)
                    inst = o.fn(eng)
                    if o.dma is not None:
                        inst.then_inc(dsems[o.dma], 16)
                    elif o.signal:
                        inst.then_inc(sems[engname], 1)
                for d in final_waits.get(engname, ()):
                    eng.wait_ge(dsems[d.dma], d.cnt)

            @block.tensor
            def _(eng):
                run("pe", eng)

            @block.scalar
            def _(eng):
                run("act", eng)

            @block.vector
            def _(eng):
                run("dve", eng)

            @block.gpsimd
            def _(eng):
                run("pool", eng)

            @block.sync
            def _(eng):
                run("sp", eng)


def build_nc(debug=False):
    nc = bass.Bass("TRN2", target_bir_lowering=False)

    def dram(name, shape, kind="ExternalInput"):
        return nc.dram_tensor(name, list(shape), F32, kind=kind).ap()

    x_d = dram("x", [NB, T, D])
    ctx_d = dram("ctx", [NB, CT, D])
    cvec_d = dram("cvec", [3, D])
    wmod_d = dram("w_mod", [D, 3 * D])
    win_d = dram("w_in", [D, INW])
    prm_d = dram("params", [64, 128])
    bmod_d = dram("b_mod", [1, 3 * D])
    lamv_d = dram("lamv", [1, 256])
    bif_d = dram("b_if", [1, 16])
    wq_d = dram("ml_wq", [4, 128, 128])
    wk_d = dram("ml_wk", [4, 128, 128])
    woa_d = dram("w_out_a", [512, D])
    wob_d = dram("w_out_b", [512, D])
    wo_d = dram("w_o", [D, D])
    cst_d = dram("consts", [128, 1408])
    rope_d = dram("rope", [128, 2, T])
    out_d = dram("out", [NB, T, D], kind="ExternalOutput")
    dbg = {}
    if debug:
        dbg["hT"] = dram("dbg_hT", [128, 8 * TT], kind="ExternalOutput")
        dbg["yaT"] = dram("dbg_yaT", [128, 4 * T], kind="ExternalOutput")
        dbg["ybT"] = dram("dbg_ybT", [128, 4 * T], kind="ExternalOutput")

    es = contextlib.ExitStack()
    with es:
        def sb(name, shape, dt):
            return es.enter_context(nc.sbuf_tensor("sb_" + name, list(shape), dt))

        PS = es.enter_context(nc.psum_tensor("PS", [128, 4096], F32))

        def bank(i, n=512, off=0):
            return PS[:, i * 512 + off:i * 512 + off + n]

        def PK(*banks):
            return [("ps", b) for b in banks]

        S = Sched(nc)

        def MM(out, lhsT, rhs, start, stop, reads, writes):
            S.op("pe", lambda e: e.matmul(out, lhsT, rhs, start=start, stop=stop), reads, writes)

        def TR(out, in_, ident, reads, writes):
            S.op("pe", lambda e: e.transpose(out, in_, ident), reads, writes)

        def ACT(out, in_, func, reads, writes, scale=1.0, bias=None, accum=None):
            def f(e):
                kw = {}
                if bias is not None:
                    kw["bias"] = bias
                if accum is not None:
                    kw["accum_out"] = accum
                return e.activation(out=out, in_=in_, func=func, scale=scale, **kw)
            S.op("act", f, reads, writes)

        def TS(eng, out, in0, s1, s2, op0, op1, reads, writes):
            if s2 is None:
                S.op(eng, lambda e: e.tensor_scalar(out=out, in0=in0, scalar1=s1, scalar2=None, op0=op0), reads, writes)
            else:
                S.op(eng, lambda e: e.tensor_scalar(out=out, in0=in0, scalar1=s1, scalar2=s2, op0=op0, op1=op1), reads, writes)

        def TTo(eng, out, in0, in1, op, reads, writes):
            S.op(eng, lambda e: e.tensor_tensor(out=out, in0=in0, in1=in1, op=op), reads, writes)

        def STT(eng, out, in0, scalar, in1, op0, op1, reads, writes):
            S.op(eng, lambda e: e.scalar_tensor_tensor(out=out, in0=in0, scalar=scalar, in1=in1, op0=op0, op1=op1), reads, writes)

        def CP(eng, out, in_, reads, writes):
            if eng == "act":
                return ACT(out, in_, AF.Copy, reads, writes)
            S.op(eng, lambda e: e.tensor_copy(out=out, in_=in_), reads, writes)

        def MS(eng, ap, val, writes):
            S.op(eng, lambda e: e.memset(ap, val), (), writes)

        def DMA(eng, out, in_, stream, reads, writes):
            return S.op(eng, lambda e: e.dma_start(out=out, in_=in_), reads, writes, dma=stream)

        def sigmoid_act(out, in_, tmp, reads, writes, tkey):
            ACT(tmp, in_, AF.Exp, reads, [tkey], scale=-1.0)
            ACT(tmp, tmp, AF.Ln, [tkey], [tkey], bias=1.0)
            ACT(out, tmp, AF.Exp, [tkey], writes, scale=-1.0)

        def rstd_act(out, in_, n, reads, writes):
            ACT(out, in_, AF.Ln, reads, writes, scale=1.0 / n, bias=EPS)
            ACT(out, out, AF.Exp, writes, writes, scale=-0.5)

        cst = sb("cst", [128, 1408], F32)
        ident = cst[:, 0:128]
        mask_f = cst[:, 128:256]
        mask_b = cst[:, 256:384]
        tri_f = cst[:, 384:512]
        tri_b = cst[:, 512:640]
        ones = cst[:, 640:768]
        c16 = sb("c16", [128, 384], BF16)
        ident16 = c16[:, 0:128]
        blk16 = c16[:, 128:256]
        perm16 = c16[:, 256:384]
        rope = sb("rope", [128, 2, T], F32)
        pcol = sb("pcol", [128, 64], F32)
        gcol = sb("gcol", [128, 4], F32)
        gtrow = sb("gtrow", [3, D], F32)
        modT = sb("modT", [128, 16, 3], F32)
        svec = sb("svec", [128, 8, 3], F32)
        bifbc = sb("bifbc", [128, 16], F32)
        wqk16 = sb("wqk16", [128, 2, 4, 128], BF16)
        wg16 = sb("wg16", [128, 8, 16], BF16)
        hT = sb("hT", [128, 8, TT], BF16)
        yT = sb("yT", [128, 2, 4, T], BF16)
        smallc = sb("smallc", [128, 64], F32)
        ARENA_N = 54600
        arena = sb("arena", [128, ARENA_N], BF16)

        class Carve:
            def __init__(self):
                self.off = 0
                self.keys = []

            def take(self, key, nelem, dt):
                if dt == F32:
                    if self.off % 2:
                        self.off += 1
                    ap = arena[:, self.off:self.off + 2 * nelem].bitcast(F32)
                    self.off += 2 * nelem
                else:
                    ap = arena[:, self.off:self.off + nelem]
                    self.off += nelem
                    if self.off % 2:
                        self.off += 1
                assert self.off <= ARENA_N, (key, self.off)
                self.keys.append(key)
                return ap

        prev_keys = []

        def fence(new_keys):
            nonlocal prev_keys
            allk = list(dict.fromkeys(prev_keys + new_keys))
            MS("pool", smallc[:, 63:64], 0.0, allk + ["fence_dummy"])
            prev_keys = list(new_keys)

        DMA("sp", cst[:], cst_d, "c0", [], ["cst"])
        DMA("sp", rope[:], rope_d, "c1", [], ["rope"])
        CP("dve", c16[:, 0:128], cst[:, 0:128], ["cst"], ["c16"])
        CP("dve", c16[:, 128:384], cst[:, 768:1024], ["cst"], ["c16"])
        DMA("sp", bifbc[:], bif_d.partition_broadcast(128), "c2", [], ["bifbc"])
        DMA("pool", wqk16[:, 0], wq_d.rearrange("h d e -> d h e"), "c3", [], ["wqk16"])
        DMA("pool", wqk16[:, 1], wk_d.rearrange("h d e -> d h e"), "c4", [], ["wqk16"])
        DMA("pool", wg16[:], win_d[:, C_G:C_G + 16].rearrange("(kc p) c -> p kc c", p=128), "c5", [], ["wg16"])

        P0 = Carve()
        prm = P0.take("prm", 128, F32)[0:64, :]
        cs = P0.take("cs", D, F32)[0:3, :]
        cs2 = P0.take("cs2", D, F32)[0:3, :]
        scT = P0.take("scT", 24, F32).rearrange("p (k b) -> p k b", b=3)
        brow = P0.take("brow", 3 * D, F32)[0:3, :]
        modrow = P0.take("modrow", 3 * D, F32)[0:3, :]
        lv = P0.take("lv", 256, F32)[0:1, :]
        lt = P0.take("lt", 128, F32)[0:1, :]
        ls = P0.take("ls", 8, F32)[0:1, :]
        wms = [P0.take("wm%d" % i, 8 * 512, F32).rearrange("p (k c) -> p k c", c=512) for i in range(2)]
        fence(P0.keys)
        DMA("sp", prm, prm_d, "c6", [], ["prm"])
        DMA("sp", cs, cvec_d, "c7", [], ["cs"])
        DMA("sp", brow, bmod_d.partition_broadcast(3), "c8", [], ["brow"])
        DMA("sp", lv, lamv_d, "c9", [], ["lv"])
        TR(bank(0, 64), prm, ident[0:64, 0:64], ["prm", "cst"], PK(0))
        CP("dve", pcol[:], bank(0, 64), PK(0), ["pcol"])
        TS("dve", gcol[:, 0:1], pcol[:, 32:33], 1.0 - LAM_INIT, None, ALU.mult, None, ["pcol"], ["gcol"])
        TS("dve", gcol[:, 1:2], pcol[:, 33:34], 0.125, None, ALU.mult, None, ["pcol"], ["gcol"])
        CP("dve", gcol[:, 2:3], pcol[:, 34:35], ["pcol"], ["gcol"])
        lv4 = lv.rearrange("p (a b c) -> p a b c", a=2, b=2)
        TTo("dve", lt.rearrange("p (a c) -> p a c", a=2), lv4[:, :, 0, :], lv4[:, :, 1, :], ALU.mult, ["lv"], ["lt"])
        S.op("dve", lambda e: e.reduce_sum(out=ls[:, 0:2], in_=lt.rearrange("p (a c) -> p a c", a=2), axis=AX.X), ["lt"], ["ls"])
        ACT(ls[:, 2:4], ls[:, 0:2], AF.Exp, ["ls"], ["ls2"])
        TTo("dve", ls[:, 4:5], ls[:, 3:4], ls[:, 2:3], ALU.subtract, ["ls2"], ["ls3"])
        TS("dve", ls[:, 5:6], ls[:, 4:5], -LAM_INIT, None, ALU.add, None, ["ls3"], ["ls4"])
        MM(bank(0, 1, 64), ones[0:1, 0:128], ls[:, 5:6], True, True, ["ls4", "cst"], PK(0))
        CP("dve", gcol[:, 3:4], bank(0, 1, 64), PK(0), ["gcol"])
        sigmoid_act(cs2, cs, cs2, ["cs"], ["cs2"], "cs2")
        TTo("dve", cs2, cs2, cs, ALU.mult, ["cs2", "cs"], ["cs2"])
        for kc in range(8):
            TR(bank(1, 3, 4 * kc), cs2[:, kc * 128:(kc + 1) * 128], ident[0:3, 0:3], ["cs2", "cst"], PK(1))
        CP("dve", scT, bank(1, 32).rearrange("p (k b) -> p k b", b=4)[:, :, 0:3], PK(1), ["scT"])
        for cg in range(6):
            wm = wms[cg % 2]
            wk_ = "wm%d" % (cg % 2)
            DMA("sp", wm, wmod_d[:, cg * 512:(cg + 1) * 512].rearrange("(kc p) c -> p kc c", p=128), wk_, [], [wk_])
            pb = 2 + (cg % 2)
            for kc in range(8):
                MM(bank(pb)[0:3, :], scT[:, kc, :], wm[:, kc, :], kc == 0, kc == 7, ["scT", wk_], PK(pb))
            TTo("dve", modrow[:, cg * 512:(cg + 1) * 512], bank(pb)[0:3, :], brow[:, cg * 512:(cg + 1) * 512], ALU.add,
                PK(pb) + ["brow"], ["modrow"])
        for j in range(16):
            TR(bank(1, 3, 4 * j), modrow[:, j * 128:(j + 1) * 128], ident[0:3, 0:3], ["modrow", "cst"], PK(1))
        CP("dve", modT[:], bank(1, 64).rearrange("p (k b) -> p k b", b=4)[:, :, 0:3], PK(1), ["modT"])
        TS("dve", svec[:], modT[:, 8:16, :], 1.0, None, ALU.add, None, ["modT"], ["svec"])
        TTo("dve", svec[:], svec[:], pcol[:, 0:8].unsqueeze(2).to_broadcast([128, 8, 3]), ALU.mult, ["svec", "pcol"], ["svec"])
        CP("dve", gtrow[:], modrow[:, 2048:3072], ["modrow"], ["gtrow"])

        out_ops = []

        for b in range(NB):
            PA = Carve()
            xts = [PA.take("xt%d" % i, D, F32) for i in range(2)]
            sqj = PA.take("sqj", D, BF16)
            xns = [PA.take("xn%d" % i, D, BF16) for i in range(2)]
            fence(PA.keys)
            groups = [(0, 2)] + [(2 + 4 * g, 4) for g in range(4)]
            psb_all = PS[:, 0:2048].bitcast(BF16).rearrange("p (k t) -> p k t", t=512)
            ti = 0
            for (t0, ntile) in groups:
                mi = 2 if t0 == 0 else b
                for j in range(ntile):
                    tt = t0 + j
                    sl = ti % 2
                    ti += 1
                    src = ctx_d[b, tt * 128:(tt + 1) * 128, :] if tt < 2 else x_d[b, (tt - 2) * 128:(tt - 1) * 128, :]
                    DMA("sp", xts[sl], src, "xt%d" % sl, [], ["xt%d" % sl])
                    ACT(sqj, xts[sl], AF.Square, ["xt%d" % sl], ["sqj", "ssA"], accum=smallc[:, 0:1])
                    rstd_act(smallc[:, 1:2], smallc[:, 0:1], D, ["ssA"], ["rsA"])
                    TS("pool" if sl else "dve", xns[sl], xts[sl], smallc[:, 1:2], None, ALU.mult, None, ["xt%d" % sl, "rsA"], ["xn%d" % sl])
                    for kc in range(8):
                        TR(psb_all[:, kc, j * 128:(j + 1) * 128], xns[sl][:, kc * 128:(kc + 1) * 128], ident16,
                           ["xn%d" % sl, "c16"], PK(kc // 2))
                n = ntile * 128
                for kc in range(8):
                    o_ = hT[:, kc, t0 * 128:t0 * 128 + n]
                    i_ = psb_all[:, kc, 0:n]
                    if kc % 2:
                        ACT(o_, i_, AF.Identity, PK(kc // 2) + ["svec", "modT"], ["hT"], scale=svec[:, kc, mi:mi + 1], bias=modT[:, kc, mi:mi + 1])
                    else:
                        TS("dve", o_, i_, svec[:, kc, mi:mi + 1], modT[:, kc, mi:mi + 1], ALU.mult, ALU.add, PK(kc // 2) + ["svec", "modT"], ["hT"])
            if debug and b == 0:
                out_ops.append(DMA("pool", dbg["hT"], hT[:].rearrange("p k t -> p (k t)"), "dbgA", ["hT"], []))

            PB = Carve()
            wh = [[PB.take("wh%d_%d" % (i, j), 8 * 128, BF16).rearrange("p (k c) -> p k c", c=128) for j in range(3)] for i in range(2)]
            vaug = PB.take("vaug", NT * 4 * 130, BF16).rearrange("p (t h c) -> p t h c", h=4, c=130)
            kTs = [PB.take("kT%d" % i, TT, BF16) for i in range(2)]
            qTs = [PB.take("qT%d" % i, 512, BF16) for i in range(2)]
            E = [PB.take("E%d" % m, NT * 512, BF16).rearrange("p (t q) -> p t q", q=512) for m in range(2)]
            wv16 = E[1].rearrange("p t q -> p (t q)")[:, 0:8 * 512].rearrange("p (k c) -> p k c", c=512)
            x2 = PB.take("x2", 512, BF16)
            rsb = PB.take("rsb", 512, F32)
            xn = PB.take("xnq", 512, F32)
            xn16 = PB.take("xn16", 512, BF16)
            t1 = PB.take("t1", 512, F32)
            t2 = PB.take("t2", 512, F32)
            zss = [PB.take("zs%d" % i, 512, F32) for i in range(3)]
            ztmp = PB.take("ztmp", 512, F32)
            Osb = [PB.take("Osb%d" % i, 4 * 2 * 130, F32).rearrange("p (q m c) -> p q m c", m=2, c=130) for i in range(2)]
            otmp = PB.take("otmp", 128, F32)
            ofin = PB.take("ofin", 4 * 128, F32).rearrange("p (q c) -> p q c", c=128)
            onb = PB.take("onb", 128, F32)
            fence(PB.keys)
            DMA("pool", wv16, win_d[:, C_V:C_V + 512].rearrange("(kc p) c -> p kc c", p=128), "wv", [], ["E1"])
            MS("dve", vaug[:, :, :, 128:130], 1.0, ["vaug"])

            def load_wh(h):
                for j, c0 in enumerate((C_Q, C_K, C_ZA)):
                    k_ = "wh%d_%d" % (h % 2, j)
                    DMA("pool", wh[h % 2][j], win_d[:, c0 + h * 128:c0 + (h + 1) * 128].rearrange("(kc p) c -> p kc c", p=128), k_, [], [k_])

            load_wh(0)
            for tt in range(NT):
                pb = tt % 2
                for kc in range(8):
                    MM(bank(pb), hT[:, kc, tt * 128:(tt + 1) * 128], wv16[:, kc, :], kc == 0, kc == 7, ["hT", "E1"], PK(pb))
                CP("dve", vaug[:, tt, :, 0:128], bank(pb).rearrange("p (h c) -> p h c", c=128), PK(pb), ["vaug"])

            def qk_chain(wsel, wkey, tok0, n, gc, rope_off, out16, okey):
                for kc in range(8):
                    MM(bank(0, n), wsel[:, kc, :], hT[:, kc, tok0:tok0 + n], kc == 0, kc == 7, ["hT", wkey], PK(0))
                yield
                ACT(x2[:, 0:n], bank(0, n), AF.Square, PK(0), ["x2"])
                yield
                MM(bank(1, n), blk16, x2[:, 0:n], True, True, ["x2", "c16"], PK(1))
                yield
                ACT(rsb[:, 0:n], bank(1, n), AF.Ln, PK(1), ["rsb"], bias=EPS)
                ACT(rsb[:, 0:n], rsb[:, 0:n], AF.Exp, ["rsb"], ["rsb"], scale=-0.5)
                yield
                if rope_off is None:
                    STT("dve", out16, bank(0, n), gc, rsb[:, 0:n], ALU.mult, ALU.mult, PK(0) + ["rsb", "gcol"], [okey])
                    return
                STT("dve", xn[:, 0:n], bank(0, n), gc, rsb[:, 0:n], ALU.mult, ALU.mult, PK(0) + ["rsb", "gcol"], ["xnq"])
                yield
                CP("pool", xn16[:, 0:n], xn[:, 0:n], ["xnq"], ["xn16"])
                TTo("pool", t1[:, 0:n], xn[:, 0:n], rope[:, 0, rope_off:rope_off + n], ALU.mult, ["xnq", "rope"], ["t1"])
                yield
                MM(bank(1, n), perm16, xn16[:, 0:n], True, True, ["xn16", "c16"], PK(1))
                yield
                TTo("dve", t2[:, 0:n], bank(1, n), rope[:, 1, rope_off:rope_off + n], ALU.mult, PK(1) + ["rope"], ["t2"])
                yield
                TTo("pool", out16, t1[:, 0:n], t2[:, 0:n], ALU.add, ["t1", "t2"], [okey])

            def k_groups(h):
                ws = wh[h % 2]
                return [qk_chain(ws[1], "wh%d_1" % (h % 2), tok0, n, gcol[:, 2:3], roff, kTs[h % 2][:, tok0:tok0 + n], "kT%d" % (h % 2))
                        for (tok0, n, roff) in [(0, 256, None)] + [(256 + 512 * g, 512, 512 * g) for g in range(4)]]

            def q_chain(h, qg):
                ws = wh[h % 2]
                par = (4 * h + qg) % 2
                zp = (4 * h + qg) % 3
                yield from qk_chain(ws[0], "wh%d_0" % (h % 2), 256 + 512 * qg, 512, gcol[:, 1:2], 512 * qg, qTs[par], "qT%d" % par)
                yield
                q0 = 256 + 512 * qg
                for kc in range(8):
                    MM(bank(0), ws[2][:, kc, :], hT[:, kc, q0:q0 + 512], kc == 0, kc == 7, ["hT", "wh%d_2" % (h % 2)], PK(0))
                yield
                sigmoid_act(ztmp, bank(0), ztmp, PK(0), ["ztmp"], "ztmp")
                yield
                TTo("dve", zss[zp], bank(0), ztmp, ALU.mult, PK(0) + ["ztmp"], ["zs%d" % zp])

            def fin_chain(h, qg):
                par = (4 * h + qg) % 2
                zp = (4 * h + qg) % 3
                O = Osb[par]
                ok = "Osb%d" % par
                S.op("dve", lambda e: e.reciprocal(out=smallc[:, 16:24].rearrange("p (q m) -> p q m", m=2), in_=O[:, :, :, 128]), [ok], ["rB"])
                yield
                TTo("dve", smallc[:, 24:28], smallc[:, 16:24].rearrange("p (q m) -> p q m", m=2)[:, :, 1], gcol[:, 3:4].to_broadcast([128, 4]), ALU.mult,
                    ["rB", "gcol"], ["rB2"])
                yield
                for qt in range(4):
                    TS("pool", otmp, O[:, qt, 0, 0:128], smallc[:, 16 + 2 * qt:17 + 2 * qt], None, ALU.mult, None, [ok, "rB"], ["otmp"])
                    STT("dve", ofin[:, qt, :], O[:, qt, 1, 0:128], smallc[:, 24 + qt:25 + qt], otmp, ALU.mult, ALU.add, [ok, "rB2", "otmp"], ["ofin"])
                    ACT(onb, ofin[:, qt, :], AF.Square, ["ofin"], ["onb", "ssB"], accum=smallc[:, 28 + qt:29 + qt])
                    yield
                rstd_act(smallc[:, 32:36], smallc[:, 28:32], 128, ["ssB"], ["rsB"])
                yield
                for qt in range(4):
                    TS("pool", ofin[:, qt, :], ofin[:, qt, :], smallc[:, 32 + qt:33 + qt], None, ALU.mult, None, ["ofin", "rsB"], ["ofin"])
                    yield
                    TR(bank(1, 128), ofin[:, qt, :], ident, ["ofin", "cst"], PK(1))
                    yield
                    tq = 512 * qg + 128 * qt
                    STT("dve", yT[:, 0, h, tq:tq + 128], bank(1, 128), gcol[:, 0:1], zss[zp][:, qt * 128:(qt + 1) * 128], ALU.mult, ALU.mult,
                        PK(1) + ["gcol", "zs%d" % zp], ["yT"])
                    yield

            def run_all(gen):
                for _ in gen:
                    pass

            for g_ in k_groups(0):
                run_all(g_)
            run_all(q_chain(0, 0))
            units = [(h, qg, m) for h in range(4) for qg in range(4) for m in range(2)]
            chainq = []
            pvq = []
            pvg = [0]

            def enqueue(prio, gen, tag):
                i = len(chainq)
                for k_, it in enumerate(chainq):
                    if it[0] > prio and not it[2]:
                        i = k_
                        break
                chainq.insert(i, [prio, gen, False, tag])

            def drain_until(pred):
                while any(pred(it[3]) for it in chainq):
                    bg_step()

            def bg_step():
                while chainq:
                    it = chainq[0]
                    it[2] = True
                    try:
                        next(it[1])
                        return
                    except StopIteration:
                        chainq.pop(0)

            def make_pv(h, qg, m):
                par = (4 * h + qg) % 2
                lst = []
                for qt in range(4):
                    pb = 6 + (pvg[0] % 2)
                    pvg[0] += 1
                    for kt in range(NT):
                        def f(qt=qt, kt=kt, pb=pb):
                            MM(bank(pb, 130), E[m][:, kt, qt * 128:(qt + 1) * 128], vaug[:, kt, h, :], kt == 0, kt == NT - 1,
                               ["E%d" % m, "vaug"], PK(pb))
                            if kt == NT - 1:
                                CP("dve", Osb[par][:, qt, m, :], bank(pb, 130), PK(pb), ["Osb%d" % par])
                        lst.append(f)
                return lst

            def pv_n(n):
                for _ in range(n):
                    if pvq:
                        pvq.pop(0)()

            pending_fin = None
            for ui, (h, qg, m) in enumerate(units):
                par = (4 * h + qg) % 2
                if m == 0:
                    cur = 4 * h + qg
                    drain_until(lambda t: t == ("q", cur) or t == ("k", h) or (t[0] == "f" and t[1] <= cur - 2))
                    nxt = cur + 1
                    if nxt < 16:
                        enqueue(1, q_chain(nxt // 4, nxt % 4), ("q", nxt))
                    if qg == 0 and h < 3:
                        load_wh(h + 1)
                        for g_ in k_groups(h + 1):
                            enqueue(2, g_, ("k", h + 1))
                kT = kTs[h % 2]
                qT = qTs[par]
                for kp in range(NT // 2):
                    pb = 2 + 2 * (kp % 2)
                    for u in range(2):
                        kt = 2 * kp + u
                        MM(bank(pb + u), kT[m * 64:(m + 1) * 64, kt * 128:(kt + 1) * 128], qT[m * 64:(m + 1) * 64, :], True, True,
                           ["kT%d" % (h % 2), "qT%d" % par], PK(pb + u))
                    ACT(E[m][:, 2 * kp:2 * kp + 2, :], PS[:, pb * 512:(pb + 2) * 512].rearrange("p (t q) -> p t q", q=512), AF.Exp,
                        PK(pb, pb + 1), ["E%d" % m])
                    pv_n(3)
                    bg_step()
                    pv_n(3)
                    bg_step()
                    pv_n(2)
                    bg_step()
                pv_n(len(pvq))
                if pending_fin is not None:
                    enqueue(0, fin_chain(*pending_fin), ("f", 4 * pending_fin[0] + pending_fin[1]))
                    pending_fin = None
                pvq = make_pv(h, qg, m)
                if m == 1:
                    pending_fin = (h, qg)
            pv_n(len(pvq))
            enqueue(0, fin_chain(*pending_fin), ("f", 15))
            while chainq:
                bg_step()
            if debug and b == 0:
                out_ops.append(DMA("pool", dbg["yaT"], yT[:, 0].rearrange("p h t -> p (h t)"), "dbgB", ["yT"], []))

            PC = Carve()
            whc = [[PC.take("whc%d_%d" % (i, j), 8 * 128, BF16).rearrange("p (k c) -> p k c", c=128) for j in range(4)] for i in range(2)]
            gsb = PC.take("gsb", NT * 16, F32).rearrange("p (t c) -> p t c", c=16)
            nlf = PC.take("nlf", 2 * NT * 4, F32).rearrange("p (d t h) -> p d t h", d=2, h=4)
            garg = PC.take("garg", 2 * NT * 4, F32).rearrange("p (d t h) -> p d t h", d=2, h=4)
            wcol = PC.take("wcol", 2 * NT * 4, F32).rearrange("p (d t h) -> p d t h", d=2, h=4)
            wkcol = PC.take("wkcol", 2 * NT * 4, F32).rearrange("p (d t h) -> p d t h", d=2, h=4)
            fcol = PC.take("fcol", 2 * NT * 4, F32).rearrange("p (d t h) -> p d t h", d=2, h=4)
            cbc = PC.take("cbc", 2 * NT * 4, F32).rearrange("p (d t h) -> p d t h", d=2, h=4)
            xp_c = PC.take("xp_c", CT + 2, F32)
            xp_l = PC.take("xp_l", T + 2, F32)
            xc = PC.take("xc", TT, F32)
            xctmp = PC.take("xctmp", TT, F32)
            xc16 = PC.take("xc16c", TT, BF16)
            qmT = PC.take("qmT", TT, BF16)
            kmT = PC.take("kmT", TT, BF16)
            kw = PC.take("kw", 2 * NT * 128, BF16).rearrange("p (d t c) -> p d t c", d=2, c=128)
            vm = PC.take("vm", NT * 130, BF16).rearrange("p (t c) -> p t c", c=130)
            hf = PC.take("hf", 16 * 128, F32).rearrange("p (t c) -> p t c", c=128)
            P32 = PC.take("P32", 130, F32)
            C16 = PC.take("C16", 130, BF16)
            sT16 = PC.take("sT16", 128, BF16)
            hs = PC.take("hs", 128, F32)
            hn = PC.take("hn", 128, F32)
            sigo = PC.take("sigo", 512, F32)
            silz = PC.take("silz", 512, F32)
            gtmp = PC.take("gtmp", 512, F32)
            yt1 = PC.take("yt1", 128, F32)
            yt2 = PC.take("yt2", 128, F32)
            fence(PC.keys)
            for tt in range(NT):
                for kc in range(8):
                    MM(bank(0, 16, 16 * tt), hT[:, kc, tt * 128:(tt + 1) * 128], wg16[:, kc, :], kc == 0, kc == 7, ["hT", "wg16"], PK(0))
            TTo("dve", gsb, bank(0, NT * 16).rearrange("p (t c) -> p t c", c=16), bifbc[:].unsqueeze(1).to_broadcast([128, NT, 16]), ALU.add,
                PK(0) + ["bifbc"], ["gsb"])
            g5 = gsb.rearrange("p t (d g h) -> p d g t h", d=2, g=2)
            for d in range(2):
                ACT(nlf[:, d], g5[:, d, 1], AF.Exp, ["gsb"], ["nlf"], scale=-1.0)
            nlf2 = nlf.rearrange("p d t h -> p (d t h)")
            ACT(nlf2, nlf2, AF.Ln, ["nlf"], ["nlf"], bias=1.0)
            NG = NT * 4
            for d in range(2):
                MM(bank(1, NG, d * NG), tri_f if d == 0 else tri_b, nlf[:, d].rearrange("p t h -> p (t h)"), True, True, ["nlf", "cst"], PK(1))
            MM(bank(2, 2 * NG), ones, nlf2, True, True, ["nlf", "cst"], PK(2))
            for d in range(2):
                TTo("dve", garg[:, d], g5[:, d, 0], bank(1, NG, d * NG).rearrange("p (t h) -> p t h", h=4), ALU.subtract, PK(1) + ["gsb"], ["garg"])
            ACT(wcol.rearrange("p d t h -> p (d t h)"), garg.rearrange("p d t h -> p (d t h)"), AF.Exp, ["garg"], ["wcol"])
            ACT(fcol.rearrange("p d t h -> p (d t h)"), bank(1, 2 * NG), AF.Exp, PK(1), ["fcol"], scale=-1.0)
            ACT(cbc.rearrange("p d t h -> p (d t h)"), bank(2, 2 * NG), AF.Exp, PK(2), ["cbc"], scale=-1.0)
            TS("dve", wkcol.rearrange("p d t h -> p (d t h)"), wcol.rearrange("p d t h -> p (d t h)"), 128.0 ** -0.5, None, ALU.mult, None,
               ["wcol"], ["wkcol"])
            MS("dve", vm[:, :, 128:130], 1.0, ["vm"])
            MS("dve", xp_c[:, 0:1], 0.0, ["xp_c"])
            MS("dve", xp_c[:, CT + 1:CT + 2], 0.0, ["xp_c"])
            MS("dve", xp_l[:, 0:1], 0.0, ["xp_l"])
            MS("dve", xp_l[:, T + 1:T + 2], 0.0, ["xp_l"])

            tokgroups = [(0, 256)] + [(256 + 512 * g, 512) for g in range(4)]
            for h in range(4):
                ws = whc[h % 2]
                wkeys = ["whc%d_%d" % (h % 2, j) for j in range(4)]
                for j, c0 in enumerate((C_XM, C_VM, C_ZB, C_OB)):
                    DMA("pool", ws[j], win_d[:, c0 + h * 128:c0 + (h + 1) * 128].rearrange("(kc p) c -> p kc c", p=128),
                        wkeys[j], [], [wkeys[j]])
                for gi, (tok0, n) in enumerate(tokgroups):
                    pb = gi % 2
                    for kc in range(8):
                        MM(bank(pb, n), ws[0][:, kc, :], hT[:, kc, tok0:tok0 + n], kc == 0, kc == 7, ["hT", wkeys[0]], PK(pb))
                    if tok0 == 0:
                        CP("act", xp_c[:, 1:1 + CT], bank(pb, n), PK(pb), ["xp_c"])
                    else:
                        CP("act", xp_l[:, 1 + tok0 - CT:1 + tok0 - CT + n], bank(pb, n), PK(pb), ["xp_l"])
                cw = lambda k: pcol[:, 8 + 4 * k + h:9 + 4 * k + h]
                for (xp, key, o0, n) in ((xp_c, "xp_c", 0, CT), (xp_l, "xp_l", CT, T)):
                    TS("dve", xctmp[:, o0:o0 + n], xp[:, 1:1 + n], cw(1), pcol[:, 20 + h:21 + h], ALU.mult, ALU.add, [key, "pcol"], ["xctmp"])
                    STT("dve", xctmp[:, o0:o0 + n], xp[:, 0:n], cw(0), xctmp[:, o0:o0 + n], ALU.mult, ALU.add, [key, "pcol", "xctmp"], ["xctmp"])
                    STT("dve", xctmp[:, o0:o0 + n], xp[:, 2:2 + n], cw(2), xctmp[:, o0:o0 + n], ALU.mult, ALU.add, [key, "pcol", "xctmp"], ["xctmp"])
                sigmoid_act(xc, xctmp, xc, ["xctmp"], ["xc"], "xc")
                TTo("pool", xc, xc, xctmp, ALU.mult, ["xc", "xctmp"], ["xc"])
                CP("dve", xc16, xc, ["xc"], ["xc16c"])
                for gi, (tok0, n) in enumerate(tokgroups):
                    MM(bank(2, n), wqk16[:, 0, h, :], xc16[:, tok0:tok0 + n], True, True, ["xc16c", "wqk16"], PK(2))
                    MM(bank(3, n), wqk16[:, 1, h, :], xc16[:, tok0:tok0 + n], True, True, ["xc16c", "wqk16"], PK(3))
                    CP("act", qmT[:, tok0:tok0 + n], bank(2, n), PK(2), ["qmT"])
                    ACT(kmT[:, tok0:tok0 + n], bank(3, n), AF.Copy, PK(3), ["kmT"], scale=128.0 ** -0.5)
                for tt in range(NT):
                    pb = 4 + (tt % 2)
                    MM(bank(pb, 128), xc16[:, tt * 128:(tt + 1) * 128], wqk16[:, 1, h, :], True, True, ["xc16c", "wqk16"], PK(pb))
                    for kc in range(8):
                        MM(bank(pb, 128, 128), hT[:, kc, tt * 128:(tt + 1) * 128], ws[1][:, kc, :], kc == 0, kc == 7, ["hT", wkeys[1]], PK(pb))
                    ACT(kw[:, 0, tt, :], bank(pb, 128), AF.Copy, PK(pb) + ["wkcol"], ["kw"], scale=wkcol[:, 0, tt, h:h + 1])
                    ACT(kw[:, 1, tt, :], bank(pb, 128), AF.Copy, PK(pb) + ["wkcol"], ["kw"], scale=wkcol[:, 1, tt, h:h + 1])
                    CP("dve", vm[:, tt, 0:128], bank(pb, 128, 128), PK(pb), ["vm"])
                for d in range(2):
                    order = list(range(NT)) if d == 0 else [1, 0] + list(range(NT - 1, 1, -1))
                    msk = mask_f if d == 0 else mask_b
                    MS("dve", P32, 0.0, ["P32"])
                    for ci, tt in enumerate(order):
                        ccol = cbc[:, d, tt, h:h + 1]
                        last = ci == NT - 1
                        lat = tt >= 2
                        lt_ = tt - 2
                        if lat:
                            TS("pool", C16, P32, ccol, None, ALU.mult, None, ["P32", "cbc"], ["C16"])
                            MM(bank(6, 128), kmT[:, tt * 128:(tt + 1) * 128], qmT[:, tt * 128:(tt + 1) * 128], True, True, ["kmT", "qmT"], PK(6))
                            STT("dve", sT16, bank(6, 128), wcol[:, d, tt, h:h + 1], msk, ALU.mult, ALU.mult, PK(6) + ["wcol", "cst"], ["sT16"])
                            MM(bank(7, 130), qmT[:, tt * 128:(tt + 1) * 128], C16, True, False, ["qmT", "C16"], PK(7))
                            MM(bank(7, 130), sT16, vm[:, tt, :], False, True, ["sT16", "vm"], PK(7))
                            ACT(smallc[:, 8:9], bank(7, 1, 128), AF.Abs, PK(7), ["dnC"])
                            TTo("dve", smallc[:, 8:9], smallc[:, 8:9], fcol[:, d, tt, h:h + 1], ALU.max, ["dnC", "fcol"], ["dnC"])
                            S.op("dve", lambda e: e.reciprocal(out=smallc[:, 9:10], in_=smallc[:, 8:9]), ["dnC"], ["rdC"])
                            if d == 0:
                                ACT(hf[:, lt_, :], bank(7, 128), AF.Copy, PK(7) + ["rdC"], ["hf"], scale=smallc[:, 9:10])
                            else:
                                STT("dve", hs, bank(7, 128), smallc[:, 9:10], hf[:, lt_, :], ALU.mult, ALU.add, PK(7) + ["rdC", "hf"], ["hs"])
                        if not last:
                            pb = 4 + (ci % 2)
                            MM(bank(pb, 130), kw[:, d, tt, :], vm[:, tt, :], True, True, ["kw", "vm"], PK(pb))
                            STT("dve", P32, P32, ccol, bank(pb, 130), ALU.mult, ALU.add, ["P32", "cbc"] + PK(pb), ["P32"])
                        if lat and d == 1:
                            if lt_ % 4 == 3:
                                g0 = CT + (lt_ - 3) * 128
                                for kc in range(8):
                                    MM(bank(2), ws[3][:, kc, :], hT[:, kc, g0:g0 + 512], kc == 0, kc == 7, ["hT", wkeys[3]], PK(2))
                                sigmoid_act(sigo, bank(2), sigo, PK(2), ["sigo"], "sigo")
                                for kc in range(8):
                                    MM(bank(3), ws[2][:, kc, :], hT[:, kc, g0:g0 + 512], kc == 0, kc == 7, ["hT", wkeys[2]], PK(3))
                                sigmoid_act(gtmp, bank(3), gtmp, PK(3), ["gtmp"], "gtmp")
                                TTo("dve", silz, bank(3), gtmp, ALU.mult, PK(3) + ["gtmp"], ["silz"])
                            gofs = (lt_ % 4) * 128
                            ACT(hn, hs, AF.Square, ["hs"], ["hn", "ssC"], accum=smallc[:, 10:11])
                            rstd_act(smallc[:, 11:12], smallc[:, 10:11], 128, ["ssC"], ["rsC"])
                            TS("dve", hn, hs, smallc[:, 11:12], None, ALU.mult, None, ["hs", "rsC"], ["hn"])
                            TR(bank(1, 128), hn, ident, ["hn", "cst"], PK(1))
                            STT("dve", yt1, bank(1, 128), pcol[:, 24 + h:25 + h], sigo[:, gofs:gofs + 128], ALU.mult, ALU.mult,
                                PK(1) + ["pcol", "sigo"], ["yt1"])
                            STT("dve", yt2, xc[:, tt * 128:(tt + 1) * 128], pcol[:, 28 + h:29 + h], yt1, ALU.mult, ALU.add,
                                ["xc", "pcol", "yt1"], ["yt2"])
                            TTo("pool", yT[:, 1, h, lt_ * 128:(lt_ + 1) * 128], yt2, silz[:, gofs:gofs + 128], ALU.mult, ["yt2", "silz"], ["yT"])
            if debug and b == 0:
                out_ops.append(DMA("pool", dbg["ybT"], yT[:, 1].rearrange("p h t -> p (h t)"), "dbgC", ["yT"], []))

            PD = Carve()
            wga = PD.take("wga", 8 * 1024, BF16).rearrange("p (k c) -> p k c", c=1024)
            wgb = PD.take("wgb", 8 * 1024, BF16).rearrange("p (k c) -> p k c", c=1024)
            woa = PD.take("woa", 4 * 1024, BF16).rearrange("p (k c) -> p k c", c=1024)
            wob = PD.take("wob", 4 * 1024, BF16).rearrange("p (k c) -> p k c", c=1024)
            wo = PD.take("wo", 8 * 1024, BF16).rearrange("p (k c) -> p k c", c=1024)
            mT = PD.take("mT", 8 * 512, BF16).rearrange("p (k t) -> p k t", t=512)
            sga = PD.take("sga", 512, F32)
            sgb = PD.take("sgb", 512, F32)
            ma = PD.take("ma", 512, F32)
            mb_ = PD.take("mb", 512, F32)
            xo = [PD.take("xo%d" % i, D, F32) for i in range(2)]
            yg = PD.take("yg", D, F32)
            gtb = PD.take("gtb", D, F32)
            fence(PD.keys)
            for hf_ in range(2):
                MM(bank(4 + hf_), cst[0:3, 1024 + b * 128:1024 + (b + 1) * 128], gtrow[:, hf_ * 512:(hf_ + 1) * 512],
                   True, True, ["cst", "gtrow"], PK(4 + hf_))
                CP("dve", gtb[:, hf_ * 512:(hf_ + 1) * 512], bank(4 + hf_), PK(4 + hf_), ["gtb"])
            DMA("pool", wga, win_d[:, C_GA:C_GA + 1024].rearrange("(kc p) c -> p kc c", p=128), "wd0", [], ["wga"])
            DMA("pool", wgb, win_d[:, C_GB:C_GB + 1024].rearrange("(kc p) c -> p kc c", p=128), "wd1", [], ["wgb"])
            DMA("pool", woa, woa_d.rearrange("(kc p) c -> p kc c", p=128), "wd2", [], ["woa"])
            DMA("pool", wob, wob_d.rearrange("(kc p) c -> p kc c", p=128), "wd3", [], ["wob"])
            DMA("pool", wo, wo_d.rearrange("(kc p) c -> p kc c", p=128), "wd4", [], ["wo"])
            oi = 0
            for g in range(4):
                t0 = 512 * g
                for j in range(8):
                    cs_ = slice(j * 128, (j + 1) * 128)
                    for kc in range(4):
                        MM(bank(0), woa[:, kc, cs_], yT[:, 0, kc, t0:t0 + 512], kc == 0, kc == 3, ["woa", "yT"], PK(0))
                    for kc in range(4):
                        MM(bank(1), wob[:, kc, cs_], yT[:, 1, kc, t0:t0 + 512], kc == 0, kc == 3, ["wob", "yT"], PK(1))
                    for kc in range(8):
                        MM(bank(2), wga[:, kc, cs_], hT[:, kc, CT + t0:CT + t0 + 512], kc == 0, kc == 7, ["wga", "hT"], PK(2))
                    for kc in range(8):
                        MM(bank(3), wgb[:, kc, cs_], hT[:, kc, CT + t0:CT + t0 + 512], kc == 0, kc == 7, ["wgb", "hT"], PK(3))
                    sigmoid_act(sga, bank(2), sga, PK(2), ["sga"], "sga")
                    sigmoid_act(sgb, bank(3), sgb, PK(3), ["sgb"], "sgb")
                    TTo("dve", ma, bank(0), sga, ALU.mult, PK(0) + ["sga"], ["ma"])
                    TTo("dve", mb_, bank(1), sgb, ALU.mult, PK(1) + ["sgb"], ["mb"])
                    TTo("pool", mT[:, j, :], ma, mb_, ALU.add, ["ma", "mb"], ["mT"])
                for qt in range(4):
                    tok = t0 + qt * 128
                    sl = oi % 2
                    oi += 1
                    xk = "xo%d" % sl
                    DMA("sp", xo[sl], x_d[b, tok:tok + 128, :], "xin%d" % sl, [], [xk])
                    for hf_ in range(2):
                        pb = 4 + hf_
                        for j in range(8):
                            MM(bank(pb), mT[:, j, qt * 128:(qt + 1) * 128], wo[:, j, hf_ * 512:(hf_ + 1) * 512], j == 0, j == 7, ["mT", "wo"], PK(pb))
                        TTo("dve", yg[:, hf_ * 512:(hf_ + 1) * 512], bank(pb), gtb[:, hf_ * 512:(hf_ + 1) * 512], ALU.mult,
                            PK(pb) + ["gtb"], ["yg"])
                    TTo("pool", xo[sl], xo[sl], yg, ALU.add, [xk, "yg"], [xk])
                    out_ops.append(DMA("sp", out_d[b, tok:tok + 128, :], xo[sl], "xout%d" % sl, [xk], []))

        last = {}
        for o in out_ops:
            last[o.dma] = o
        S.emit({"sp": [o for o in last.values()]})
    return nc


def _consts():
    c = np.zeros((128, 1408), np.float32)
    i = np.arange(128)
    c[:, 0:128] = np.eye(128)
    c[:, 128:256] = (i[:, None] <= i[None, :])
    c[:, 256:384] = (i[:, None] >= i[None, :])
    c[:, 384:512] = (i[:, None] > i[None, :])
    c[:, 512:640] = (i[:, None] < i[None, :])
    c[:, 640:768] = 1.0
    c[:, 768:896] = (i[:, None] // 64 == i[None, :] // 64) / 64.0
    p = np.where((i % 32) < 16, i + 16, i - 16)
    perm = np.zeros((128, 128), np.float32)
    perm[p, i] = 1.0
    c[:, 896:1024] = perm
    for b in range(3):
        c[b, 1024 + b * 128:1024 + (b + 1) * 128] = 1.0
    t = np.arange(T)
    row = (t // 64).astype(np.float32)
    col = (t % 64).astype(np.float32)
    inv = (10000.0 ** (-np.arange(16, dtype=np.float32) / 16)).astype(np.float32)
    rope = np.zeros((128, 2, T), np.float32)
    for m in range(128):
        d = m % 64
        pos = row if d < 32 else col
        ang = (pos * inv[d % 16]).astype(np.float32)
        rope[m, 0] = np.cos(ang)
        sn = np.sin(ang)
        rope[m, 1] = -sn if (d % 32) < 16 else sn
    return c, rope


_NC_CACHE = {}


def kernel(x, c, ctx, c_ctx, norm_w, w_mod, b_mod, w_in, b_if, da_q_norm, da_k_norm,
           da_lambda_q1, da_lambda_k1, da_lambda_q2, da_lambda_k2, da_head_norm, w_out_a,
           ml_conv_w, ml_conv_b, ml_wq, ml_wk, ml_head_norm, ml_skip, w_out_b, w_o, _debug=False):
    f = lambda a: np.ascontiguousarray(np.asarray(a, dtype=np.float32))
    x, c, ctx, c_ctx = f(x), f(c), f(ctx), f(c_ctx)
    params = np.zeros((64, 128), np.float32)
    params[0:8] = f(norm_w).reshape(8, 128)
    params[8:20] = f(ml_conv_w).reshape(12, 128)
    params[20:24] = f(ml_conv_b).reshape(4, 128)
    params[24:28] = f(ml_head_norm).reshape(4, 128)
    params[28:32] = f(ml_skip).reshape(4, 128)
    params[32] = f(da_head_norm).reshape(128)
    params[33] = np.concatenate([f(da_q_norm).reshape(64)] * 2)
    params[34] = np.concatenate([f(da_k_norm).reshape(64)] * 2)
    lamv = np.concatenate([f(da_lambda_q1).reshape(64), f(da_lambda_k1).reshape(64),
                           f(da_lambda_q2).reshape(64), f(da_lambda_k2).reshape(64)])[None, :]
    consts, rope = _consts()
    shared = {
        "w_mod": f(w_mod).reshape(D, 3 * D), "w_in": f(w_in).reshape(D, INW), "params": params,
        "b_mod": f(b_mod).reshape(1, 3 * D), "lamv": np.ascontiguousarray(lamv), "b_if": f(b_if).reshape(1, 16),
        "ml_wq": f(ml_wq).reshape(4, 128, 128), "ml_wk": f(ml_wk).reshape(4, 128, 128),
        "w_out_a": f(w_out_a).reshape(512, D), "w_out_b": f(w_out_b).reshape(512, D), "w_o": f(w_o).reshape(D, D),
        "consts": consts, "rope": rope,
    }
    n = 8
    in_maps = []
    for i in range(n):
        m = dict(shared)
        m["x"] = np.ascontiguousarray(x[NB * i:NB * (i + 1)])
        m["ctx"] = np.ascontiguousarray(ctx[NB * i:NB * (i + 1)])
        m["cvec"] = np.ascontiguousarray(np.stack([c[NB * i], c[NB * i + 1], c_ctx]))
        in_maps.append(m)
    key = bool(_debug)
    if key not in _NC_CACHE:
        _NC_CACHE[key] = build_nc(debug=key)
    nc = _NC_CACHE[key]
    res = run_bass_kernel_spmd(nc, in_maps, core_ids=list(range(n)))
    out = np.concatenate([r["out"] for r in res.results], axis=0)
    if _debug:
        return out, res.results
    return out
```

```python
import contextlib
import numpy as np
import concourse.bass as bass
import concourse.mybir as mybir
from concourse.bass_utils import run_bass_kernel_spmd

F32 = mybir.dt.float32
BF16 = mybir.dt.bfloat16
AF = mybir.ActivationFunctionType
ALU = mybir.AluOpType
AX = mybir.AxisListType

COMPUTE = ("pe", "act", "dve", "pool")
ENGS = ("pe", "act", "dve", "pool", "sp")

D = 1024
NB = 2
T = 2048
CT = 256
TT = T + CT
NT = TT // 128
EPS = 1e-6
INW = 6160
C_Q, C_K, C_V, C_ZA, C_XM, C_VM, C_ZB, C_OB, C_G, C_GA, C_GB = 0, 512, 1024, 1536, 2048, 2560, 3072, 3584, 4096, 4112, 5136
LAM_INIT = 0.2


class Op:
    __slots__ = ("eng", "fn", "deps", "signal", "dma", "cnt")

    def __init__(self, eng, fn, dma):
        self.eng = eng
        self.fn = fn
        self.deps = []
        self.signal = False
        self.dma = dma
        self.cnt = 0


class Sched:
    def __init__(self, nc):
        self.nc = nc
        self.ops = {e: [] for e in ENGS}
        self.last_w = {}
        self.readers = {}
        self.dma_streams = {}

    def op(self, eng, fn, reads=(), writes=(), dma=None):
        o = Op(eng, fn, dma)
        deps = []
        is_async = dma is not None
        for r in reads:
            w = self.last_w.get(r)
            if w is not None:
                deps.append(w)
        for r in writes:
            w = self.last_w.get(r)
            if w is not None and (is_async or w.dma is not None or w.eng != eng):
                deps.append(w)
            lastrd = {}
            for rd in self.readers.get(r, ()):
                if rd.dma is not None:
                    deps.append(rd)
                elif is_async or rd.eng != eng:
                    lastrd[rd.eng] = rd
            deps.extend(lastrd.values())
        if dma is not None:
            lst = self.dma_streams.setdefault(dma, [])
            if lst:
                deps.append(lst[-1])
            lst.append(o)
        seen = set()
        for d in deps:
            if id(d) not in seen:
                seen.add(id(d))
                o.deps.append(d)
                d.signal = True
        for r in reads:
            self.readers.setdefault(r, []).append(o)
        for r in writes:
            self.last_w[r] = o
            self.readers[r] = []
        self.ops[eng].append(o)
        return o

    def emit(self, final_waits):
        nc = self.nc
        with contextlib.ExitStack() as es:
            sems = {e: es.enter_context(nc.semaphore("s_" + e)) for e in COMPUTE}
            dsems = {k: es.enter_context(nc.semaphore("d_" + k)) for k in self.dma_streams}
            for e in ENGS:
                c = 0
                for o in self.ops[e]:
                    if o.dma is None and o.signal:
                        c += 1
                    o.cnt = c
            for k, lst in self.dma_streams.items():
                for i, o in enumerate(lst):
                    o.cnt = 16 * (i + 1)
            block = es.enter_context(nc.Block())

            def run(engname, eng):
                waited = {}
                for o in self.ops[engname]:
                    for d in o.deps:
                        if d.dma is not None:
                            sem, key = dsems[d.dma], ("d", d.dma)
                        else:
                            sem, key = sems[d.eng], ("e", d.eng)
                        if waited.get(key, 0) >= d.cnt:
                            continue
                        waited[key] = d.cnt
                        eng.wait_ge(sem, d.cnt)
                    inst = o.fn(eng)
                    if o.dma is not None:
                        inst.then_inc(dsems[o.dma], 16)
                    elif o.signal:
                        inst.then_inc(sems[engname], 1)
                for d in final_waits.get(engname, ()):
                    eng.wait_ge(dsems[d.dma], d.cnt)

            @block.tensor
            def _(eng):
                run("pe", eng)

            @block.scalar
            def _(eng):
                run("act", eng)

            @block.vector
            def _(eng):
                run("dve", eng)

            @block.gpsimd
            def _(eng):
                run("pool", eng)

            @block.sync
            def _(eng):
                run("sp", eng)


def build_nc(debug=False):
    nc = bass.Bass("TRN2", target_bir_lowering=False)

    def dram(name, shape, kind="ExternalInput"):
        return nc.dram_tensor(name, list(shape), F32, kind=kind).ap()

    x_d = dram("x", [NB, T, D])
    ctx_d = dram("ctx", [NB, CT, D])
    cvec_d = dram("cvec", [3, D])
    wmod_d = dram("w_mod", [D, 3 * D])
    win_d = dram("w_in", [D, INW])
    prm_d = dram("params", [64, 128])
    bmod_d = dram("b_mod", [1, 3 * D])
    lamv_d = dram("lamv", [1, 256])
    bif_d = dram("b_if", [1, 16])
    wq_d = dram("ml_wq", [4, 128, 128])
    wk_d = dram("ml_wk", [4, 128, 128])
    woa_d = dram("w_out_a", [512, D])
    wob_d = dram("w_out_b", [512, D])
    wo_d = dram("w_o", [D, D])
    cst_d = dram("consts", [128, 1408])
    rope_d = dram("rope", [128, 2, T])
    out_d = dram("out", [NB, T, D], kind="ExternalOutput")
    dbg = {}
    if debug:
        dbg["hT"] = dram("dbg_hT", [128, 8 * TT], kind="ExternalOutput")
        dbg["yaT"] = dram("dbg_yaT", [128, 4 * T], kind="ExternalOutput")
        dbg["ybT"] = dram("dbg_ybT", [128, 4 * T], kind="ExternalOutput")

    es = contextlib.ExitStack()
    with es:
        def sb(name, shape, dt):
            return es.enter_context(nc.sbuf_tensor("sb_" + name, list(shape), dt))

        PS = es.enter_context(nc.psum_tensor("PS", [128, 4096], F32))

        def bank(i, n=512, off=0):
            return PS[:, i * 512 + off:i * 512 + off + n]

        def PK(*banks):
            return [("ps", b) for b in banks]

        S = Sched(nc)

        def MM(out, lhsT, rhs, start, stop, reads, writes):
            S.op("pe", lambda e: e.matmul(out, lhsT, rhs, start=start, stop=stop), reads, writes)

        def TR(out, in_, ident, reads, writes):
            S.op("pe", lambda e: e.transpose(out, in_, ident), reads, writes)

        def ACT(out, in_, func, reads, writes, scale=1.0, bias=None, accum=None):
            def f(e):
                kw = {}
                if bias is not None:
                    kw["bias"] = bias
                if accum is not None:
                    kw["accum_out"] = accum
                return e.activation(out=out, in_=in_, func=func, scale=scale, **kw)
            S.op("act", f, reads, writes)

        def TS(eng, out, in0, s1, s2, op0, op1, reads, writes):
            if s2 is None:
                S.op(eng, lambda e: e.tensor_scalar(out=out, in0=in0, scalar1=s1, scalar2=None, op0=op0), reads, writes)
            else:
                S.op(eng, lambda e: e.tensor_scalar(out=out, in0=in0, scalar1=s1, scalar2=s2, op0=op0, op1=op1), reads, writes)

        def TTo(eng, out, in0, in1, op, reads, writes):
            S.op(eng, lambda e: e.tensor_tensor(out=out, in0=in0, in1=in1, op=op), reads, writes)

        def STT(eng, out, in0, scalar, in1, op0, op1, reads, writes):
            S.op(eng, lambda e: e.scalar_tensor_tensor(out=out, in0=in0, scalar=scalar, in1=in1, op0=op0, op1=op1), reads, writes)

        def CP(eng, out, in_, reads, writes):
            if eng == "act":
                return ACT(out, in_, AF.Copy, reads, writes)
            S.op(eng, lambda e: e.tensor_copy(out=out, in_=in_), reads, writes)

        def MS(eng, ap, val, writes):
            S.op(eng, lambda e: e.memset(ap, val), (), writes)

        def DMA(eng, out, in_, stream, reads, writes):
            return S.op(eng, lambda e: e.dma_start(out=out, in_=in_), reads, writes, dma=stream)

        def sigmoid_act(out, in_, tmp, reads, writes, tkey):
            ACT(tmp, in_, AF.Exp, reads, [tkey], scale=-1.0)
            ACT(tmp, tmp, AF.Ln, [tkey], [tkey], bias=1.0)
            ACT(out, tmp, AF.Exp, [tkey], writes, scale=-1.0)

        def rstd_act(out, in_, n, reads, writes):
            ACT(out, in_, AF.Ln, reads, writes, scale=1.0 / n, bias=EPS)
            ACT(out, out, AF.Exp, writes, writes, scale=-0.5)

        cst = sb("cst", [128, 1408], F32)
        ident = cst[:, 0:128]
        mask_f = cst[:, 128:256]
        mask_b = cst[:, 256:384]
        tri_f = cst[:, 384:512]
        tri_b = cst[:, 512:640]
        ones = cst[:, 640:768]
        c16 = sb("c16", [128, 384], BF16)
        ident16 = c16[:, 0:128]
        blk16 = c16[:, 128:256]
        perm16 = c16[:, 256:384]
        rope = sb("rope", [128, 2, T], F32)
        pcol = sb("pcol", [128, 64], F32)
        gcol = sb("gcol", [128, 4], F32)
        gtrow = sb("gtrow", [3, D], F32)
        modT = sb("modT", [128, 16, 3], F32)
        svec = sb("svec", [128, 8, 3], F32)
        bifbc = sb("bifbc", [128, 16], F32)
        wqk16 = sb("wqk16", [128, 2, 4, 128], BF16)
        wg16 = sb("wg16", [128, 8, 16], BF16)
        hT = sb("hT", [128, 8, TT], BF16)
        yT = sb("yT", [128, 2, 4, T], BF16)
        smallc = sb("smallc", [128, 64], F32)
        ARENA_N = 54600
        arena = sb("arena", [128, ARENA_N], BF16)

        class Carve:
            def __init__(self):
                self.off = 0
                self.keys = []

            def take(self, key, nelem, dt):
                if dt == F32:
                    if self.off % 2:
                        self.off += 1
                    ap = arena[:, self.off:self.off + 2 * nelem].bitcast(F32)
                    self.off += 2 * nelem
                else:
                    ap = arena[:, self.off:self.off + nelem]
                    self.off += nelem
                    if self.off % 2:
                        self.off += 1
                assert self.off <= ARENA_N, (key, self.off)
                self.keys.append(key)
                return ap

        prev_keys = []

        def fence(new_keys):
            nonlocal prev_keys
            allk = list(dict.fromkeys(prev_keys + new_keys))
            MS("pool", smallc[:, 63:64], 0.0, allk + ["fence_dummy"])
            prev_keys = list(new_keys)

        DMA("sp", cst[:], cst_d, "c0", [], ["cst"])
        DMA("sp", rope[:], rope_d, "c1", [], ["rope"])
        CP("dve", c16[:, 0:128], cst[:, 0:128], ["cst"], ["c16"])
        CP("dve", c16[:, 128:384], cst[:, 768:1024], ["cst"], ["c16"])
        DMA("sp", bifbc[:], bif_d.partition_broadcast(128), "c2", [], ["bifbc"])
        DMA("pool", wqk16[:, 0], wq_d.rearrange("h d e -> d h e"), "c3", [], ["wqk16"])
        DMA("pool", wqk16[:, 1], wk_d.rearrange("h d e -> d h e"), "c4", [], ["wqk16"])
        DMA("pool", wg16[:], win_d[:, C_G:C_G + 16].rearrange("(kc p) c -> p kc c", p=128), "c5", [], ["wg16"])

        P0 = Carve()
        prm = P0.take("prm", 128, F32)[0:64, :]
        cs = P0.take("cs", D, F32)[0:3, :]
        cs2 = P0.take("cs2", D, F32)[0:3, :]
        scT = P0.take("scT", 24, F32).rearrange("p (k b) -> p k b", b=3)
        brow = P0.take("brow", 3 * D, F32)[0:3, :]
        modrow = P0.take("modrow", 3 * D, F32)[0:3, :]
        lv = P0.take("lv", 256, F32)[0:1, :]
        lt = P0.take("lt", 128, F32)[0:1, :]
        ls = P0.take("ls", 8, F32)[0:1, :]
        wms = [P0.take("wm%d" % i, 8 * 512, F32).rearrange("p (k c) -> p k c", c=512) for i in range(2)]
        fence(P0.keys)
        DMA("sp", prm, prm_d, "c6", [], ["prm"])
        DMA("sp", cs, cvec_d, "c7", [], ["cs"])
        DMA("sp", brow, bmod_d.partition_broadcast(3), "c8", [], ["brow"])
        DMA("sp", lv, lamv_d, "c9", [], ["lv"])
        TR(bank(0, 64), prm, ident[0:64, 0:64], ["prm", "cst"], PK(0))
        CP("dve", pcol[:], bank(0, 64), PK(0), ["pcol"])
        TS("dve", gcol[:, 0:1], pcol[:, 32:33], 1.0 - LAM_INIT, None, ALU.mult, None, ["pcol"], ["gcol"])
        TS("dve", gcol[:, 1:2], pcol[:, 33:34], 0.125, None, ALU.mult, None, ["pcol"], ["gcol"])
        CP("dve", gcol[:, 2:3], pcol[:, 34:35], ["pcol"], ["gcol"])
        lv4 = lv.rearrange("p (a b c) -> p a b c", a=2, b=2)
        TTo("dve", lt.rearrange("p (a c) -> p a c", a=2), lv4[:, :, 0, :], lv4[:, :, 1, :], ALU.mult, ["lv"], ["lt"])
        S.op("dve", lambda e: e.reduce_sum(out=ls[:, 0:2], in_=lt.rearrange("p (a c) -> p a c", a=2), axis=AX.X), ["lt"], ["ls"])
        ACT(ls[:, 2:4], ls[:, 0:2], AF.Exp, ["ls"], ["ls2"])
        TTo("dve", ls[:, 4:5], ls[:, 3:4], ls[:, 2:3], ALU.subtract, ["ls2"], ["ls3"])
        TS("dve", ls[:, 5:6], ls[:, 4:5], -LAM_INIT, None, ALU.add, None, ["ls3"], ["ls4"])
        MM(bank(0, 1, 64), ones[0:1, 0:128], ls[:, 5:6], True, True, ["ls4", "cst"], PK(0))
        CP("dve", gcol[:, 3:4], bank(0, 1, 64), PK(0), ["gcol"])
        sigmoid_act(cs2, cs, cs2, ["cs"], ["cs2"], "cs2")
        TTo("dve", cs2, cs2, cs, ALU.mult, ["cs2", "cs"], ["cs2"])
        for kc in range(8):
            TR(bank(1, 3, 4 * kc), cs2[:, kc * 128:(kc + 1) * 128], ident[0:3, 0:3], ["cs2", "cst"], PK(1))
        CP("dve", scT, bank(1, 32).rearrange("p (k b) -> p k b", b=4)[:, :, 0:3], PK(1), ["scT"])
        for cg in range(6):
            wm = wms[cg % 2]
            wk_ = "wm%d" % (cg % 2)
            DMA("sp", wm, wmod_d[:, cg * 512:(cg + 1) * 512].rearrange("(kc p) c -> p kc c", p=128), wk_, [], [wk_])
            pb = 2 + (cg % 2)
            for kc in range(8):
                MM(bank(pb)[0:3, :], scT[:, kc, :], wm[:, kc, :], kc == 0, kc == 7, ["scT", wk_], PK(pb))
            TTo("dve", modrow[:, cg * 512:(cg + 1) * 512], bank(pb)[0:3, :], brow[:, cg * 512:(cg + 1) * 512], ALU.add,
                PK(pb) + ["brow"], ["modrow"])
        for j in range(16):
            TR(bank(1, 3, 4 * j), modrow[:, j * 128:(j + 1) * 128], ident[0:3, 0:3], ["modrow", "cst"], PK(1))
        CP("dve", modT[:], bank(1, 64).rearrange("p (k b) -> p k b", b=4)[:, :, 0:3], PK(1), ["modT"])
        TS("dve", svec[:], modT[:, 8:16, :], 1.0, None, ALU.add, None, ["modT"], ["svec"])
        TTo("dve", svec[:], svec[:], pcol[:, 0:8].unsqueeze(2).to_broadcast([128, 8, 3]), ALU.mult, ["svec", "pcol"], ["svec"])
        CP("dve", gtrow[:], modrow[:, 2048:3072], ["modrow"], ["gtrow"])

        out_ops = []

        for b in range(NB):
            PA = Carve()
            xts = [PA.take("xt%d" % i, D, F32) for i in range(2)]
            sqj = PA.take("sqj", D, BF16)
            xns = [PA.take("xn%d" % i, D, BF16) for i in range(2)]
            fence(PA.keys)
            groups = [(0, 2)] + [(2 + 4 * g, 4) for g in range(4)]
            psb_all = PS[:, 0:2048].bitcast(BF16).rearrange("p (k t) -> p k t", t=512)
            ti = 0
            for (t0, ntile) in groups:
                mi = 2 if t0 == 0 else b
                for j in range(ntile):
                    tt = t0 + j
                    sl = ti % 2
                    ti += 1
                    src = ctx_d[b, tt * 128:(tt + 1) * 128, :] if tt < 2 else x_d[b, (tt - 2) * 128:(tt - 1) * 128, :]
                    DMA("sp", xts[sl], src, "xt%d" % sl, [], ["xt%d" % sl])
                    ACT(sqj, xts[sl], AF.Square, ["xt%d" % sl], ["sqj", "ssA"], accum=smallc[:, 0:1])
                    rstd_act(smallc[:, 1:2], smallc[:, 0:1], D, ["ssA"], ["rsA"])
                    TS("pool" if sl else "dve", xns[sl], xts[sl], smallc[:, 1:2], None, ALU.mult, None, ["xt%d" % sl, "rsA"], ["xn%d" % sl])
                    for kc in range(8):
                        TR(psb_all[:, kc, j * 128:(j + 1) * 128], xns[sl][:, kc * 128:(kc + 1) * 128], ident16,
                           ["xn%d" % sl, "c16"], PK(kc // 2))
                n = ntile * 128
                for kc in range(8):
                    o_ = hT[:, kc, t0 * 128:t0 * 128 + n]
                    i_ = psb_all[:, kc, 0:n]
                    if kc % 2:
                        ACT(o_, i_, AF.Identity, PK(kc // 2) + ["svec", "modT"], ["hT"], scale=svec[:, kc, mi:mi + 1], bias=modT[:, kc, mi:mi + 1])
                    else:
                        TS("dve", o_, i_, svec[:, kc, mi:mi + 1], modT[:, kc, mi:mi + 1], ALU.mult, ALU.add, PK(kc // 2) + ["svec", "modT"], ["hT"])
            if debug and b == 0:
                out_ops.append(DMA("pool", dbg["hT"], hT[:].rearrange("p k t -> p (k t)"), "dbgA", ["hT"], []))

            PB = Carve()
            wh = [[PB.take("wh%d_%d" % (i, j), 8 * 128, BF16).rearrange("p (k c) -> p k c", c=128) for j in range(3)] for i in range(2)]
            vaug = PB.take("vaug", NT * 4 * 130, BF16).rearrange("p (t h c) -> p t h c", h=4, c=130)
            kTs = [PB.take("kT%d" % i, TT, BF16) for i in range(2)]
            qTs = [PB.take("qT%d" % i, 512, BF16) for i in range(2)]
            E = [PB.take("E%d" % m, NT * 512, BF16).rearrange("p (t q) -> p t q", q=512) for m in range(2)]
            wv16 = E[1].rearrange("p t q -> p (t q)")[:, 0:8 * 512].rearrange("p (k c) -> p k c", c=512)
            x2 = PB.take("x2", 512, BF16)
            rsb = PB.take("rsb", 512, F32)
            xn = PB.take("xnq", 512, F32)
            xn16 = PB.take("xn16", 512, BF16)
            t1 = PB.take("t1", 512, F32)
            t2 = PB.take("t2", 512, F32)
            zss = [PB.take("zs%d" % i, 512, F32) for i in range(3)]
            ztmp = PB.take("ztmp", 512, F32)
            Osb = [PB.take("Osb%d" % i, 4 * 2 * 130, F32).rearrange("p (q m c) -> p q m c", m=2, c=130) for i in range(2)]
            otmp = PB.take("otmp", 128, F32)
            ofin = PB.take("ofin", 4 * 128, F32).rearrange("p (q c) -> p q c", c=128)
            onb = PB.take("onb", 128, F32)
            fence(PB.keys)
            DMA("pool", wv16, win_d[:, C_V:C_V + 512].rearrange("(kc p) c -> p kc c", p=128), "wv", [], ["E1"])
            MS("dve", vaug[:, :, :, 128:130], 1.0, ["vaug"])

            def load_wh(h):
                for j, c0 in enumerate((C_Q, C_K, C_ZA)):
                    k_ = "wh%d_%d" % (h % 2, j)
                    DMA("pool", wh[h % 2][j], win_d[:, c0 + h * 128:c0 + (h + 1) * 128].rearrange("(kc p) c -> p kc c", p=128), k_, [], [k_])

            load_wh(0)
            for tt in range(NT):
                pb = tt % 2
                for kc in range(8):
                    MM(bank(pb), hT[:, kc, tt * 128:(tt + 1) * 128], wv16[:, kc, :], kc == 0, kc == 7, ["hT", "E1"], PK(pb))
                CP("dve", vaug[:, tt, :, 0:128], bank(pb).rearrange("p (h c) -> p h c", c=128), PK(pb), ["vaug"])

            def qk_chain(wsel, wkey, tok0, n, gc, rope_off, out16, okey):
                for kc in range(8):
                    MM(bank(0, n), wsel[:, kc, :], hT[:, kc, tok0:tok0 + n], kc == 0, kc == 7, ["hT", wkey], PK(0))
                yield
                ACT(x2[:, 0:n], bank(0, n), AF.Square, PK(0), ["x2"])
                yield
                MM(bank(1, n), blk16, x2[:, 0:n], True, True, ["x2", "c16"], PK(1))
                yield
                ACT(rsb[:, 0:n], bank(1, n), AF.Ln, PK(1), ["rsb"], bias=EPS)
                ACT(rsb[:, 0:n], rsb[:, 0:n], AF.Exp, ["rsb"], ["rsb"], scale=-0.5)
                yield
                if rope_off is None:
                    STT("dve", out16, bank(0, n), gc, rsb[:, 0:n], ALU.mult, ALU.mult, PK(0) + ["rsb", "gcol"], [okey])
                    return
                STT("dve", xn[:, 0:n], bank(0, n), gc, rsb[:, 0:n], ALU.mult, ALU.mult, PK(0) + ["rsb", "gcol"], ["xnq"])
                yield
                CP("pool", xn16[:, 0:n], xn[:, 0:n], ["xnq"], ["xn16"])
                TTo("pool", t1[:, 0:n], xn[:, 0:n], rope[:, 0, rope_off:rope_off + n], ALU.mult, ["xnq", "rope"], ["t1"])
                yield
                MM(bank(1, n), perm16, xn16[:, 0:n], True, True, ["xn16", "c16"], PK(1))
                yield
                TTo("dve", t2[:, 0:n], bank(1, n), rope[:, 1, rope_off:rope_off + n], ALU.mult, PK(1) + ["rope"], ["t2"])
                yield
                TTo("pool", out16, t1[:, 0:n], t2[:, 0:n], ALU.add, ["t1", "t2"], [okey])

            def k_groups(h):
                ws = wh[h % 2]
                return [qk_chain(ws[1], "wh%d_1" % (h % 2), tok0, n, gcol[:, 2:3], roff, kTs[h % 2][:, tok0:tok0 + n], "kT%d" % (h % 2))
                        for (tok0, n, roff) in [(0, 256, None)] + [(256 + 512 * g, 512, 512 * g) for g in range(4)]]

            def q_chain(h, qg):
                ws = wh[h % 2]
                par = (4 * h + qg) % 2
                zp = (4 * h + qg) % 3
                yield from qk_chain(ws[0], "wh%d_0" % (h % 2), 256 + 512 * qg, 512, gcol[:, 1:2], 512 * qg, qTs[par], "qT%d" % par)
                yield
                q0 = 256 + 512 * qg
                for kc in range(8):
                    MM(bank(0), ws[2][:, kc, :], hT[:, kc, q0:q0 + 512], kc == 0, kc == 7, ["hT", "wh%d_2" % (h % 2)], PK(0))
                yield
                sigmoid_act(ztmp, bank(0), ztmp, PK(0), ["ztmp"], "ztmp")
                yield
                TTo("dve", zss[zp], bank(0), ztmp, ALU.mult, PK(0) + ["ztmp"], ["zs%d" % zp])

            def fin_chain(h, qg):
                par = (4 * h + qg) % 2
                zp = (4 * h + qg) % 3
                O = Osb[par]
                ok = "Osb%d" % par
                S.op("dve", lambda e: e.reciprocal(out=smallc[:, 16:24].rearrange("p (q m) -> p q m", m=2), in_=O[:, :, :, 128]), [ok], ["rB"])
                yield
                TTo("dve", smallc[:, 24:28], smallc[:, 16:24].rearrange("p (q m) -> p q m", m=2)[:, :, 1], gcol[:, 3:4].to_broadcast([128, 4]), ALU.mult,
                    ["rB", "gcol"], ["rB2"])
                yield
                for qt in range(4):
                    TS("pool", otmp, O[:, qt, 0, 0:128], smallc[:, 16 + 2 * qt:17 + 2 * qt], None, ALU.mult, None, [ok, "rB"], ["otmp"])
                    STT("dve", ofin[:, qt, :], O[:, qt, 1, 0:128], smallc[:, 24 + qt:25 + qt], otmp, ALU.mult, ALU.add, [ok, "rB2", "otmp"], ["ofin"])
                    ACT(onb, ofin[:, qt, :], AF.Square, ["ofin"], ["onb", "ssB"], accum=smallc[:, 28 + qt:29 + qt])
                    yield
                rstd_act(smallc[:, 32:36], smallc[:, 28:32], 128, ["ssB"], ["rsB"])
                yield
                for qt in range(4):
                    TS("pool", ofin[:, qt, :], ofin[:, qt, :], smallc[:, 32 + qt:33 + qt], None, ALU.mult, None, ["ofin", "rsB"], ["ofin"])
                    yield
                    TR(bank(1, 128), ofin[:, qt, :], ident, ["ofin", "cst"], PK(1))
                    yield
                    tq = 512 * qg + 128 * qt
                    STT("dve", yT[:, 0, h, tq:tq + 128], bank(1, 128), gcol[:, 0:1], zss[zp][:, qt * 128:(qt + 1) * 128], ALU.mult, ALU.mult,
                        PK(1) + ["gcol", "zs%d" % zp], ["yT"])
                    yield

            def run_all(gen):
                for _ in gen:
                    pass

            for g_ in k_groups(0):
                run_all(g_)
            run_all(q_chain(0, 0))
            units = [(h, qg, m) for h in range(4) for qg in range(4) for m in range(2)]
            chainq = []
            pvq = []
            pvg = [0]

            def enqueue(prio, gen, tag):
                i = len(chainq)
                for k_, it in enumerate(chainq):
                    if it[0] > prio and not it[2]:
                        i = k_
                        break
                chainq.insert(i, [prio, gen, False, tag])

            def drain_until(pred):
                while any(pred(it[3]) for it in chainq):
                    bg_step()

            def bg_step():
                while chainq:
                    it = chainq[0]
                    it[2] = True
                    try:
                        next(it[1])
                        return
                    except StopIteration:
                        chainq.pop(0)

            def make_pv(h, qg, m):
                par = (4 * h + qg) % 2
                lst = []
                for qt in range(4):
                    pb = 6 + (pvg[0] % 2)
                    pvg[0] += 1
                    for kt in range(NT):
                        def f(qt=qt, kt=kt, pb=pb):
                            MM(bank(pb, 130), E[m][:, kt, qt * 128:(qt + 1) * 128], vaug[:, kt, h, :], kt == 0, kt == NT - 1,
                               ["E%d" % m, "vaug"], PK(pb))
                            if kt == NT - 1:
                                CP("dve", Osb[par][:, qt, m, :], bank(pb, 130), PK(pb), ["Osb%d" % par])
                        lst.append(f)
                return lst

            def pv_n(n):
                for _ in range(n):
                    if pvq:
                        pvq.pop(0)()

            pending_fin = None
            for ui, (h, qg, m) in enumerate(units):
                par = (4 * h + qg) % 2
                if m == 0:
                    cur = 4 * h + qg
                    drain_until(lambda t: t == ("q", cur) or t == ("k", h) or (t[0] == "f" and t[1] <= cur - 2))
                    nxt = cur + 1
                    if nxt < 16:
                        enqueue(1, q_chain(nxt // 4, nxt % 4), ("q", nxt))
                    if qg == 0 and h < 3:
                        load_wh(h + 1)
                        for g_ in k_groups(h + 1):
                            enqueue(2, g_, ("k", h + 1))
                kT = kTs[h % 2]
                qT = qTs[par]
                for kp in range(NT // 2):
                    pb = 2 + 2 * (kp % 2)
                    for u in range(2):
                        kt = 2 * kp + u
                        MM(bank(pb + u), kT[m * 64:(m + 1) * 64, kt * 128:(kt + 1) * 128], qT[m * 64:(m + 1) * 64, :], True, True,
                           ["kT%d" % (h % 2), "qT%d" % par], PK(pb + u))
                    ACT(E[m][:, 2 * kp:2 * kp + 2, :], PS[:, pb * 512:(pb + 2) * 512].rearrange("p (t q) -> p t q", q=512), AF.Exp,
                        PK(pb, pb + 1), ["E%d" % m])
                    pv_n(3)
                    bg_step()
                    pv_n(3)
                    bg_step()
                    pv_n(2)
                    bg_step()
                pv_n(len(pvq))
                if pending_fin is not None:
                    enqueue(0, fin_chain(*pending_fin), ("f", 4 * pending_fin[0] + pending_fin[1]))
                    pending_fin = None
                pvq = make_pv(h, qg, m)
                if m == 1:
                    pending_fin = (h, qg)
            pv_n(len(pvq))
            enqueue(0, fin_chain(*pending_fin), ("f", 15))
            while chainq:
                bg_step()
            if debug and b == 0:
                out_ops.append(DMA("pool", dbg["yaT"], yT[:, 0].rearrange("p h t -> p (h t)"), "dbgB", ["yT"], []))

            PC = Carve()
            whc = [[PC.take("whc%d_%d" % (i, j), 8 * 128, BF16).rearrange("p (k c) -> p k c", c=128) for j in range(4)] for i in range(2)]
            gsb = PC.take("gsb", NT * 16, F32).rearrange("p (t c) -> p t c", c=16)
            nlf = PC.take("nlf", 2 * NT * 4, F32).rearrange("p (d t h) -> p d t h", d=2, h=4)
            garg = PC.take("garg", 2 * NT * 4, F32).rearrange("p (d t h) -> p d t h", d=2, h=4)
            wcol = PC.take("wcol", 2 * NT * 4, F32).rearrange("p (d t h) -> p d t h", d=2, h=4)
            wkcol = PC.take("wkcol", 2 * NT * 4, F32).rearrange("p (d t h) -> p d t h", d=2, h=4)
            fcol = PC.take("fcol", 2 * NT * 4, F32).rearrange("p (d t h) -> p d t h", d=2, h=4)
            cbc = PC.take("cbc", 2 * NT * 4, F32).rearrange("p (d t h) -> p d t h", d=2, h=4)
            xp_c = PC.take("xp_c", CT + 2, F32)
            xp_l = PC.take("xp_l", T + 2, F32)
            xc = PC.take("xc", TT, F32)
            xctmp = PC.take("xctmp", TT, F32)
            xc16 = PC.take("xc16c", TT, BF16)
            qmT = PC.take("qmT", TT, BF16)
            kmT = PC.take("kmT", TT, BF16)
            kw = PC.take("kw", 2 * NT * 128, BF16).rearrange("p (d t c) -> p d t c", d=2, c=128)
            vm = PC.take("vm", NT * 130, BF16).rearrange("p (t c) -> p t c", c=130)
            hf = PC.take("hf", 16 * 128, F32).rearrange("p (t c) -> p t c", c=128)
            P32 = PC.take("P32", 130, F32)
            C16 = PC.take("C16", 130, BF16)
            sT16 = PC.take("sT16", 128, BF16)
            hs = PC.take("hs", 128, F32)
            hn = PC.take("hn", 128, F32)
            sigo = PC.take("sigo", 512, F32)
            silz = PC.take("silz", 512, F32)
            gtmp = PC.take("gtmp", 512, F32)
            yt1 = PC.take("yt1", 128, F32)
            yt2 = PC.take("yt2", 128, F32)
            fence(PC.keys)
            for tt in range(NT):
                for kc in range(8):
                    MM(bank(0, 16, 16 * tt), hT[:, kc, tt * 128:(tt + 1) * 128], wg16[:, kc, :], kc == 0, kc == 7, ["hT", "wg16"], PK(0))
            TTo("dve", gsb, bank(0, NT * 16).rearrange("p (t c) -> p t c", c=16), bifbc[:].unsqueeze(1).to_broadcast([128, NT, 16]), ALU.add,
                PK(0) + ["bifbc"], ["gsb"])
            g5 = gsb.rearrange("p t (d g h) -> p d g t h", d=2, g=2)
            for d in range(2):
                ACT(nlf[:, d], g5[:, d, 1], AF.Exp, ["gsb"], ["nlf"], scale=-1.0)
            nlf2 = nlf.rearrange("p d t h -> p (d t h)")
            ACT(nlf2, nlf2, AF.Ln, ["nlf"], ["nlf"], bias=1.0)
            NG = NT * 4
            for d in range(2):
                MM(bank(1, NG, d * NG), tri_f if d == 0 else tri_b, nlf[:, d].rearrange("p t h -> p (t h)"), True, True, ["nlf", "cst"], PK(1))
            MM(bank(2, 2 * NG), ones, nlf2, True, True, ["nlf", "cst"], PK(2))
            for d in range(2):
                TTo("dve", garg[:, d], g5[:, d, 0], bank(1, NG, d * NG).rearrange("p (t h) -> p t h", h=4), ALU.subtract, PK(1) + ["gsb"], ["garg"])
            ACT(wcol.rearrange("p d t h -> p (d t h)"), garg.rearrange("p d t h -> p (d t h)"), AF.Exp, ["garg"], ["wcol"])
            ACT(fcol.rearrange("p d t h -> p (d t h)"), bank(1, 2 * NG), AF.Exp, PK(1), ["fcol"], scale=-1.0)
            ACT(cbc.rearrange("p d t h -> p (d t h)"), bank(2, 2 * NG), AF.Exp, PK(2), ["cbc"], scale=-1.0)
            TS("dve", wkcol.rearrange("p d t h -> p (d t h)"), wcol.rearrange("p d t h -> p (d t h)"), 128.0 ** -0.5, None, ALU.mult, None,
               ["wcol"], ["wkcol"])
            MS("dve", vm[:, :, 128:130], 1.0, ["vm"])
            MS("dve", xp_c[:, 0:1], 0.0, ["xp_c"])
            MS("dve", xp_c[:, CT + 1:CT + 2], 0.0, ["xp_c"])
            MS("dve", xp_l[:, 0:1], 0.0, ["xp_l"])
            MS("dve", xp_l[:, T + 1:T + 2], 0.0, ["xp_l"])

            tokgroups = [(0, 256)] + [(256 + 512 * g, 512) for g in range(4)]
            for h in range(4):
                ws = whc[h % 2]
                wkeys = ["whc%d_%d" % (h % 2, j) for j in range(4)]
                for j, c0 in enumerate((C_XM, C_VM, C_ZB, C_OB)):
                    DMA("pool", ws[j], win_d[:, c0 + h * 128:c0 + (h + 1) * 128].rearrange("(kc p) c -> p kc c", p=128),
                        wkeys[j], [], [wkeys[j]])
                for gi, (tok0, n) in enumerate(tokgroups):
                    pb = gi % 2
                    for kc in range(8):
                        MM(bank(pb, n), ws[0][:, kc, :], hT[:, kc, tok0:tok0 + n], kc == 0, kc == 7, ["hT", wkeys[0]], PK(pb))
                    if tok0 == 0:
                        CP("act", xp_c[:, 1:1 + CT], bank(pb, n), PK(pb), ["xp_c"])
                    else:
                        CP("act", xp_l[:, 1 + tok0 - CT:1 + tok0 - CT + n], bank(pb, n), PK(pb), ["xp_l"])
                cw = lambda k: pcol[:, 8 + 4 * k + h:9 + 4 * k + h]
                for (xp, key, o0, n) in ((xp_c, "xp_c", 0, CT), (xp_l, "xp_l", CT, T)):
                    TS("dve", xctmp[:, o0:o0 + n], xp[:, 1:1 + n], cw(1), pcol[:, 20 + h:21 + h], ALU.mult, ALU.add, [key, "pcol"], ["xctmp"])
                    STT("dve", xctmp[:, o0:o0 + n], xp[:, 0:n], cw(0), xctmp[:, o0:o0 + n], ALU.mult, ALU.add, [key, "pcol", "xctmp"], ["xctmp"])
                    STT("dve", xctmp[:, o0:o0 + n], xp[:, 2:2 + n], cw(2), xctmp[:, o0:o0 + n], ALU.mult, ALU.add, [key, "pcol", "xctmp"], ["xctmp"])
                sigmoid_act(xc, xctmp, xc, ["xctmp"], ["xc"], "xc")
                TTo("pool", xc, xc, xctmp, ALU.mult, ["xc", "xctmp"], ["xc"])
                CP("dve", xc16, xc, ["xc"], ["xc16c"])
                for gi, (tok0, n) in enumerate(tokgroups):
                    MM(bank(2, n), wqk16[:, 0, h, :], xc16[:, tok0:tok0 + n], True, True, ["xc16c", "wqk16"], PK(2))
                    MM(bank(3, n), wqk16[:, 1, h, :], xc16[:, tok0:tok0 + n], True, True, ["xc16c", "wqk16"], PK(3))
                    CP("act", qmT[:, tok0:tok0 + n], bank(2, n), PK(2), ["qmT"])
                    ACT(kmT[:, tok0:tok0 + n], bank(3, n), AF.Copy, PK(3), ["kmT"], scale=128.0 ** -0.5)
                for tt in range(NT):
                    pb = 4 + (tt % 2)
                    MM(bank(pb, 128), xc16[:, tt * 128:(tt + 1) * 128], wqk16[:, 1, h, :], True, True, ["xc16c", "wqk16"], PK(pb))
                    for kc in range(8):
                        MM(bank(pb, 128, 128), hT[:, kc, tt * 128:(tt + 1) * 128], ws[1][:, kc, :], kc == 0, kc == 7, ["hT", wkeys[1]], PK(pb))
                    ACT(kw[:, 0, tt, :], bank(pb, 128), AF.Copy, PK(pb) + ["wkcol"], ["kw"], scale=wkcol[:, 0, tt, h:h + 1])
                    ACT(kw[:, 1, tt, :], bank(pb, 128), AF.Copy, PK(pb) + ["wkcol"], ["kw"], scale=wkcol[:, 1, tt, h:h + 1])
                    CP("dve", vm[:, tt, 0:128], bank(pb, 128, 128), PK(pb), ["vm"])
                for d in range(2):
                    order = list(range(NT)) if d == 0 else [1, 0] + list(range(NT - 1, 1, -1))
                    msk = mask_f if d == 0 else mask_b
                    MS("dve", P32, 0.0, ["P32"])
                    for ci, tt in enumerate(order):
                        ccol = cbc[:, d, tt, h:h + 1]
                        last = ci == NT - 1
                        lat = tt >= 2
                        lt_ = tt - 2
                        if lat:
                            TS("pool", C16, P32, ccol, None, ALU.mult, None, ["P32", "cbc"], ["C16"])
                            MM(bank(6, 128), kmT[:, tt * 128:(tt + 1) * 128], qmT[:, tt * 128:(tt + 1) * 128], True, True, ["kmT", "qmT"], PK(6))
                            STT("dve", sT16, bank(6, 128), wcol[:, d, tt, h:h + 1], msk, ALU.mult, ALU.mult, PK(6) + ["wcol", "cst"], ["sT16"])
                            MM(bank(7, 130), qmT[:, tt * 128:(tt + 1) * 128], C16, True, False, ["qmT", "C16"], PK(7))
                            MM(bank(7, 130), sT16, vm[:, tt, :], False, True, ["sT16", "vm"], PK(7))
                            ACT(smallc[:, 8:9], bank(7, 1, 128), AF.Abs, PK(7), ["dnC"])
                            TTo("dve", smallc[:, 8:9], smallc[:, 8:9], fcol[:, d, tt, h:h + 1], ALU.max, ["dnC", "fcol"], ["dnC"])
                            S.op("dve", lambda e: e.reciprocal(out=smallc[:, 9:10], in_=smallc[:, 8:9]), ["dnC"], ["rdC"])
                            if d == 0:
                                ACT(hf[:, lt_, :], bank(7, 128), AF.Copy, PK(7) + ["rdC"], ["hf"], scale=smallc[:, 9:10])
                            else:
                                STT("dve", hs, bank(7, 128), smallc[:, 9:10], hf[:, lt_, :], ALU.mult, ALU.add, PK(7) + ["rdC", "hf"], ["hs"])
                        if not last:
                            pb = 4 + (ci % 2)
                            MM(bank(pb, 130), kw[:, d, tt, :], vm[:, tt, :], True, True, ["kw", "vm"], PK(pb))
                            STT("dve", P32, P32, ccol, bank(pb, 130), ALU.mult, ALU.add, ["P32", "cbc"] + PK(pb), ["P32"])
                        if lat and d == 1:
                            if lt_ % 4 == 3:
                                g0 = CT + (lt_ - 3) * 128
                                for kc in range(8):
                                    MM(bank(2), ws[3][:, kc, :], hT[:, kc, g0:g0 + 512], kc == 0, kc == 7, ["hT", wkeys[3]], PK(2))
                                sigmoid_act(sigo, bank(2), sigo, PK(2), ["sigo"], "sigo")
                                for kc in range(8):
                                    MM(bank(3), ws[2][:, kc, :], hT[:, kc, g0:g0 + 512], kc == 0, kc == 7, ["hT", wkeys[2]], PK(3))
                                sigmoid_act(gtmp, bank(3), gtmp, PK(3), ["gtmp"], "gtmp")
                                TTo("dve", silz, bank(3), gtmp, ALU.mult, PK(3) + ["gtmp"], ["silz"])
                            gofs = (lt_ % 4) * 128
                            ACT(hn, hs, AF.Square, ["hs"], ["hn", "ssC"], accum=smallc[:, 10:11])
                            rstd_act(smallc[:, 11:12], smallc[:, 10:11], 128, ["ssC"], ["rsC"])
                            TS("dve", hn, hs, smallc[:, 11:12], None, ALU.mult, None, ["hs", "rsC"], ["hn"])
                            TR(bank(1, 128), hn, ident, ["hn", "cst"], PK(1))
                            STT("dve", yt1, bank(1, 128), pcol[:, 24 + h:25 + h], sigo[:, gofs:gofs + 128], ALU.mult, ALU.mult,
                                PK(1) + ["pcol", "sigo"], ["yt1"])
                            STT("dve", yt2, xc[:, tt * 128:(tt + 1) * 128], pcol[:, 28 + h:29 + h], yt1, ALU.mult, ALU.add,
                                ["xc", "pcol", "yt1"], ["yt2"])
                            TTo("pool", yT[:, 1, h, lt_ * 128:(lt_ + 1) * 128], yt2, silz[:, gofs:gofs + 128], ALU.mult, ["yt2", "silz"], ["yT"])
            if debug and b == 0:
                out_ops.append(DMA("pool", dbg["ybT"], yT[:, 1].rearrange("p h t -> p (h t)"), "dbgC", ["yT"], []))

            PD = Carve()
            wga = PD.take("wga", 8 * 1024, BF16).rearrange("p (k c) -> p k c", c=1024)
            wgb = PD.take("wgb", 8 * 1024, BF16).rearrange("p (k c) -> p k c", c=1024)
            woa = PD.take("woa", 4 * 1024, BF16).rearrange("p (k c) -> p k c", c=1024)
            wob = PD.take("wob", 4 * 1024, BF16).rearrange("p (k c) -> p k c", c=1024)
            wo = PD.take("wo", 8 * 1024, BF16).rearrange("p (k c) -> p k c", c=1024)
            mT = PD.take("mT", 8 * 512, BF16).rearrange("p (k t) -> p k t", t=512)
            sga = PD.take("sga", 512, F32)
            sgb = PD.take("sgb", 512, F32)
            ma = PD.take("ma", 512, F32)
            mb_ = PD.take("mb", 512, F32)
            xo = [PD.take("xo%d" % i, D, F32) for i in range(2)]
            yg = PD.take("yg", D, F32)
            gtb = PD.take("gtb", D, F32)
            fence(PD.keys)
            for hf_ in range(2):
                MM(bank(4 + hf_), cst[0:3, 1024 + b * 128:1024 + (b + 1) * 128], gtrow[:, hf_ * 512:(hf_ + 1) * 512],
                   True, True, ["cst", "gtrow"], PK(4 + hf_))
                CP("dve", gtb[:, hf_ * 512:(hf_ + 1) * 512], bank(4 + hf_), PK(4 + hf_), ["gtb"])
            DMA("pool", wga, win_d[:, C_GA:C_GA + 1024].rearrange("(kc p) c -> p kc c", p=128), "wd0", [], ["wga"])
            DMA("pool", wgb, win_d[:, C_GB:C_GB + 1024].rearrange("(kc p) c -> p kc c", p=128), "wd1", [], ["wgb"])
            DMA("pool", woa, woa_d.rearrange("(kc p) c -> p kc c", p=128), "wd2", [], ["woa"])
            DMA("pool", wob, wob_d.rearrange("(kc p) c -> p kc c", p=128), "wd3", [], ["wob"])
            DMA("pool", wo, wo_d.rearrange("(kc p) c -> p kc c", p=128), "wd4", [], ["wo"])
            oi = 0
            yc = 0
            for g in range(4):
                t0 = 512 * g
                for j in range(8):
                    cs_ = slice(j * 128, (j + 1) * 128)
                    bs = 4 * (j % 2)
                    for kc in range(4):
                        MM(bank(bs), woa[:, kc, cs_], yT[:, 0, kc, t0:t0 + 512], kc == 0, kc == 3, ["woa", "yT"], PK(bs))
                    for kc in range(4):
                        MM(bank(bs + 1), wob[:, kc, cs_], yT[:, 1, kc, t0:t0 + 512], kc == 0, kc == 3, ["wob", "yT"], PK(bs + 1))
                    for kc in range(8):
                        MM(bank(bs + 2), wga[:, kc, cs_], hT[:, kc, CT + t0:CT + t0 + 512], kc == 0, kc == 7, ["wga", "hT"], PK(bs + 2))
                    for kc in range(8):
                        MM(bank(bs + 3), wgb[:, kc, cs_], hT[:, kc, CT + t0:CT + t0 + 512], kc == 0, kc == 7, ["wgb", "hT"], PK(bs + 3))
                    sigmoid_act(sga, bank(bs + 2), sga, PK(bs + 2), ["sga"], "sga")
                    sigmoid_act(sgb, bank(bs + 3), sgb, PK(bs + 3), ["sgb"], "sgb")
                    TTo("dve", ma, bank(bs), sga, ALU.mult, PK(bs) + ["sga"], ["ma"])
                    TTo("dve", mb_, bank(bs + 1), sgb, ALU.mult, PK(bs + 1) + ["sgb"], ["mb"])
                    TTo("pool", mT[:, j, :], ma, mb_, ALU.add, ["ma", "mb"], ["mT"])
                for qt in range(4):
                    tok = t0 + qt * 128
                    sl = oi % 2
                    oi += 1
                    xk = "xo%d" % sl
                    DMA("sp", xo[sl], x_d[b, tok:tok + 128, :], "xin%d" % sl, [], [xk])
                    for hf_ in range(2):
                        pb = yc % 8
                        yc += 1
                        for j in range(8):
                            MM(bank(pb), mT[:, j, qt * 128:(qt + 1) * 128], wo[:, j, hf_ * 512:(hf_ + 1) * 512], j == 0, j == 7, ["mT", "wo"], PK(pb))
                        TTo("dve", yg[:, hf_ * 512:(hf_ + 1) * 512], bank(pb), gtb[:, hf_ * 512:(hf_ + 1) * 512], ALU.mult,
                            PK(pb) + ["gtb"], ["yg"])
                    TTo("pool", xo[sl], xo[sl], yg, ALU.add, [xk, "yg"], [xk])
                    out_ops.append(DMA("sp", out_d[b, tok:tok + 128, :], xo[sl], "xout%d" % sl, [xk], []))

        last = {}
        for o in out_ops:
            last[o.dma] = o
        S.emit({"sp": [o for o in last.values()]})
    return nc


def _consts():
    c = np.zeros((128, 1408), np.float32)
    i = np.arange(128)
    c[:, 0:128] = np.eye(128)
    c[:, 128:256] = (i[:, None] <= i[None, :])
    c[:, 256:384] = (i[:, None] >= i[None, :])
    c[:, 384:512] = (i[:, None] > i[None, :])
    c[:, 512:640] = (i[:, None] < i[None, :])
    c[:, 640:768] = 1.0
    c[:, 768:896] = (i[:, None] // 64 == i[None, :] // 64) / 64.0
    p = np.where((i % 32) < 16, i + 16, i - 16)
    perm = np.zeros((128, 128), np.float32)
    perm[p, i] = 1.0
    c[:, 896:1024] = perm
    for b in range(3):
        c[b, 1024 + b * 128:1024 + (b + 1) * 128] = 1.0
    t = np.arange(T)
    row = (t // 64).astype(np.float32)
    col = (t % 64).astype(np.float32)
    inv = (10000.0 ** (-np.arange(16, dtype=np.float32) / 16)).astype(np.float32)
    rope = np.zeros((128, 2, T), np.float32)
    for m in range(128):
        d = m % 64
        pos = row if d < 32 else col
        ang = (pos * inv[d % 16]).astype(np.float32)
        rope[m, 0] = np.cos(ang)
        sn = np.sin(ang)
        rope[m, 1] = -sn if (d % 32) < 16 else sn
    return c, rope


_NC_CACHE = {}


def kernel(x, c, ctx, c_ctx, norm_w, w_mod, b_mod, w_in, b_if, da_q_norm, da_k_norm,
           da_lambda_q1, da_lambda_k1, da_lambda_q2, da_lambda_k2, da_head_norm, w_out_a,
           ml_conv_w, ml_conv_b, ml_wq, ml_wk, ml_head_norm, ml_skip, w_out_b, w_o, _debug=False):
    f = lambda a: np.ascontiguousarray(np.asarray(a, dtype=np.float32))
    x, c, ctx, c_ctx = f(x), f(c), f(ctx), f(c_ctx)
    params = np.zeros((64, 128), np.float32)
    params[0:8] = f(norm_w).reshape(8, 128)
    params[8:20] = f(ml_conv_w).reshape(12, 128)
    params[20:24] = f(ml_conv_b).reshape(4, 128)
    params[24:28] = f(ml_head_norm).reshape(4, 128)
    params[28:32] = f(ml_skip).reshape(4, 128)
    params[32] = f(da_head_norm).reshape(128)
    params[33] = np.concatenate([f(da_q_norm).reshape(64)] * 2)
    params[34] = np.concatenate([f(da_k_norm).reshape(64)] * 2)
    lamv = np.concatenate([f(da_lambda_q1).reshape(64), f(da_lambda_k1).reshape(64),
                           f(da_lambda_q2).reshape(64), f(da_lambda_k2).reshape(64)])[None, :]
    consts, rope = _consts()
    shared = {
        "w_mod": f(w_mod).reshape(D, 3 * D), "w_in": f(w_in).reshape(D, INW), "params": params,
        "b_mod": f(b_mod).reshape(1, 3 * D), "lamv": np.ascontiguousarray(lamv), "b_if": f(b_if).reshape(1, 16),
        "ml_wq": f(ml_wq).reshape(4, 128, 128), "ml_wk": f(ml_wk).reshape(4, 128, 128),
        "w_out_a": f(w_out_a).reshape(512, D), "w_out_b": f(w_out_b).reshape(512, D), "w_o": f(w_o).reshape(D, D),
        "consts": consts, "rope": rope,
    }
    n = 8
    in_maps = []
    for i in range(n):
        m = dict(shared)
        m["x"] = np.ascontiguousarray(x[NB * i:NB * (i + 1)])
        m["ctx"] = np.ascontiguousarray(ctx[NB * i:NB * (i + 1)])
        m["cvec"] = np.ascontiguousarray(np.stack([c[NB * i], c[NB * i + 1], c_ctx]))
        in_maps.append(m)
    key = bool(_debug)
    if key not in _NC_CACHE:
        _NC_CACHE[key] = build_nc(debug=key)
    nc = _NC_CACHE[key]
    res = run_bass_kernel_spmd(nc, in_maps, core_ids=list(range(n)))
    out = np.concatenate([r["out"] for r in res.results], axis=0)
    if _debug:
        return out, res.results
    return out
```

```python
import contextlib
import numpy as np
import concourse.bass as bass
import concourse.mybir as mybir
from concourse.bass_utils import run_bass_kernel_spmd

F32 = mybir.dt.float32
BF16 = mybir.dt.bfloat16
AF = mybir.ActivationFunctionType
ALU = mybir.AluOpType
AX = mybir.AxisListType

COMPUTE = ("pe", "act", "dve", "pool")
ENGS = ("pe", "act", "dve", "pool", "sp")

D = 1024
NB = 2
T = 2048
CT = 256
TT = T + CT
NT = TT // 128
EPS = 1e-6
INW = 6160
C_Q, C_K, C_V, C_ZA, C_XM, C_VM, C_ZB, C_OB, C_G, C_GA, C_GB = 0, 512, 1024, 1536, 2048, 2560, 3072, 3584, 4096, 4112, 5136
LAM_INIT = 0.2


class Op:
    __slots__ = ("eng", "fn", "deps", "signal", "dma", "cnt")

    def __init__(self, eng, fn, dma):
        self.eng = eng
        self.fn = fn
        self.deps = []
        self.signal = False
        self.dma = dma
        self.cnt = 0


class Sched:
    def __init__(self, nc):
        self.nc = nc
        self.ops = {e: [] for e in ENGS}
        self.last_w = {}
        self.readers = {}
        self.dma_streams = {}

    def op(self, eng, fn, reads=(), writes=(), dma=None):
        o = Op(eng, fn, dma)
        deps = []
        is_async = dma is not None
        for r in reads:
            w = self.last_w.get(r)
            if w is not None:
                deps.append(w)
        for r in writes:
            w = self.last_w.get(r)
            if w is not None and (is_async or w.dma is not None or w.eng != eng):
                deps.append(w)
            lastrd = {}
            for rd in self.readers.get(r, ()):
                if rd.dma is not None:
                    deps.append(rd)
                elif is_async or rd.eng != eng:
                    lastrd[rd.eng] = rd
            deps.extend(lastrd.values())
        if dma is not None:
            lst = self.dma_streams.setdefault(dma, [])
            if lst:
                deps.append(lst[-1])
            lst.append(o)
        seen = set()
        for d in deps:
            if id(d) not in seen:
                seen.add(id(d))
                o.deps.append(d)
                d.signal = True
        for r in reads:
            self.readers.setdefault(r, []).append(o)
        for r in writes:
            self.last_w[r] = o
            self.readers[r] = []
        self.ops[eng].append(o)
        return o

    def emit(self, final_waits):
        nc = self.nc
        with contextlib.ExitStack() as es:
            sems = {e: es.enter_context(nc.semaphore("s_" + e)) for e in COMPUTE}
            dsems = {k: es.enter_context(nc.semaphore("d_" + k)) for k in self.dma_streams}
            for e in ENGS:
                c = 0
                for o in self.ops[e]:
                    if o.dma is None and o.signal:
                        c += 1
                    o.cnt = c
            for k, lst in self.dma_streams.items():
                for i, o in enumerate(lst):
                    o.cnt = 16 * (i + 1)
            block = es.enter_context(nc.Block())

            def run(engname, eng):
                waited = {}
                for o in self.ops[engname]:
                    for d in o.deps:
                        if d.dma is not None:
                            sem, key = dsems[d.dma], ("d", d.dma)
                        else:
                            sem, key = sems[d.eng], ("e", d.eng)
                        if waited.get(key, 0) >= d.cnt:
                            continue
                        waited[key] = d.cnt
                        eng.wait_ge(sem, d.cnt)
                    inst = o.fn(eng)
                    if o.dma is not None:
                        inst.then_inc(dsems[o.dma], 16)
                    elif o.signal:
                        inst.then_inc(sems[engname], 1)
                for d in final_waits.get(engname, ()):
                    eng.wait_ge(dsems[d.dma], d.cnt)

            @block.tensor
            def _(eng):
                run("pe", eng)

            @block.scalar
            def _(eng):
                run("act", eng)

            @block.vector
            def _(eng):
                run("dve", eng)

            @block.gpsimd
            def _(eng):
                run("pool", eng)

            @block.sync
            def _(eng):
                run("sp", eng)


def build_nc(debug=False):
    nc = bass.Bass("TRN2", target_bir_lowering=False)

    def dram(name, shape, kind="ExternalInput"):
        return nc.dram_tensor(name, list(shape), F32, kind=kind).ap()

    x_d = dram("x", [NB, T, D])
    ctx_d = dram("ctx", [NB, CT, D])
    cvec_d = dram("cvec", [3, D])
    wmod_d = dram("w_mod", [D, 3 * D])
    win_d = dram("w_in", [D, INW])
    prm_d = dram("params", [64, 128])
    bmod_d = dram("b_mod", [1, 3 * D])
    lamv_d = dram("lamv", [1, 256])
    bif_d = dram("b_if", [1, 16])
    wq_d = dram("ml_wq", [4, 128, 128])
    wk_d = dram("ml_wk", [4, 128, 128])
    woa_d = dram("w_out_a", [512, D])
    wob_d = dram("w_out_b", [512, D])
    wo_d = dram("w_o", [D, D])
    cst_d = dram("consts", [128, 1408])
    rope_d = dram("rope", [128, 2, T])
    out_d = dram("out", [NB, T, D], kind="ExternalOutput")
    dbg = {}
    if debug:
        dbg["hT"] = dram("dbg_hT", [128, 8 * TT], kind="ExternalOutput")
        dbg["yaT"] = dram("dbg_yaT", [128, 4 * T], kind="ExternalOutput")
        dbg["ybT"] = dram("dbg_ybT", [128, 4 * T], kind="ExternalOutput")

    es = contextlib.ExitStack()
    with es:
        def sb(name, shape, dt):
            return es.enter_context(nc.sbuf_tensor("sb_" + name, list(shape), dt))

        PS = es.enter_context(nc.psum_tensor("PS", [128, 4096], F32))

        def bank(i, n=512, off=0):
            return PS[:, i * 512 + off:i * 512 + off + n]

        def PK(*banks):
            return [("ps", b) for b in banks]

        S = Sched(nc)

        def MM(out, lhsT, rhs, start, stop, reads, writes):
            S.op("pe", lambda e: e.matmul(out, lhsT, rhs, start=start, stop=stop), reads, writes)

        def TR(out, in_, ident, reads, writes):
            S.op("pe", lambda e: e.transpose(out, in_, ident), reads, writes)

        def ACT(out, in_, func, reads, writes, scale=1.0, bias=None, accum=None):
            def f(e):
                kw = {}
                if bias is not None:
                    kw["bias"] = bias
                if accum is not None:
                    kw["accum_out"] = accum
                return e.activation(out=out, in_=in_, func=func, scale=scale, **kw)
            S.op("act", f, reads, writes)

        def TS(eng, out, in0, s1, s2, op0, op1, reads, writes):
            if s2 is None:
                S.op(eng, lambda e: e.tensor_scalar(out=out, in0=in0, scalar1=s1, scalar2=None, op0=op0), reads, writes)
            else:
                S.op(eng, lambda e: e.tensor_scalar(out=out, in0=in0, scalar1=s1, scalar2=s2, op0=op0, op1=op1), reads, writes)

        def TTo(eng, out, in0, in1, op, reads, writes):
            S.op(eng, lambda e: e.tensor_tensor(out=out, in0=in0, in1=in1, op=op), reads, writes)

        def STT(eng, out, in0, scalar, in1, op0, op1, reads, writes):
            S.op(eng, lambda e: e.scalar_tensor_tensor(out=out, in0=in0, scalar=scalar, in1=in1, op0=op0, op1=op1), reads, writes)

        def CP(eng, out, in_, reads, writes):
            if eng == "act":
                return ACT(out, in_, AF.Copy, reads, writes)
            S.op(eng, lambda e: e.tensor_copy(out=out, in_=in_), reads, writes)

        def MS(eng, ap, val, writes):
            S.op(eng, lambda e: e.memset(ap, val), (), writes)

        def DMA(eng, out, in_, stream, reads, writes):
            return S.op(eng, lambda e: e.dma_start(out=out, in_=in_), reads, writes, dma=stream)

        def sigmoid_act(out, in_, tmp, reads, writes, tkey):
            ACT(tmp, in_, AF.Exp, reads, [tkey], scale=-1.0)
            ACT(tmp, tmp, AF.Ln, [tkey], [tkey], bias=1.0)
            ACT(out, tmp, AF.Exp, [tkey], writes, scale=-1.0)

        def rstd_act(out, in_, n, reads, writes):
            ACT(out, in_, AF.Ln, reads, writes, scale=1.0 / n, bias=EPS)
            ACT(out, out, AF.Exp, writes, writes, scale=-0.5)

        cst = sb("cst", [128, 1408], F32)
        ident = cst[:, 0:128]
        mask_f = cst[:, 128:256]
        mask_b = cst[:, 256:384]
        tri_f = cst[:, 384:512]
        tri_b = cst[:, 512:640]
        ones = cst[:, 640:768]
        c16 = sb("c16", [128, 384], BF16)
        ident16 = c16[:, 0:128]
        blk16 = c16[:, 128:256]
        perm16 = c16[:, 256:384]
        rope = sb("rope", [128, 2, T], F32)
        pcol = sb("pcol", [128, 64], F32)
        gcol = sb("gcol", [128, 4], F32)
        gtrow = sb("gtrow", [3, D], F32)
        modT = sb("modT", [128, 16, 3], F32)
        svec = sb("svec", [128, 8, 3], F32)
        bifbc = sb("bifbc", [128, 16], F32)
        wqk16 = sb("wqk16", [128, 2, 4, 128], BF16)
        wg16 = sb("wg16", [128, 8, 16], BF16)
        hT = sb("hT", [128, 8, TT], BF16)
        yT = sb("yT", [128, 2, 4, T], BF16)
        smallc = sb("smallc", [128, 64], F32)
        ARENA_N = 54600
        arena = sb("arena", [128, ARENA_N], BF16)

        class Carve:
            def __init__(self):
                self.off = 0
                self.keys = []

            def take(self, key, nelem, dt):
                if dt == F32:
                    if self.off % 2:
                        self.off += 1
                    ap = arena[:, self.off:self.off + 2 * nelem].bitcast(F32)
                    self.off += 2 * nelem
                else:
                    ap = arena[:, self.off:self.off + nelem]
                    self.off += nelem
                    if self.off % 2:
                        self.off += 1
                assert self.off <= ARENA_N, (key, self.off)
                self.keys.append(key)
                return ap

        prev_keys = []

        def fence(new_keys):
            nonlocal prev_keys
            allk = list(dict.fromkeys(prev_keys + new_keys))
            MS("pool", smallc[:, 63:64], 0.0, allk + ["fence_dummy"])
            prev_keys = list(new_keys)

        DMA("sp", cst[:], cst_d, "c0", [], ["cst"])
        DMA("sp", rope[:], rope_d, "c1", [], ["rope"])
        CP("dve", c16[:, 0:128], cst[:, 0:128], ["cst"], ["c16"])
        CP("dve", c16[:, 128:384], cst[:, 768:1024], ["cst"], ["c16"])
        DMA("sp", bifbc[:], bif_d.partition_broadcast(128), "c2", [], ["bifbc"])
        DMA("pool", wqk16[:, 0], wq_d.rearrange("h d e -> d h e"), "c3", [], ["wqk16"])
        DMA("pool", wqk16[:, 1], wk_d.rearrange("h d e -> d h e"), "c4", [], ["wqk16"])
        DMA("pool", wg16[:], win_d[:, C_G:C_G + 16].rearrange("(kc p) c -> p kc c", p=128), "c5", [], ["wg16"])

        P0 = Carve()
        prm = P0.take("prm", 128, F32)[0:64, :]
        cs = P0.take("cs", D, F32)[0:3, :]
        cs2 = P0.take("cs2", D, F32)[0:3, :]
        scT = P0.take("scT", 24, F32).rearrange("p (k b) -> p k b", b=3)
        brow = P0.take("brow", 3 * D, F32)[0:3, :]
        modrow = P0.take("modrow", 3 * D, F32)[0:3, :]
        lv = P0.take("lv", 256, F32)[0:1, :]
        lt = P0.take("lt", 128, F32)[0:1, :]
        ls = P0.take("ls", 8, F32)[0:1, :]
        wms = [P0.take("wm%d" % i, 8 * 512, F32).rearrange("p (k c) -> p k c", c=512) for i in range(2)]
        fence(P0.keys)
        DMA("sp", prm, prm_d, "c6", [], ["prm"])
        DMA("sp", cs, cvec_d, "c7", [], ["cs"])
        DMA("sp", brow, bmod_d.partition_broadcast(3), "c8", [], ["brow"])
        DMA("sp", lv, lamv_d, "c9", [], ["lv"])
        TR(bank(0, 64), prm, ident[0:64, 0:64], ["prm", "cst"], PK(0))
        CP("dve", pcol[:], bank(0, 64), PK(0), ["pcol"])
        TS("dve", gcol[:, 0:1], pcol[:, 32:33], 1.0 - LAM_INIT, None, ALU.mult, None, ["pcol"], ["gcol"])
        TS("dve", gcol[:, 1:2], pcol[:, 33:34], 0.125, None, ALU.mult, None, ["pcol"], ["gcol"])
        CP("dve", gcol[:, 2:3], pcol[:, 34:35], ["pcol"], ["gcol"])
        lv4 = lv.rearrange("p (a b c) -> p a b c", a=2, b=2)
        TTo("dve", lt.rearrange("p (a c) -> p a c", a=2), lv4[:, :, 0, :], lv4[:, :, 1, :], ALU.mult, ["lv"], ["lt"])
        S.op("dve", lambda e: e.reduce_sum(out=ls[:, 0:2], in_=lt.rearrange("p (a c) -> p a c", a=2), axis=AX.X), ["lt"], ["ls"])
        ACT(ls[:, 2:4], ls[:, 0:2], AF.Exp, ["ls"], ["ls2"])
        TTo("dve", ls[:, 4:5], ls[:, 3:4], ls[:, 2:3], ALU.subtract, ["ls2"], ["ls3"])
        TS("dve", ls[:, 5:6], ls[:, 4:5], -LAM_INIT, None, ALU.add, None, ["ls3"], ["ls4"])
        MM(bank(0, 1, 64), ones[0:1, 0:128], ls[:, 5:6], True, True, ["ls4", "cst"], PK(0))
        CP("dve", gcol[:, 3:4], bank(0, 1, 64), PK(0), ["gcol"])
        sigmoid_act(cs2, cs, cs2, ["cs"], ["cs2"], "cs2")
        TTo("dve", cs2, cs2, cs, ALU.mult, ["cs2", "cs"], ["cs2"])
        for kc in range(8):
            TR(bank(1, 3, 4 * kc), cs2[:, kc * 128:(kc + 1) * 128], ident[0:3, 0:3], ["cs2", "cst"], PK(1))
        CP("dve", scT, bank(1, 32).rearrange("p (k b) -> p k b", b=4)[:, :, 0:3], PK(1), ["scT"])
        for cg in range(6):
            wm = wms[cg % 2]
            wk_ = "wm%d" % (cg % 2)
            DMA("sp", wm, wmod_d[:, cg * 512:(cg + 1) * 512].rearrange("(kc p) c -> p kc c", p=128), wk_, [], [wk_])
            pb = 2 + (cg % 2)
            for kc in range(8):
                MM(bank(pb)[0:3, :], scT[:, kc, :], wm[:, kc, :], kc == 0, kc == 7, ["scT", wk_], PK(pb))
            TTo("dve", modrow[:, cg * 512:(cg + 1) * 512], bank(pb)[0:3, :], brow[:, cg * 512:(cg + 1) * 512], ALU.add,
                PK(pb) + ["brow"], ["modrow"])
        for j in range(16):
            TR(bank(1, 3, 4 * j), modrow[:, j * 128:(j + 1) * 128], ident[0:3, 0:3], ["modrow", "cst"], PK(1))
        CP("dve", modT[:], bank(1, 64).rearrange("p (k b) -> p k b", b=4)[:, :, 0:3], PK(1), ["modT"])
        TS("dve", svec[:], modT[:, 8:16, :], 1.0, None, ALU.add, None, ["modT"], ["svec"])
        TTo("dve", svec[:], svec[:], pcol[:, 0:8].unsqueeze(2).to_broadcast([128, 8, 3]), ALU.mult, ["svec", "pcol"], ["svec"])
        CP("dve", gtrow[:], modrow[:, 2048:3072], ["modrow"], ["gtrow"])

        out_ops = []

        for b in range(NB):
            PA = Carve()
            NSA = 4
            xts = [PA.take("xt%d" % i, D, F32) for i in range(NSA)]
            sqj = PA.take("sqj", D, BF16)
            xns = [PA.take("xn%d" % i, D, BF16) for i in range(NSA)]
            fence(PA.keys)
            groups = [(0, 2)] + [(2 + 4 * g, 4) for g in range(4)]
            psb_all = PS[:, 0:2048].bitcast(BF16).rearrange("p (k t) -> p k t", t=512)
            ti = 0
            for (t0, ntile) in groups:
                mi = 2 if t0 == 0 else b
                for j in range(ntile):
                    tt = t0 + j
                    sl = ti % NSA
                    ti += 1
                    src = ctx_d[b, tt * 128:(tt + 1) * 128, :] if tt < 2 else x_d[b, (tt - 2) * 128:(tt - 1) * 128, :]
                    DMA("sp", xts[sl], src, "xt%d" % sl, [], ["xt%d" % sl])
                    ACT(sqj, xts[sl], AF.Square, ["xt%d" % sl], ["sqj", "ssA%d" % sl], accum=smallc[:, 2 * sl:2 * sl + 1])
                    rstd_act(smallc[:, 2 * sl + 1:2 * sl + 2], smallc[:, 2 * sl:2 * sl + 1], D, ["ssA%d" % sl], ["rsA%d" % sl])
                    TS("pool" if sl % 2 else "dve", xns[sl], xts[sl], smallc[:, 2 * sl + 1:2 * sl + 2], None, ALU.mult, None,
                       ["xt%d" % sl, "rsA%d" % sl], ["xn%d" % sl])
                    for kc in range(8):
                        TR(psb_all[:, kc, j * 128:(j + 1) * 128], xns[sl][:, kc * 128:(kc + 1) * 128], ident16,
                           ["xn%d" % sl, "c16"], PK(kc // 2))
                n = ntile * 128
                for kc in range(8):
                    o_ = hT[:, kc, t0 * 128:t0 * 128 + n]
                    i_ = psb_all[:, kc, 0:n]
                    if kc % 2:
                        ACT(o_, i_, AF.Identity, PK(kc // 2) + ["svec", "modT"], ["hT"], scale=svec[:, kc, mi:mi + 1], bias=modT[:, kc, mi:mi + 1])
                    else:
                        TS("dve", o_, i_, svec[:, kc, mi:mi + 1], modT[:, kc, mi:mi + 1], ALU.mult, ALU.add, PK(kc // 2) + ["svec", "modT"], ["hT"])
            if debug and b == 0:
                out_ops.append(DMA("pool", dbg["hT"], hT[:].rearrange("p k t -> p (k t)"), "dbgA", ["hT"], []))

            PB = Carve()
            wh = [[PB.take("wh%d_%d" % (i, j), 8 * 128, BF16).rearrange("p (k c) -> p k c", c=128) for j in range(3)] for i in range(2)]
            vaug = PB.take("vaug", NT * 4 * 130, BF16).rearrange("p (t h c) -> p t h c", h=4, c=130)
            kTs = [PB.take("kT%d" % i, TT, BF16) for i in range(2)]
            qTs = [PB.take("qT%d" % i, 512, BF16) for i in range(2)]
            E = [PB.take("E%d" % m, NT * 512, BF16).rearrange("p (t q) -> p t q", q=512) for m in range(2)]
            wv16 = E[1].rearrange("p t q -> p (t q)")[:, 0:8 * 512].rearrange("p (k c) -> p k c", c=512)
            x2 = PB.take("x2", 512, BF16)
            rsb = PB.take("rsb", 512, F32)
            xn = PB.take("xnq", 512, F32)
            xn16 = PB.take("xn16", 512, BF16)
            t1 = PB.take("t1", 512, F32)
            t2 = PB.take("t2", 512, F32)
            zss = [PB.take("zs%d" % i, 512, F32) for i in range(3)]
            ztmp = PB.take("ztmp", 512, F32)
            Osb = [PB.take("Osb%d" % i, 4 * 2 * 130, F32).rearrange("p (q m c) -> p q m c", m=2, c=130) for i in range(2)]
            otmp = PB.take("otmp", 128, F32)
            ofin = PB.take("ofin", 4 * 128, F32).rearrange("p (q c) -> p q c", c=128)
            onb = PB.take("onb", 128, F32)
            fence(PB.keys)
            DMA("pool", wv16, win_d[:, C_V:C_V + 512].rearrange("(kc p) c -> p kc c", p=128), "wv", [], ["E1"])
            MS("dve", vaug[:, :, :, 128:130], 1.0, ["vaug"])

            def load_wh(h):
                for j, c0 in enumerate((C_Q, C_K, C_ZA)):
                    k_ = "wh%d_%d" % (h % 2, j)
                    DMA("pool", wh[h % 2][j], win_d[:, c0 + h * 128:c0 + (h + 1) * 128].rearrange("(kc p) c -> p kc c", p=128), k_, [], [k_])

            load_wh(0)
            for tt in range(NT):
                pb = tt % 2
                for kc in range(8):
                    MM(bank(pb), hT[:, kc, tt * 128:(tt + 1) * 128], wv16[:, kc, :], kc == 0, kc == 7, ["hT", "E1"], PK(pb))
                CP("dve", vaug[:, tt, :, 0:128], bank(pb).rearrange("p (h c) -> p h c", c=128), PK(pb), ["vaug"])

            def qk_chain(wsel, wkey, tok0, n, gc, rope_off, out16, okey):
                for kc in range(8):
                    MM(bank(0, n), wsel[:, kc, :], hT[:, kc, tok0:tok0 + n], kc == 0, kc == 7, ["hT", wkey], PK(0))
                yield
                ACT(x2[:, 0:n], bank(0, n), AF.Square, PK(0), ["x2"])
                yield
                MM(bank(1, n), blk16, x2[:, 0:n], True, True, ["x2", "c16"], PK(1))
                yield
                ACT(rsb[:, 0:n], bank(1, n), AF.Ln, PK(1), ["rsb"], bias=EPS)
                ACT(rsb[:, 0:n], rsb[:, 0:n], AF.Exp, ["rsb"], ["rsb"], scale=-0.5)
                yield
                if rope_off is None:
                    STT("dve", out16, bank(0, n), gc, rsb[:, 0:n], ALU.mult, ALU.mult, PK(0) + ["rsb", "gcol"], [okey])
                    return
                STT("dve", xn[:, 0:n], bank(0, n), gc, rsb[:, 0:n], ALU.mult, ALU.mult, PK(0) + ["rsb", "gcol"], ["xnq"])
                yield
                CP("pool", xn16[:, 0:n], xn[:, 0:n], ["xnq"], ["xn16"])
                TTo("pool", t1[:, 0:n], xn[:, 0:n], rope[:, 0, rope_off:rope_off + n], ALU.mult, ["xnq", "rope"], ["t1"])
                yield
                MM(bank(1, n), perm16, xn16[:, 0:n], True, True, ["xn16", "c16"], PK(1))
                yield
                TTo("dve", t2[:, 0:n], bank(1, n), rope[:, 1, rope_off:rope_off + n], ALU.mult, PK(1) + ["rope"], ["t2"])
                yield
                TTo("pool", out16, t1[:, 0:n], t2[:, 0:n], ALU.add, ["t1", "t2"], [okey])

            def k_groups(h):
                ws = wh[h % 2]
                return [qk_chain(ws[1], "wh%d_1" % (h % 2), tok0, n, gcol[:, 2:3], roff, kTs[h % 2][:, tok0:tok0 + n], "kT%d" % (h % 2))
                        for (tok0, n, roff) in [(0, 256, None)] + [(256 + 512 * g, 512, 512 * g) for g in range(4)]]

            def q_chain(h, qg):
                ws = wh[h % 2]
                par = (4 * h + qg) % 2
                zp = (4 * h + qg) % 3
                yield from qk_chain(ws[0], "wh%d_0" % (h % 2), 256 + 512 * qg, 512, gcol[:, 1:2], 512 * qg, qTs[par], "qT%d" % par)
                yield
                q0 = 256 + 512 * qg
                for kc in range(8):
                    MM(bank(0), ws[2][:, kc, :], hT[:, kc, q0:q0 + 512], kc == 0, kc == 7, ["hT", "wh%d_2" % (h % 2)], PK(0))
                yield
                sigmoid_act(ztmp, bank(0), ztmp, PK(0), ["ztmp"], "ztmp")
                yield
                TTo("dve", zss[zp], bank(0), ztmp, ALU.mult, PK(0) + ["ztmp"], ["zs%d" % zp])

            def fin_chain(h, qg):
                par = (4 * h + qg) % 2
                zp = (4 * h + qg) % 3
                O = Osb[par]
                ok = "Osb%d" % par
                S.op("dve", lambda e: e.reciprocal(out=smallc[:, 16:24].rearrange("p (q m) -> p q m", m=2), in_=O[:, :, :, 128]), [ok], ["rB"])
                yield
                TTo("dve", smallc[:, 24:28], smallc[:, 16:24].rearrange("p (q m) -> p q m", m=2)[:, :, 1], gcol[:, 3:4].to_broadcast([128, 4]), ALU.mult,
                    ["rB", "gcol"], ["rB2"])
                yield
                for qt in range(4):
                    TS("pool", otmp, O[:, qt, 0, 0:128], smallc[:, 16 + 2 * qt:17 + 2 * qt], None, ALU.mult, None, [ok, "rB"], ["otmp"])
                    STT("dve", ofin[:, qt, :], O[:, qt, 1, 0:128], smallc[:, 24 + qt:25 + qt], otmp, ALU.mult, ALU.add, [ok, "rB2", "otmp"], ["ofin"])
                    ACT(onb, ofin[:, qt, :], AF.Square, ["ofin"], ["onb", "ssB"], accum=smallc[:, 28 + qt:29 + qt])
                    yield
                rstd_act(smallc[:, 32:36], smallc[:, 28:32], 128, ["ssB"], ["rsB"])
                yield
                for qt in range(4):
                    TS("pool", ofin[:, qt, :], ofin[:, qt, :], smallc[:, 32 + qt:33 + qt], None, ALU.mult, None, ["ofin", "rsB"], ["ofin"])
                    yield
                    TR(bank(1, 128), ofin[:, qt, :], ident, ["ofin", "cst"], PK(1))
                    yield
                    tq = 512 * qg + 128 * qt
                    STT("dve", yT[:, 0, h, tq:tq + 128], bank(1, 128), gcol[:, 0:1], zss[zp][:, qt * 128:(qt + 1) * 128], ALU.mult, ALU.mult,
                        PK(1) + ["gcol", "zs%d" % zp], ["yT"])
                    yield

            def run_all(gen):
                for _ in gen:
                    pass

            for g_ in k_groups(0):
                run_all(g_)
            run_all(q_chain(0, 0))
            units = [(h, qg, m) for h in range(4) for qg in range(4) for m in range(2)]
            chainq = []
            pvq = []
            pvg = [0]

            def enqueue(prio, gen, tag):
                i = len(chainq)
                for k_, it in enumerate(chainq):
                    if it[0] > prio and not it[2]:
                        i = k_
                        break
                chainq.insert(i, [prio, gen, False, tag])

            def drain_until(pred):
                while any(pred(it[3]) for it in chainq):
                    bg_step()

            def bg_step():
                while chainq:
                    it = chainq[0]
                    it[2] = True
                    try:
                        next(it[1])
                        return
                    except StopIteration:
                        chainq.pop(0)

            def make_pv(h, qg, m):
                par = (4 * h + qg) % 2
                lst = []
                for qt in range(4):
                    pb = 6 + (pvg[0] % 2)
                    pvg[0] += 1
                    for kt in range(NT):
                        def f(qt=qt, kt=kt, pb=pb):
                            MM(bank(pb, 130), E[m][:, kt, qt * 128:(qt + 1) * 128], vaug[:, kt, h, :], kt == 0, kt == NT - 1,
                               ["E%d" % m, "vaug"], PK(pb))
                            if kt == NT - 1:
                                CP("dve", Osb[par][:, qt, m, :], bank(pb, 130), PK(pb), ["Osb%d" % par])
                        lst.append(f)
                return lst

            def pv_n(n):
                for _ in range(n):
                    if pvq:
                        pvq.pop(0)()

            pending_fin = None
            for ui, (h, qg, m) in enumerate(units):
                par = (4 * h + qg) % 2
                if m == 0:
                    cur = 4 * h + qg
                    drain_until(lambda t: t == ("q", cur) or t == ("k", h) or (t[0] == "f" and t[1] <= cur - 2))
                    nxt = cur + 1
                    if nxt < 16:
                        enqueue(1, q_chain(nxt // 4, nxt % 4), ("q", nxt))
                    if qg == 0 and h < 3:
                        load_wh(h + 1)
                        for g_ in k_groups(h + 1):
                            enqueue(2, g_, ("k", h + 1))
                kT = kTs[h % 2]
                qT = qTs[par]
                for kp in range(NT // 2):
                    pb = 2 + 2 * (kp % 2)
                    for u in range(2):
                        kt = 2 * kp + u
                        MM(bank(pb + u), kT[m * 64:(m + 1) * 64, kt * 128:(kt + 1) * 128], qT[m * 64:(m + 1) * 64, :], True, True,
                           ["kT%d" % (h % 2), "qT%d" % par], PK(pb + u))
                    ACT(E[m][:, 2 * kp:2 * kp + 2, :], PS[:, pb * 512:(pb + 2) * 512].rearrange("p (t q) -> p t q", q=512), AF.Exp,
                        PK(pb, pb + 1), ["E%d" % m])
                    pv_n(3)
                    bg_step()
                    pv_n(3)
                    bg_step()
                    pv_n(2)
                    bg_step()
                pv_n(len(pvq))
                if pending_fin is not None:
                    enqueue(0, fin_chain(*pending_fin), ("f", 4 * pending_fin[0] + pending_fin[1]))
                    pending_fin = None
                pvq = make_pv(h, qg, m)
                if m == 1:
                    pending_fin = (h, qg)
            pv_n(len(pvq))
            enqueue(0, fin_chain(*pending_fin), ("f", 15))
            while chainq:
                bg_step()
            if debug and b == 0:
                out_ops.append(DMA("pool", dbg["yaT"], yT[:, 0].rearrange("p h t -> p (h t)"), "dbgB", ["yT"], []))

            PC = Carve()
            whc = [[PC.take("whc%d_%d" % (i, j), 8 * 128, BF16).rearrange("p (k c) -> p k c", c=128) for j in range(4)] for i in range(2)]
            gsb = PC.take("gsb", NT * 16, F32).rearrange("p (t c) -> p t c", c=16)
            nlf = PC.take("nlf", 2 * NT * 4, F32).rearrange("p (d t h) -> p d t h", d=2, h=4)
            garg = PC.take("garg", 2 * NT * 4, F32).rearrange("p (d t h) -> p d t h", d=2, h=4)
            wcol = PC.take("wcol", 2 * NT * 4, F32).rearrange("p (d t h) -> p d t h", d=2, h=4)
            wkcol = PC.take("wkcol", 2 * NT * 4, F32).rearrange("p (d t h) -> p d t h", d=2, h=4)
            fcol = PC.take("fcol", 2 * NT * 4, F32).rearrange("p (d t h) -> p d t h", d=2, h=4)
            cbc = PC.take("cbc", 2 * NT * 4, F32).rearrange("p (d t h) -> p d t h", d=2, h=4)
            xp_c = PC.take("xp_c", CT + 2, F32)
            xp_l = PC.take("xp_l", T + 2, F32)
            xc = PC.take("xc", TT, F32)
            xctmp = PC.take("xctmp", TT, F32)
            xc16 = PC.take("xc16c", TT, BF16)
            qmT = PC.take("qmT", TT, BF16)
            kmT = PC.take("kmT", TT, BF16)
            kw = PC.take("kw", 2 * NT * 128, BF16).rearrange("p (d t c) -> p d t c", d=2, c=128)
            vm = PC.take("vm", NT * 130, BF16).rearrange("p (t c) -> p t c", c=130)
            hf = PC.take("hf", 16 * 128, F32).rearrange("p (t c) -> p t c", c=128)
            P32 = PC.take("P32", 130, F32)
            C16 = PC.take("C16", 130, BF16)
            sT16 = PC.take("sT16", 128, BF16)
            hs = PC.take("hs", 128, F32)
            hn = PC.take("hn", 128, F32)
            sigo = PC.take("sigo", 512, F32)
            silz = PC.take("silz", 512, F32)
            gtmp = PC.take("gtmp", 512, F32)
            yt1 = PC.take("yt1", 128, F32)
            yt2 = PC.take("yt2", 128, F32)
            fence(PC.keys)
            for tt in range(NT):
                for kc in range(8):
                    MM(bank(0, 16, 16 * tt), hT[:, kc, tt * 128:(tt + 1) * 128], wg16[:, kc, :], kc == 0, kc == 7, ["hT", "wg16"], PK(0))
            TTo("dve", gsb, bank(0, NT * 16).rearrange("p (t c) -> p t c", c=16), bifbc[:].unsqueeze(1).to_broadcast([128, NT, 16]), ALU.add,
                PK(0) + ["bifbc"], ["gsb"])
            g5 = gsb.rearrange("p t (d g h) -> p d g t h", d=2, g=2)
            for d in range(2):
                ACT(nlf[:, d], g5[:, d, 1], AF.Exp, ["gsb"], ["nlf"], scale=-1.0)
            nlf2 = nlf.rearrange("p d t h -> p (d t h)")
            ACT(nlf2, nlf2, AF.Ln, ["nlf"], ["nlf"], bias=1.0)
            NG = NT * 4
            for d in range(2):
                MM(bank(1, NG, d * NG), tri_f if d == 0 else tri_b, nlf[:, d].rearrange("p t h -> p (t h)"), True, True, ["nlf", "cst"], PK(1))
            MM(bank(2, 2 * NG), ones, nlf2, True, True, ["nlf", "cst"], PK(2))
            for d in range(2):
                TTo("dve", garg[:, d], g5[:, d, 0], bank(1, NG, d * NG).rearrange("p (t h) -> p t h", h=4), ALU.subtract, PK(1) + ["gsb"], ["garg"])
            ACT(wcol.rearrange("p d t h -> p (d t h)"), garg.rearrange("p d t h -> p (d t h)"), AF.Exp, ["garg"], ["wcol"])
            ACT(fcol.rearrange("p d t h -> p (d t h)"), bank(1, 2 * NG), AF.Exp, PK(1), ["fcol"], scale=-1.0)
            ACT(cbc.rearrange("p d t h -> p (d t h)"), bank(2, 2 * NG), AF.Exp, PK(2), ["cbc"], scale=-1.0)
            TS("dve", wkcol.rearrange("p d t h -> p (d t h)"), wcol.rearrange("p d t h -> p (d t h)"), 128.0 ** -0.5, None, ALU.mult, None,
               ["wcol"], ["wkcol"])
            MS("dve", vm[:, :, 128:130], 1.0, ["vm"])
            MS("dve", xp_c[:, 0:1], 0.0, ["xp_c"])
            MS("dve", xp_c[:, CT + 1:CT + 2], 0.0, ["xp_c"])
            MS("dve", xp_l[:, 0:1], 0.0, ["xp_l"])
            MS("dve", xp_l[:, T + 1:T + 2], 0.0, ["xp_l"])

            tokgroups = [(0, 256)] + [(256 + 512 * g, 512) for g in range(4)]
            for h in range(4):
                ws = whc[h % 2]
                wkeys = ["whc%d_%d" % (h % 2, j) for j in range(4)]
                for j, c0 in enumerate((C_XM, C_VM, C_ZB, C_OB)):
                    DMA("pool", ws[j], win_d[:, c0 + h * 128:c0 + (h + 1) * 128].rearrange("(kc p) c -> p kc c", p=128),
                        wkeys[j], [], [wkeys[j]])
                for gi, (tok0, n) in enumerate(tokgroups):
                    pb = gi % 2
                    for kc in range(8):
                        MM(bank(pb, n), ws[0][:, kc, :], hT[:, kc, tok0:tok0 + n], kc == 0, kc == 7, ["hT", wkeys[0]], PK(pb))
                    if tok0 == 0:
                        CP("act", xp_c[:, 1:1 + CT], bank(pb, n), PK(pb), ["xp_c"])
                    else:
                        CP("act", xp_l[:, 1 + tok0 - CT:1 + tok0 - CT + n], bank(pb, n), PK(pb), ["xp_l"])
                cw = lambda k: pcol[:, 8 + 4 * k + h:9 + 4 * k + h]
                for (xp, key, o0, n) in ((xp_c, "xp_c", 0, CT), (xp_l, "xp_l", CT, T)):
                    TS("dve", xctmp[:, o0:o0 + n], xp[:, 1:1 + n], cw(1), pcol[:, 20 + h:21 + h], ALU.mult, ALU.add, [key, "pcol"], ["xctmp"])
                    STT("dve", xctmp[:, o0:o0 + n], xp[:, 0:n], cw(0), xctmp[:, o0:o0 + n], ALU.mult, ALU.add, [key, "pcol", "xctmp"], ["xctmp"])
                    STT("dve", xctmp[:, o0:o0 + n], xp[:, 2:2 + n], cw(2), xctmp[:, o0:o0 + n], ALU.mult, ALU.add, [key, "pcol", "xctmp"], ["xctmp"])
                sigmoid_act(xc, xctmp, xc, ["xctmp"], ["xc"], "xc")
                TTo("pool", xc, xc, xctmp, ALU.mult, ["xc", "xctmp"], ["xc"])
                CP("dve", xc16, xc, ["xc"], ["xc16c"])
                for gi, (tok0, n) in enumerate(tokgroups):
                    MM(bank(2, n), wqk16[:, 0, h, :], xc16[:, tok0:tok0 + n], True, True, ["xc16c", "wqk16"], PK(2))
                    MM(bank(3, n), wqk16[:, 1, h, :], xc16[:, tok0:tok0 + n], True, True, ["xc16c", "wqk16"], PK(3))
                    CP("act", qmT[:, tok0:tok0 + n], bank(2, n), PK(2), ["qmT"])
                    ACT(kmT[:, tok0:tok0 + n], bank(3, n), AF.Copy, PK(3), ["kmT"], scale=128.0 ** -0.5)
                for tt in range(NT):
                    pb = 4 + (tt % 2)
                    MM(bank(pb, 128), xc16[:, tt * 128:(tt + 1) * 128], wqk16[:, 1, h, :], True, True, ["xc16c", "wqk16"], PK(pb))
                    for kc in range(8):
                        MM(bank(pb, 128, 128), hT[:, kc, tt * 128:(tt + 1) * 128], ws[1][:, kc, :], kc == 0, kc == 7, ["hT", wkeys[1]], PK(pb))
                    ACT(kw[:, 0, tt, :], bank(pb, 128), AF.Copy, PK(pb) + ["wkcol"], ["kw"], scale=wkcol[:, 0, tt, h:h + 1])
                    ACT(kw[:, 1, tt, :], bank(pb, 128), AF.Copy, PK(pb) + ["wkcol"], ["kw"], scale=wkcol[:, 1, tt, h:h + 1])
                    CP("dve", vm[:, tt, 0:128], bank(pb, 128, 128), PK(pb), ["vm"])
                for d in range(2):
                    order = list(range(NT)) if d == 0 else [1, 0] + list(range(NT - 1, 1, -1))
                    msk = mask_f if d == 0 else mask_b
                    MS("dve", P32, 0.0, ["P32"])
                    for ci, tt in enumerate(order):
                        ccol = cbc[:, d, tt, h:h + 1]
                        last = ci == NT - 1
                        lat = tt >= 2
                        lt_ = tt - 2
                        if lat:
                            TS("pool", C16, P32, ccol, None, ALU.mult, None, ["P32", "cbc"], ["C16"])
                            MM(bank(6, 128), kmT[:, tt * 128:(tt + 1) * 128], qmT[:, tt * 128:(tt + 1) * 128], True, True, ["kmT", "qmT"], PK(6))
                            STT("dve", sT16, bank(6, 128), wcol[:, d, tt, h:h + 1], msk, ALU.mult, ALU.mult, PK(6) + ["wcol", "cst"], ["sT16"])
                            MM(bank(7, 130), qmT[:, tt * 128:(tt + 1) * 128], C16, True, False, ["qmT", "C16"], PK(7))
                            MM(bank(7, 130), sT16, vm[:, tt, :], False, True, ["sT16", "vm"], PK(7))
                            ACT(smallc[:, 8:9], bank(7, 1, 128), AF.Abs, PK(7), ["dnC"])
                            TTo("dve", smallc[:, 8:9], smallc[:, 8:9], fcol[:, d, tt, h:h + 1], ALU.max, ["dnC", "fcol"], ["dnC"])
                            S.op("dve", lambda e: e.reciprocal(out=smallc[:, 9:10], in_=smallc[:, 8:9]), ["dnC"], ["rdC"])
                            if d == 0:
                                ACT(hf[:, lt_, :], bank(7, 128), AF.Copy, PK(7) + ["rdC"], ["hf"], scale=smallc[:, 9:10])
                            else:
                                STT("dve", hs, bank(7, 128), smallc[:, 9:10], hf[:, lt_, :], ALU.mult, ALU.add, PK(7) + ["rdC", "hf"], ["hs"])
                        if not last:
                            pb = 4 + (ci % 2)
                            MM(bank(pb, 130), kw[:, d, tt, :], vm[:, tt, :], True, True, ["kw", "vm"], PK(pb))
                            STT("dve", P32, P32, ccol, bank(pb, 130), ALU.mult, ALU.add, ["P32", "cbc"] + PK(pb), ["P32"])
                        if lat and d == 1:
                            if lt_ % 4 == 3:
                                g0 = CT + (lt_ - 3) * 128
                                for kc in range(8):
                                    MM(bank(2), ws[3][:, kc, :], hT[:, kc, g0:g0 + 512], kc == 0, kc == 7, ["hT", wkeys[3]], PK(2))
                                sigmoid_act(sigo, bank(2), sigo, PK(2), ["sigo"], "sigo")
                                for kc in range(8):
                                    MM(bank(3), ws[2][:, kc, :], hT[:, kc, g0:g0 + 512], kc == 0, kc == 7, ["hT", wkeys[2]], PK(3))
                                sigmoid_act(gtmp, bank(3), gtmp, PK(3), ["gtmp"], "gtmp")
                                TTo("dve", silz, bank(3), gtmp, ALU.mult, PK(3) + ["gtmp"], ["silz"])
                            gofs = (lt_ % 4) * 128
                            ACT(hn, hs, AF.Square, ["hs"], ["hn", "ssC"], accum=smallc[:, 10:11])
                            rstd_act(smallc[:, 11:12], smallc[:, 10:11], 128, ["ssC"], ["rsC"])
                            TS("dve", hn, hs, smallc[:, 11:12], None, ALU.mult, None, ["hs", "rsC"], ["hn"])
                            TR(bank(1, 128), hn, ident, ["hn", "cst"], PK(1))
                            STT("dve", yt1, bank(1, 128), pcol[:, 24 + h:25 + h], sigo[:, gofs:gofs + 128], ALU.mult, ALU.mult,
                                PK(1) + ["pcol", "sigo"], ["yt1"])
                            STT("dve", yt2, xc[:, tt * 128:(tt + 1) * 128], pcol[:, 28 + h:29 + h], yt1, ALU.mult, ALU.add,
                                ["xc", "pcol", "yt1"], ["yt2"])
                            TTo("pool", yT[:, 1, h, lt_ * 128:(lt_ + 1) * 128], yt2, silz[:, gofs:gofs + 128], ALU.mult, ["yt2", "silz"], ["yT"])
            if debug and b == 0:
                out_ops.append(DMA("pool", dbg["ybT"], yT[:, 1].rearrange("p h t -> p (h t)"), "dbgC", ["yT"], []))

            PD = Carve()
            wga = PD.take("wga", 8 * 1024, BF16).rearrange("p (k c) -> p k c", c=1024)
            wgb = PD.take("wgb", 8 * 1024, BF16).rearrange("p (k c) -> p k c", c=1024)
            woa = PD.take("woa", 4 * 1024, BF16).rearrange("p (k c) -> p k c", c=1024)
            wob = PD.take("wob", 4 * 1024, BF16).rearrange("p (k c) -> p k c", c=1024)
            wo = PD.take("wo", 8 * 1024, BF16).rearrange("p (k c) -> p k c", c=1024)
            mT = PD.take("mT", 8 * 512, BF16).rearrange("p (k t) -> p k t", t=512)
            sga = PD.take("sga", 512, F32)
            sgb = PD.take("sgb", 512, F32)
            ma = PD.take("ma", 512, F32)
            mb_ = PD.take("mb", 512, F32)
            xo = [PD.take("xo%d" % i, D, F32) for i in range(2)]
            yg = PD.take("yg", D, F32)
            gtb = PD.take("gtb", D, F32)
            fence(PD.keys)
            for hf_ in range(2):
                MM(bank(4 + hf_), cst[0:3, 1024 + b * 128:1024 + (b + 1) * 128], gtrow[:, hf_ * 512:(hf_ + 1) * 512],
                   True, True, ["cst", "gtrow"], PK(4 + hf_))
                CP("dve", gtb[:, hf_ * 512:(hf_ + 1) * 512], bank(4 + hf_), PK(4 + hf_), ["gtb"])
            DMA("pool", wga, win_d[:, C_GA:C_GA + 1024].rearrange("(kc p) c -> p kc c", p=128), "wd0", [], ["wga"])
            DMA("pool", wgb, win_d[:, C_GB:C_GB + 1024].rearrange("(kc p) c -> p kc c", p=128), "wd1", [], ["wgb"])
            DMA("pool", woa, woa_d.rearrange("(kc p) c -> p kc c", p=128), "wd2", [], ["woa"])
            DMA("pool", wob, wob_d.rearrange("(kc p) c -> p kc c", p=128), "wd3", [], ["wob"])
            DMA("pool", wo, wo_d.rearrange("(kc p) c -> p kc c", p=128), "wd4", [], ["wo"])
            oi = 0
            yc = 0
            for g in range(4):
                t0 = 512 * g
                for j in range(8):
                    cs_ = slice(j * 128, (j + 1) * 128)
                    bs = 4 * (j % 2)
                    for kc in range(4):
                        MM(bank(bs), woa[:, kc, cs_], yT[:, 0, kc, t0:t0 + 512], kc == 0, kc == 3, ["woa", "yT"], PK(bs))
                    for kc in range(4):
                        MM(bank(bs + 1), wob[:, kc, cs_], yT[:, 1, kc, t0:t0 + 512], kc == 0, kc == 3, ["wob", "yT"], PK(bs + 1))
                    for kc in range(8):
                        MM(bank(bs + 2), wga[:, kc, cs_], hT[:, kc, CT + t0:CT + t0 + 512], kc == 0, kc == 7, ["wga", "hT"], PK(bs + 2))
                    for kc in range(8):
                        MM(bank(bs + 3), wgb[:, kc, cs_], hT[:, kc, CT + t0:CT + t0 + 512], kc == 0, kc == 7, ["wgb", "hT"], PK(bs + 3))
                    sigmoid_act(sga, bank(bs + 2), sga, PK(bs + 2), ["sga"], "sga")
                    sigmoid_act(sgb, bank(bs + 3), sgb, PK(bs + 3), ["sgb"], "sgb")
                    TTo("dve", ma, bank(bs), sga, ALU.mult, PK(bs) + ["sga"], ["ma"])
                    TTo("dve", mb_, bank(bs + 1), sgb, ALU.mult, PK(bs + 1) + ["sgb"], ["mb"])
                    TTo("pool", mT[:, j, :], ma, mb_, ALU.add, ["ma", "mb"], ["mT"])
                for qt in range(4):
                    tok = t0 + qt * 128
                    sl = oi % 2
                    oi += 1
                    xk = "xo%d" % sl
                    DMA("sp", xo[sl], x_d[b, tok:tok + 128, :], "xin%d" % sl, [], [xk])
                    for hf_ in range(2):
                        pb = yc % 8
                        yc += 1
                        for j in range(8):
                            MM(bank(pb), mT[:, j, qt * 128:(qt + 1) * 128], wo[:, j, hf_ * 512:(hf_ + 1) * 512], j == 0, j == 7, ["mT", "wo"], PK(pb))
                        TTo("dve", yg[:, hf_ * 512:(hf_ + 1) * 512], bank(pb), gtb[:, hf_ * 512:(hf_ + 1) * 512], ALU.mult,
                            PK(pb) + ["gtb"], ["yg"])
                    TTo("pool", xo[sl], xo[sl], yg, ALU.add, [xk, "yg"], [xk])
                    out_ops.append(DMA("sp", out_d[b, tok:tok + 128, :], xo[sl], "xout%d" % sl, [xk], []))

        last = {}
        for o in out_ops:
            last[o.dma] = o
        S.emit({"sp": [o for o in last.values()]})
    return nc


def _consts():
    c = np.zeros((128, 1408), np.float32)
    i = np.arange(128)
    c[:, 0:128] = np.eye(128)
    c[:, 128:256] = (i[:, None] <= i[None, :])
    c[:, 256:384] = (i[:, None] >= i[None, :])
    c[:, 384:512] = (i[:, None] > i[None, :])
    c[:, 512:640] = (i[:, None] < i[None, :])
    c[:, 640:768] = 1.0
    c[:, 768:896] = (i[:, None] // 64 == i[None, :] // 64) / 64.0
    p = np.where((i % 32) < 16, i + 16, i - 16)
    perm = np.zeros((128, 128), np.float32)
    perm[p, i] = 1.0
    c[:, 896:1024] = perm
    for b in range(3):
        c[b, 1024 + b * 128:1024 + (b + 1) * 128] = 1.0
    t = np.arange(T)
    row = (t // 64).astype(np.float32)
    col = (t % 64).astype(np.float32)
    inv = (10000.0 ** (-np.arange(16, dtype=np.float32) / 16)).astype(np.float32)
    rope = np.zeros((128, 2, T), np.float32)
    for m in range(128):
        d = m % 64
        pos = row if d < 32 else col
        ang = (pos * inv[d % 16]).astype(np.float32)
        rope[m, 0] = np.cos(ang)
        sn = np.sin(ang)
        rope[m, 1] = -sn if (d % 32) < 16 else sn
    return c, rope


_NC_CACHE = {}


def kernel(x, c, ctx, c_ctx, norm_w, w_mod, b_mod, w_in, b_if, da_q_norm, da_k_norm,
           da_lambda_q1, da_lambda_k1, da_lambda_q2, da_lambda_k2, da_head_norm, w_out_a,
           ml_conv_w, ml_conv_b, ml_wq, ml_wk, ml_head_norm, ml_skip, w_out_b, w_o, _debug=False):
    f = lambda a: np.ascontiguousarray(np.asarray(a, dtype=np.float32))
    x, c, ctx, c_ctx = f(x), f(c), f(ctx), f(c_ctx)
    params = np.zeros((64, 128), np.float32)
    params[0:8] = f(norm_w).reshape(8, 128)
    params[8:20] = f(ml_conv_w).reshape(12, 128)
    params[20:24] = f(ml_conv_b).reshape(4, 128)
    params[24:28] = f(ml_head_norm).reshape(4, 128)
    params[28:32] = f(ml_skip).reshape(4, 128)
    params[32] = f(da_head_norm).reshape(128)
    params[33] = np.concatenate([f(da_q_norm).reshape(64)] * 2)
    params[34] = np.concatenate([f(da_k_norm).reshape(64)] * 2)
    lamv = np.concatenate([f(da_lambda_q1).reshape(64), f(da_lambda_k1).reshape(64),
                           f(da_lambda_q2).reshape(64), f(da_lambda_k2).reshape(64)])[None, :]
    consts, rope = _consts()
    shared = {
        "w_mod": f(w_mod).reshape(D, 3 * D), "w_in": f(w_in).reshape(D, INW), "params": params,
        "b_mod": f(b_mod).reshape(1, 3 * D), "lamv": np.ascontiguousarray(lamv), "b_if": f(b_if).reshape(1, 16),
        "ml_wq": f(ml_wq).reshape(4, 128, 128), "ml_wk": f(ml_wk).reshape(4, 128, 128),
        "w_out_a": f(w_out_a).reshape(512, D), "w_out_b": f(w_out_b).reshape(512, D), "w_o": f(w_o).reshape(D, D),
        "consts": consts, "rope": rope,
    }
    n = 8
    in_maps = []
    for i in range(n):
        m = dict(shared)
        m["x"] = np.ascontiguousarray(x[NB * i:NB * (i + 1)])
        m["ctx"] = np.ascontiguousarray(ctx[NB * i:NB * (i + 1)])
        m["cvec"] = np.ascontiguousarray(np.stack([c[NB * i], c[NB * i + 1], c_ctx]))
        in_maps.append(m)
    key = bool(_debug)
    if key not in _NC_CACHE:
        _NC_CACHE[key] = build_nc(debug=key)
    nc = _NC_CACHE[key]
    res = run_bass_kernel_spmd(nc, in_maps, core_ids=list(range(n)))
    out = np.concatenate([r["out"] for r in res.results], axis=0)
    if _debug:
        return out, res.results
    return out
```
